# Optimizing a Trainium2 kernel written in Bass

```python
import jax, jax.numpy as jnp
from jax import lax
import numpy as np

D_MODEL = 1024
BATCH = 8
SEQ = 2048
DEPTH = 4
DEC_BATCH = 128
DEC_SEQ = 4
PAST_LEN = 16384
PAGE_SIZE = 128

D_CONV = D_MODEL // 2
CONV_W = 31
HEAD_SIZE = 64
D_RWKV = D_MODEL
N_HEADS_RWKV = D_RWKV // HEAD_SIZE
LORA_DECAY = 64
LORA_AAA = 64
LORA_GATE = 160
D_FF = 3 * D_MODEL
FFN_CONV_W = 3
N_BRANCH = 2
D_RW_IN = 3 * D_RWKV + LORA_DECAY + LORA_AAA + LORA_GATE
D_IN = 2 * D_CONV + D_RW_IN + N_BRANCH * D_MODEL
RMS_EPS = 1e-6
LN_EPS = 1e-5
GN_EPS = 64e-5

kernel_name = "hybrid_conformer_rwkv7_convffn_step"


def rmsnorm(x, g):
    xf = x.astype(jnp.float32)
    y = xf * lax.rsqrt(jnp.mean(xf * xf, axis=-1, keepdims=True) + RMS_EPS)
    return (y * g.astype(jnp.float32)).astype(x.dtype)


def layernorm(x, g, b):
    xf = x.astype(jnp.float32)
    mu = jnp.mean(xf, axis=-1, keepdims=True)
    var = jnp.mean(jnp.square(xf - mu), axis=-1, keepdims=True)
    return (xf - mu) * lax.rsqrt(var + LN_EPS) * g.astype(jnp.float32) + b.astype(jnp.float32)


def causal_dwconv(buf, u, w, b):
    width = w.shape[0]
    full = jnp.concatenate([buf.astype(u.dtype), u], axis=1)
    y = lax.conv_general_dilated(full, w.astype(u.dtype)[:, None, :], window_strides=(1,),
                                 padding='VALID', dimension_numbers=('NWC', 'WIO', 'NWC'),
                                 feature_group_count=u.shape[-1])
    return y + b.astype(u.dtype), full[:, full.shape[1] - (width - 1):]


def wkv7_scan(S0, r, w, k, a, b, v):
    def step(S, inp):
        r_t, w_t, k_t, a_t, b_t, v_t = inp
        sa = jnp.einsum('bhij,bhj->bhi', S, a_t)
        S = S * w_t[:, :, None, :] + sa[..., :, None] * b_t[..., None, :] + v_t[..., :, None] * k_t[..., None, :]
        y = jnp.einsum('bhij,bhj->bhi', S, r_t)
        return S, y
    xs = tuple(jnp.moveaxis(t, 1, 0) for t in (r, w, k, a, b, v))
    S, y = lax.scan(step, S0, xs)
    return jnp.moveaxis(y, 0, 1), S


def layer(x, conv_buf, shift_buf, wkv_state, ffn_buf, p):
    (norm_mix_g, w_in, conv_dw_w, conv_dw_b, conv_ln_g, conv_ln_b, w_conv_out,
     rw_mu, rw_w0, rw_w2, rw_a0, rw_a2, rw_g2, rw_k_k, rw_k_a, rw_r_k, rw_ln_g, rw_ln_b, w_rw_out,
     w_mix_out, norm_ffn_g, w_up, ffn_dw_w, ffn_dw_b, w_down) = p
    f32 = jnp.float32
    B, T, _ = x.shape
    H, N = N_HEADS_RWKV, HEAD_SIZE

    h = rmsnorm(x, norm_mix_g)
    z = h @ w_in
    zc = z[..., :2 * D_CONV]
    zr = z[..., 2 * D_CONV:2 * D_CONV + D_RW_IN]
    zg = z[..., 2 * D_CONV + D_RW_IN:]

    u = zc[..., :D_CONV] * jax.nn.sigmoid(zc[..., D_CONV:])
    c, new_conv = causal_dwconv(conv_buf, u, conv_dw_w, conv_dw_b)
    c = jax.nn.silu(layernorm(c, conv_ln_g, conv_ln_b))
    ya = c.astype(x.dtype) @ w_conv_out

    zr32 = zr.astype(f32)
    prev = jnp.concatenate([shift_buf.astype(f32)[:, None, :], zr32[:, :-1]], axis=1)
    new_shift = zr[:, -1]
    xs = zr32 + (prev - zr32) * rw_mu.astype(f32)
    o = 0
    r = xs[..., o:o + D_RWKV]; o += D_RWKV
    k = xs[..., o:o + D_RWKV]; o += D_RWKV
    v = xs[..., o:o + D_RWKV]; o += D_RWKV
    wl = xs[..., o:o + LORA_DECAY]; o += LORA_DECAY
    al = xs[..., o:o + LORA_AAA]; o += LORA_AAA
    gl = xs[..., o:o + LORA_GATE]
    wlog = -jax.nn.softplus(-(rw_w0.astype(f32) + jnp.tanh(wl) @ rw_w2.astype(f32))) - 0.5
    decay = jnp.exp(-jnp.exp(wlog))
    alr = jax.nn.sigmoid(rw_a0.astype(f32) + al @ rw_a2.astype(f32))
    gout = jax.nn.sigmoid(gl) @ rw_g2.astype(f32)
    kk = (k * rw_k_k.astype(f32)).reshape(B, T, H, N)
    kk = kk / jnp.maximum(jnp.sqrt(jnp.sum(kk * kk, axis=-1, keepdims=True)), 1e-12)
    k = k * (1.0 + (alr - 1.0) * rw_k_a.astype(f32))
    rh = r.reshape(B, T, H, N); kh = k.reshape(B, T, H, N); vh = v.reshape(B, T, H, N)
    ah = alr.reshape(B, T, H, N)
    yh, new_wkv = wkv7_scan(wkv_state.astype(f32), rh, decay.reshape(B, T, H, N), kh, -kk, kk * ah, vh)
    mu = jnp.mean(yh, axis=-1, keepdims=True)
    var = jnp.mean(jnp.square(yh - mu), axis=-1, keepdims=True)
    yh = (yh - mu) * lax.rsqrt(var + GN_EPS)
    y = yh.reshape(B, T, D_RWKV) * rw_ln_g.astype(f32) + rw_ln_b.astype(f32)
    bonus = jnp.sum(rh * kh * rw_r_k.astype(f32), axis=-1, keepdims=True) * vh
    y = (y + bonus.reshape(B, T, D_RWKV)) * gout
    yb = y.astype(x.dtype) @ w_rw_out

    gates = jax.nn.sigmoid(zg.astype(f32))
    m = gates[..., :D_MODEL] * ya.astype(f32) + gates[..., D_MODEL:] * yb.astype(f32)
    x = x + (m.astype(x.dtype) @ w_mix_out)

    h2 = rmsnorm(x, norm_ffn_g)
    up = h2 @ w_up
    cu, new_ffn = causal_dwconv(ffn_buf, up, ffn_dw_w, ffn_dw_b)
    f = jax.nn.gelu(cu[..., :D_FF], approximate=True) * cu[..., D_FF:]
    x = x + f @ w_down
    return x, new_conv, new_shift, new_wkv, new_ffn


def trunk(x, conv0, shift0, wkv0, ffn0, params, norm_final_g):
    convs, shifts, wkvs, ffns = [], [], [], []
    for l in range(DEPTH):
        p = tuple(t[l] for t in params)
        x, c, s, w, f = layer(x, conv0[l], shift0[l], wkv0[l], ffn0[l], p)
        convs.append(c); shifts.append(s); wkvs.append(w); ffns.append(f)
    y = rmsnorm(x, norm_final_g)
    return y, jnp.stack(convs), jnp.stack(shifts), jnp.stack(wkvs), jnp.stack(ffns)


def setup_inputs(seed: int = 0) -> dict:
    key = jax.random.key(seed)
    ks = iter(jax.random.split(key, 40))

    def nrm(shape, scale):
        return jax.random.normal(next(ks), shape, jnp.float32) * scale

    def unif(shape, lo, hi):
        return jax.random.uniform(next(ks), shape, jnp.float32, lo, hi)

    L, D = DEPTH, D_MODEL
    H, N = N_HEADS_RWKV, HEAD_SIZE
    return {
        "x_prompt": nrm((BATCH, SEQ, D), 1.0),
        "x_sample": nrm((DEC_BATCH, DEC_SEQ, D), 1.0),
        "state_conv": nrm((L, DEC_BATCH, CONV_W - 1, D_CONV), 0.5),
        "state_shift": nrm((L, DEC_BATCH, D_RW_IN), 1.0),
        "state_wkv": nrm((L, DEC_BATCH, H, N, N), 0.3),
        "state_ffn": nrm((L, DEC_BATCH, FFN_CONV_W - 1, 2 * D_FF), 1.0),
        "norm_mix_g": 1.0 + nrm((L, D), 0.02),
        "w_in": nrm((L, D, D_IN), D ** -0.5),
        "conv_dw_w": nrm((L, CONV_W, D_CONV), CONV_W ** -0.5),
        "conv_dw_b": nrm((L, D_CONV), 0.02),
        "conv_ln_g": 1.0 + nrm((L, D_CONV), 0.02),
        "conv_ln_b": nrm((L, D_CONV), 0.02),
        "w_conv_out": nrm((L, D_CONV, D), D_CONV ** -0.5),
        "rw_mu": unif((L, D_RW_IN), 0.0, 1.0),
        "rw_w0": unif((L, D_RWKV), -5.0, 0.5),
        "rw_w2": nrm((L, LORA_DECAY, D_RWKV), 0.5 * LORA_DECAY ** -0.5),
        "rw_a0": nrm((L, D_RWKV), 0.1),
        "rw_a2": nrm((L, LORA_AAA, D_RWKV), 0.5 * LORA_AAA ** -0.5),
        "rw_g2": nrm((L, LORA_GATE, D_RWKV), LORA_GATE ** -0.5),
        "rw_k_k": 0.85 + nrm((L, D_RWKV), 0.05),
        "rw_k_a": 1.0 + nrm((L, D_RWKV), 0.05),
        "rw_r_k": nrm((L, H, N), 0.1),
        "rw_ln_g": 1.0 + nrm((L, D_RWKV), 0.02),
        "rw_ln_b": nrm((L, D_RWKV), 0.02),
        "w_rw_out": nrm((L, D_RWKV, D), D_RWKV ** -0.5),
        "w_mix_out": nrm((L, D, D), D ** -0.5),
        "norm_ffn_g": 1.0 + nrm((L, D), 0.02),
        "w_up": nrm((L, D, 2 * D_FF), D ** -0.5),
        "ffn_dw_w": nrm((L, FFN_CONV_W, 2 * D_FF), FFN_CONV_W ** -0.5),
        "ffn_dw_b": nrm((L, 2 * D_FF), 0.02),
        "w_down": nrm((L, D_FF, D), D_FF ** -0.5),
        "norm_final_g": 1.0 + nrm((D,), 0.02),
    }


def reference(x_prompt, x_sample, state_conv, state_shift, state_wkv, state_ffn,
              norm_mix_g, w_in, conv_dw_w, conv_dw_b, conv_ln_g, conv_ln_b, w_conv_out,
              rw_mu, rw_w0, rw_w2, rw_a0, rw_a2, rw_g2, rw_k_k, rw_k_a, rw_r_k, rw_ln_g, rw_ln_b, w_rw_out,
              w_mix_out, norm_ffn_g, w_up, ffn_dw_w, ffn_dw_b, w_down, norm_final_g):
    params = (norm_mix_g, w_in, conv_dw_w, conv_dw_b, conv_ln_g, conv_ln_b, w_conv_out,
              rw_mu, rw_w0, rw_w2, rw_a0, rw_a2, rw_g2, rw_k_k, rw_k_a, rw_r_k, rw_ln_g, rw_ln_b, w_rw_out,
              w_mix_out, norm_ffn_g, w_up, ffn_dw_w, ffn_dw_b, w_down)
    B = x_prompt.shape[0]
    dt = x_prompt.dtype
    conv0 = jnp.zeros((DEPTH, B, CONV_W - 1, D_CONV), dt)
    shift0 = jnp.zeros((DEPTH, B, D_RW_IN), dt)
    wkv0 = jnp.zeros((DEPTH, B, N_HEADS_RWKV, HEAD_SIZE, HEAD_SIZE), jnp.float32)
    ffn0 = jnp.zeros((DEPTH, B, FFN_CONV_W - 1, 2 * D_FF), dt)
    y_prompt, p_conv, p_shift, p_wkv, p_ffn = trunk(x_prompt, conv0, shift0, wkv0, ffn0, params, norm_final_g)
    y_sample, s_conv, s_shift, s_wkv, s_ffn = trunk(x_sample, state_conv, state_shift, state_wkv, state_ffn,
                                                    params, norm_final_g)
    return (y_prompt, y_sample, p_conv, p_shift, p_wkv, p_ffn, s_conv, s_shift, s_wkv, s_ffn)
```

```python
import numpy as np
import concourse.bass as bass
import concourse.mybir as mybir

F32 = mybir.dt.float32
BF16 = mybir.dt.bfloat16
AF = mybir.ActivationFunctionType
ALU = mybir.AluOpType


ATTACH_WAIT = 1
SAME_ENG_MASK = 7


class V:
    __slots__ = ("tile", "ap", "sub")

    def __init__(self, tile, ap, sub=None):
        self.tile = tile
        self.ap = ap
        self.sub = sub

    def __getitem__(self, idx):
        return V(self.tile, self.ap[idx], self.sub)

    def k(self, sub):
        return V(self.tile, self.ap, sub)

    def r(self, pat, **kw):
        return V(self.tile, self.ap.rearrange(pat, **kw), self.sub)

    def bc(self, shape):
        return V(self.tile, self.ap.broadcast_to(shape), self.sub)

    def un(self, axis):
        return V(self.tile, self.ap.unsqueeze(axis), self.sub)

    def bitcast(self, dt):
        return V(self.tile, self.ap.bitcast(dt), self.sub)


class Tile:
    def __init__(self, name, handle):
        self.name = name
        self.h = handle

    def __getitem__(self, idx):
        return V(self.name, self.h[idx], None)

    def v(self, idx, sub):
        return V(self.name, self.h[idx], sub)


class Op:
    __slots__ = ("id", "eng", "fn", "dma", "deps", "sem", "semval", "signal", "rank", "prewait")

    def __init__(self, id, eng, fn, dma):
        self.id = id
        self.eng = eng
        self.fn = fn
        self.dma = dma
        self.deps = {}
        self.sem = None
        self.semval = 0
        self.signal = False
        self.rank = 0
        self.prewait = None


class Prog:
    ENGS = ("pe", "act", "dve", "pool", "sp")

    def __init__(self, nc, n_dma_sems=40):
        self.nc = nc
        self.ops = []
        self.state = {}
        self.n_dma_sems = n_dma_sems
        self.sb_bytes = 0

    def sb(self, name, shape, dtype):
        h = self.nc.alloc_sbuf_tensor("sb_" + name, list(shape), dtype)
        n = 1
        for s in shape[1:]:
            n *= s
        self.sb_bytes += n * (4 if dtype == F32 else 2)
        return Tile(name, h)

    def ps(self, name, shape, dtype=F32):
        h = self.nc.alloc_psum_tensor("ps_" + name, list(shape), dtype)
        return Tile(name, h)

    def _collect(self, v, is_write, deps):
        if v is None or not isinstance(v, V) or v.tile is None:
            return
        st = self.state.setdefault(v.tile, {})
        subs = list(st.keys()) if v.sub is None else [s for s in (v.sub, None) if s in st]
        for s in subs:
            w, rs = st[s]
            if w is not None:
                deps[w] = deps.get(w, 0) | (2 if is_write else 1)
            if is_write:
                for r in rs:
                    deps[r] = deps.get(r, 0) | 4

    def _update(self, v, is_write, opid):
        if v is None or not isinstance(v, V) or v.tile is None:
            return
        st = self.state.setdefault(v.tile, {})
        if is_write:
            if v.sub is None:
                st.clear()
            st[v.sub] = [opid, []]
        else:
            if v.sub not in st:
                st[v.sub] = [None, []]
            st[v.sub][1].append(opid)

    def add(self, eng, fn, reads, writes, dma=False):
        op = Op(len(self.ops), eng, fn, dma)
        deps = {}
        for v in reads:
            self._collect(v, False, deps)
        for v in writes:
            self._collect(v, True, deps)
        for v in reads:
            self._update(v, False, op.id)
        for v in writes:
            self._update(v, True, op.id)
        deps.pop(op.id, None)
        op.deps = deps
        self.ops.append(op)
        return op

    @staticmethod
    def _a(x):
        return x.ap if isinstance(x, V) else x

    def mm(self, out, lhsT, rhs, start=True, stop=True):
        a = self._a
        return self.add("pe", lambda e: e.matmul(a(out), a(lhsT), a(rhs), start=start, stop=stop),
                        [lhsT, rhs], [out])

    def tr(self, out, in_, ident):
        a = self._a
        return self.add("pe", lambda e: e.transpose(a(out), a(in_), a(ident)), [in_, ident], [out])

    def act(self, out, in_, func, scale=1.0, bias=0.0, eng="act"):
        a = self._a
        return self.add(eng, lambda e: e.activation(a(out), a(in_), func, bias=a(bias), scale=a(scale)),
                        [in_, scale, bias], [out])

    def tt(self, out, in0, in1, op, eng="dve"):
        a = self._a
        return self.add(eng, lambda e: e.tensor_tensor(a(out), a(in0), a(in1), op), [in0, in1], [out])

    def ts(self, out, in0, s1, op0, s2=None, op1=None, eng="dve"):
        a = self._a
        if op1 is None:
            return self.add(eng, lambda e: e.tensor_scalar(a(out), a(in0), a(s1), None, op0), [in0, s1], [out])
        return self.add(eng, lambda e: e.tensor_scalar(a(out), a(in0), a(s1), a(s2), op0, op1),
                        [in0, s1, s2], [out])

    def stt(self, out, in0, scalar, in1, op0, op1):
        a = self._a
        return self.add("dve", lambda e: e.scalar_tensor_tensor(a(out), a(in0), a(scalar), a(in1), op0, op1),
                        [in0, scalar, in1], [out])

    def scan(self, out, d0, d1, init, op0, op1):
        a = self._a
        return self.add("dve", lambda e: e.tensor_tensor_scan(a(out), a(d0), a(d1), a(init), op0, op1),
                        [d0, d1, init], [out])

    def copy(self, out, in_, eng="dve"):
        a = self._a
        if eng == "act":
            return self.add("act", lambda e: e.copy(a(out), a(in_)), [in_], [out])
        return self.add(eng, lambda e: e.tensor_copy(a(out), a(in_)), [in_], [out])

    def recip(self, out, in_):
        a = self._a
        return self.add("dve", lambda e: e.reciprocal(a(out), a(in_)), [in_], [out])

    def memset(self, out, val, eng="dve"):
        a = self._a
        return self.add(eng, lambda e: e.memset(a(out), val), [], [out])

    def dma(self, out, in_, eng="sp"):
        a = self._a
        return self.add(eng, lambda e: e.dma_start(out=a(out), in_=a(in_)), [in_], [out], dma=True)

    def finish(self, eng="sp"):
        op = Op(len(self.ops), eng, lambda e: None, False)
        op.deps = {o.id: 1 for o in self.ops if o.dma}
        self.ops.append(op)

    def emit(self):
        nc = self.nc
        ops = self.ops
        from contextlib import ExitStack
        with ExitStack() as es:
            eng_sem = {e: es.enter_context(nc.semaphore("s_" + e)) for e in self.ENGS}
            dma_sems = [es.enter_context(nc.semaphore("d%d" % i)) for i in range(self.n_dma_sems)]
            nsw = (self.n_dma_sems * 3) // 5
            pools = {True: list(range(0, nsw)), False: list(range(nsw, self.n_dma_sems))}
            dma_cnt = [0] * self.n_dma_sems
            dma_last = [None] * self.n_dma_sems
            kk_ = {True: 0, False: 0}
            for op in ops:
                if op.dma:
                    sw = op.eng == "pool"
                    pl = pools[sw]
                    i = pl[kk_[sw] % len(pl)]
                    kk_[sw] += 1
                    op.prewait = dma_last[i]
                    dma_cnt[i] += 16
                    op.sem = dma_sems[i]
                    op.semval = dma_cnt[i]
                    dma_last[i] = op.id
            for op in ops:
                for d, kind in op.deps.items():
                    dop = ops[d]
                    if dop.dma:
                        continue
                    if dop.eng == op.eng:
                        if op.eng == "pe":
                            continue
                        if not (kind & SAME_ENG_MASK):
                            continue
                    dop.signal = True
            rank = {e: 0 for e in self.ENGS}
            for op in ops:
                if not op.dma and op.signal:
                    rank[op.eng] += 1
                    op.rank = rank[op.eng]
            clocks = [None] * len(ops)
            eng_clock = {e: {} for e in self.ENGS}
            dma_known = {e: set() for e in self.ENGS}
            plans = {e: [] for e in self.ENGS}
            for op in ops:
                ck = eng_clock[op.eng]
                waits = []
                deplist = list(op.deps.items())
                if op.prewait is not None:
                    deplist.append((op.prewait, 3))
                for d, kind in sorted(deplist):
                    dop = ops[d]
                    if dop.dma:
                        if d in dma_known[op.eng]:
                            continue
                        waits.append((dop.sem, dop.semval))
                        dma_known[op.eng].add(d)
                    else:
                        if dop.eng == op.eng and (op.eng == "pe" or not (kind & SAME_ENG_MASK)):
                            continue
                        key = dop.eng
                        if ck.get(key, 0) >= dop.rank:
                            continue
                        waits.append((eng_sem[dop.eng], dop.rank))
                    for kk, vv in clocks[d].items():
                        if ck.get(kk, 0) < vv:
                            ck[kk] = vv
                best = {}
                for s, val in waits:
                    kid = id(s)
                    if kid not in best or best[kid][1] < val:
                        best[kid] = (s, val)
                myck = dict(ck)
                if (not op.dma) and op.signal:
                    myck[op.eng] = op.rank
                clocks[op.id] = myck
                plans[op.eng].append((op, list(best.values())))
            self.n_waits = sum(len(w) for e in plans for _, w in plans[e])
            handles = {"pe": "tensor", "act": "scalar", "dve": "vector", "pool": "gpsimd", "sp": "sync"}
            with nc.Block() as block:
                def run(eng_name):
                    def body(e):
                        for op, waits in plans[eng_name]:
                            attach = None
                            if ATTACH_WAIT and waits and not op.dma and op.eng != "sp":
                                attach = waits[-1]
                                waits = waits[:-1]
                            for s, val in waits:
                                e.wait_ge(s, val)
                            ins = op.fn(e)
                            if ins is None:
                                if attach is not None:
                                    e.wait_ge(*attach)
                                continue
                            if attach is not None:
                                ins._wait_ge(attach[0], attach[1])
                            if op.dma:
                                ins.then_inc(op.sem, 16)
                            elif op.signal:
                                ins.then_inc(eng_sem[eng_name], 1)
                    return body
                block.tensor(run("pe"))
                block.scalar(run("act"))
                block.vector(run("dve"))
                block.gpsimd(run("pool"))
                block.sync(run("sp"))


from concourse.bass_utils import run_bass_kernel_spmd

DEPTH = 4
NCORE = 8
HP_LAG = 0
D = 1024
SEQ = 2048
NTOK = 2112
RMS_EPS = 1e-6
LN_EPS = 1e-5
GN_EPS = 64e-5
H0 = 0.5 * float(np.exp(-0.5))

NMG, NFG, CDW, CDB, CLG, CLB, MU, W0, A0, KK, KA, RK, LNG, LNB, FDW, FDB, NPV = (
    0, 8, 16, 140, 144, 148, 152, 179, 187, 195, 203, 211, 219, 227, 235, 379, 427)
CLGH, CLBH, W0H, A0H, KAH, KAB, OMM, NDV = 0, 4, 8, 16, 24, 32, 40, 67
C_ID, C_IDB, NCST = 0, 128, 192
C_MP, C_MS, C_RMP, C_RMS, C_SEQM, C_SEQMT, NCSTB = 0, 320, 640, 1152, 1216, 2240, 2256


def make_consts():
    import ml_dtypes
    c0 = np.zeros((128, NCST), np.float32)
    c0[:, C_ID:C_ID + 128] = np.eye(128, dtype=np.float32)
    p = np.arange(128) % 64
    c0[:, C_IDB:C_IDB + 64] = (p[:, None] == np.arange(64)[None, :])
    c = np.zeros((128, NCSTB), np.float32)
    t = np.arange(64)[None, :]
    s = p[:, None]
    for base, same in ((C_MP, np.ones((128, 64), bool)), (C_MS, (s // 4) == (t // 4))):
        su = (s < t) & same
        iu = (s <= t) & same
        sl = (s > t) & same
        c[:, base:base + 320] = np.concatenate([su, iu, su, iu, sl], axis=1)
    c[:, C_RMP:C_RMP + 512] = (np.arange(512) % 64 != 0)[None, :]
    c[:, C_RMS:C_RMS + 64] = (np.arange(64) % 4 != 0)[None, :]
    sm = (np.arange(16)[:, None] == (np.arange(64) // 4)[None, :]).astype(np.float32)
    c[:, C_SEQM:C_SEQM + 1024] = sm.reshape(1, 1024)
    c[:, C_SEQMT:C_SEQMT + 16] = ((p // 4)[:, None] == np.arange(16)[None, :])
    return c0, c.astype(ml_dtypes.bfloat16)


class Geo:
    def __init__(self, kind, t0, ti, first=False, last=False):
        self.kind, self.t0, self.ti, self.first, self.last = kind, t0, ti, first, last
        if kind == "p":
            self.N, self.nseq, self.L, self.nb, self.Lb, self.nblk = 512, 1, 512, 8, 64, 8
        else:
            self.N, self.nseq, self.L, self.nb, self.Lb, self.nblk = 64, 16, 4, 16, 4, 1


def build(depth=DEPTH, tiles=None, dump=None, stop=None):
    sched = []
    _build(depth, tiles, None, stop, sched, True)
    return _build(depth, tiles, dump, stop, sched, False)


def _build(depth, tiles, dump, stop, sched, dry):
    nc = bass.Bass("TRN2", target_bir_lowering=False)
    P = Prog(nc)
    L_ = DEPTH

    def din(name, shape, dt=F32):
        return nc.dram_tensor(name, list(shape), dt, kind="ExternalInput").ap()

    def dout(name, shape):
        return nc.dram_tensor(name, list(shape), F32, kind="ExternalOutput").ap()

    xT_d = din("xT", [128, 8, NTOK])
    cst_d = din("cst", [128, NCST])
    cstb_d = din("cstb", [128, NCSTB], BF16)
    pv_d = din("pv", [L_, 128, NPV])
    pvf_d = din("pvf", [128, 8])
    stc_d = din("stc", [L_, 128, 4, 16, 30])
    sts_d = din("sts", [L_, 128, 27, 16])
    stw_d = din("stw", [L_, 128, 8, 16, 64])
    stf_d = din("stf", [L_, 128, 48, 16, 2])
    w_in_d = din("w_in_u", [L_, 51, 128, 1024])
    wco_d = din("w_co_u", [L_, 8, 128, 512])
    lw_d = din("lw", [L_, 128, 1024])
    g2_d = din("g2", [L_, 160, 1024])
    wro_d = din("w_ro_u", [L_, 8, 128, 1024])
    wmx_d = din("w_mx_u", [L_, 8, 128, 1024])
    wup_d = din("w_up_u", [L_, 48, 128, 1024])
    wdn_d = din("w_dn_u", [L_, 24, 128, 1024])

    o_y = dout("o_y", [128, 8, NTOK])
    o_pconv = dout("o_pconv", [L_, 128, 4, 30])
    o_sconv = dout("o_sconv", [L_, 128, 4, 16, 30])
    o_pshift = dout("o_pshift", [L_, 128, 27])
    o_sshift = dout("o_sshift", [L_, 128, 27, 16])
    o_pwkv = dout("o_pwkv", [L_, 128, 8, 64])
    o_swkv = dout("o_swkv", [L_, 128, 8, 16, 64])
    o_pffn = dout("o_pffn", [L_, 128, 48, 2])
    o_sffn = dout("o_sffn", [L_, 128, 48, 16, 2])

    if tiles is None:
        tiles = [Geo("p", 512 * i, i, first=(i == 0), last=(i == 3)) for i in range(4)] + [Geo("s", 2048, 4)]

    XT = P.sb("XT", [128, 8, 512], F32)
    H = P.sb("H", [128, 8, 512], BF16)
    NSLOT = 8
    WB = [P.sb("WB%d" % i, [128, 1024], BF16) for i in range(NSLOT)]
    UB = [P.sb("UB%d" % i, [128, 544], F32) for i in range(4)]
    CA = P.sb("CA", [128, 4, 512], BF16)
    BF = P.sb("BF", [128, 16, 512], BF16)
    LWT = P.sb("LWT", [128, 1024], BF16)
    G2A = P.sb("G2A", [128, 1024], BF16)
    G2B = P.sb("G2B", [128, 1024], BF16)
    TWL = P.sb("TWL", [128, 512], BF16)
    SGL = P.sb("SGL", [128, 512], BF16)
    SGL2 = P.sb("SGL2", [128, 512], BF16)
    T = [P.sb("T%d" % i, [128, 544], F32) for i in range(12)]
    TB = [P.sb("TBh%d" % i, [128, 512], BF16) for i in range(2)]

    class HPS:
        pass

    def mk_set(i, Tl, psb):
        S = HPS()
        S.T = Tl
        S.TB = [P.sb("hTB%d_%d" % (i, j), [128, 512], BF16) for j in range(2)] if i else TB
        S.AR = P.sb("AR%d" % i, [128, 8, 128], BF16)
        S.BK = P.sb("BK%d" % i, [128, 8, 128], BF16)
        S.TOK = P.sb("TOK%d" % i, [128, 8, 3, 64], BF16)
        S.AM = P.sb("AM%d" % i, [128, 8, 320], BF16)
        S.CH = [P.sb("CH%d_%d" % (i, j), [128, 8, 64], BF16) for j in range(4)]
        S.TTa = P.sb("TTa%d" % i, [128, 8, 64], BF16)
        S.TTb = P.sb("TTb%d" % i, [128, 8, 64], BF16)
        S.RHSb = P.sb("RHSb%d" % i, [128, 64], BF16)
        S.Ub = P.sb("Ub%d" % i, [128, 64], BF16)
        S.VBF = [P.sb("VBF%d_%d" % (i, j), [128, 512], BF16) for j in range(3)]
        S.PSB = psb
        return S

    PS = [P.ps("PS%d" % i, [128, 512], F32) for i in range(8)]
    T1 = {j: P.sb("hT1_%d" % j, [128, 544], F32) for j in (0, 2, 3, 4, 5, 6, 7, 8, 9, 10)}
    SET0 = mk_set(0, {j: T[j] for j in range(12)}, PS[0:4])
    SET1 = mk_set(1, T1, PS[4:8])
    SETS = mk_set
    STpb = P.sb("STpb", [128, 8, 64], BF16)
    S0 = P.sb("S0", [128, 16, 64], F32)
    S0b = P.sb("S0b", [128, 16, 64], BF16)
    STS = P.sb("STS", [128, 27, 16], F32)
    STF = P.sb("STF", [128, 48, 16, 2], F32)
    CHALO = P.sb("CHALO", [128, L_, 4, 30], F32)
    CSH = P.sb("CSH", [128, L_, 27, 1], F32)
    STp = P.sb("STp", [128, L_, 8, 64], F32)
    CFF = P.sb("CFF", [128, L_, 48, 2], F32)
    PVs = [P.sb("PV%d" % i, [128, NPV], F32) for i in range(1)]
    DV = P.sb("DV", [128, NDV], F32)
    PVF = P.sb("PVF", [128, 8], F32)
    CST = P.sb("CST", [128, NCST], F32)
    CSTB = P.sb("CSTB", [128, NCSTB], BF16)
    IDBb = P.sb("IDBb", [128, 64], BF16)
    IDENTB = P.sb("IDENTB", [128, 128], BF16)
    DIAG = [P.sb("DIAG%d" % i, [128, 128], BF16) for i in range(8)]
    ONES = P.sb("ONES", [128, 128], BF16)
    ONEB = P.sb("ONEB", [128, 128], BF16)
    ONEB64 = P.sb("ONEB64", [128, 128], BF16)

    mask_p = CSTB[:, C_MP:C_MP + 320]
    mask_s = CSTB[:, C_MS:C_MS + 320]
    rm_p = CSTB[:, C_RMP:C_RMP + 512]
    rm_s = CSTB[:, C_RMS:C_RMS + 64]
    seqm = CSTB[:, C_SEQM:C_SEQM + 1024].r("p (q t) -> p q t", q=16)
    seqmt = CSTB[:, C_SEQMT:C_SEQMT + 16]

    dumps = {}

    def dumpv(name, view, shape):
        if dump is None or name not in dump:
            return
        d = dout("dbg_" + name, shape)
        P.dma(d, view, eng="pool")
        dumps[name] = shape

    def wsrc(kind, l, a, b=None):
        if kind == "in":
            idx = a // 128 if a < 4352 else (34 if a == 4352 else 35 + (a - 4384) // 128)
            return w_in_d[l, idx].rearrange("p (k m) -> p k m", k=8), 8, 128
        if kind == "co":
            return wco_d[l, a // 128].rearrange("p (k m) -> p k m", k=4), 4, 128
        if kind == "ro":
            return wro_d[l, a // 128].rearrange("p (k m) -> p k m", k=8), 8, 128
        if kind == "mx":
            return wmx_d[l, a // 128].rearrange("p (k m) -> p k m", k=8), 8, 128
        if kind == "up":
            return wup_d[l, a // 128].rearrange("p (k m) -> p k m", k=8), 8, 128
        if kind == "dn":
            return wdn_d[l, a * 8 + b // 128].rearrange("p (k m) -> p k m", k=8), 8, 128
        raise ValueError(kind)

    class WS:
        issued = 0
        taken = 0

    def w_issue():
        i = WS.issued
        if i >= len(sched):
            return
        src, K, M = wsrc(*sched[i])
        dst = WB[i % NSLOT][:, 0:K * M].r("p (k m) -> p k m", k=K)
        P.dma(dst, src, eng="pool")
        WS.issued += 1

    def w_next(*unit):
        i = WS.taken
        if dry:
            sched.append(unit)
        else:
            assert sched[i] == unit, (i, sched[i], unit)
            while WS.issued < min(len(sched), i + NSLOT - 1):
                w_issue()
        _, K, M = wsrc(*unit)
        WS.taken += 1
        return WB[i % NSLOT][:, 0:K * M].r("p (k m) -> p k m", k=K)

    def v3(v, g):
        return v.r("p (s t) -> p s t", s=g.nseq)

    def hv(tile, M, g, h):
        return tile[:M, 0:g.nseq * (h + g.L)].r("p (s t) -> p s t", s=g.nseq)

    def proj(ps_view, w, M, N):
        for k in range(8):
            P.mm(ps_view, w[:, k, 0:M], H[:, k, :N], start=(k == 0), stop=(k == 7))

    def rmsnorm(xc, N):
        for c in range(8):
            sq = TB[c % 2][:, :N]
            P.act(sq, xc[c], AF.Square)
            P.mm(PS[2][:, :N], ONES[:], sq, start=(c == 0), stop=(c == 7))
        t = T[10][:, :N]
        P.act(t, PS[2][:, :N], AF.Ln, scale=1.0 / D, bias=RMS_EPS)
        P.act(t, t, AF.Exp, scale=-0.5)
        return t

    P.dma(CST[:], cst_d)
    P.dma(CSTB[:], cstb_d)
    P.dma(PVF[:], pvf_d)
    P.act(IDBb[:], CST[:, C_IDB:C_IDB + 64], AF.Identity)
    P.act(IDENTB[:], CST[:, C_ID:C_ID + 128], AF.Identity)
    P.memset(ONES[:], 1.0)
    P.memset(CSH[:], 0.0)
    P.memset(G2B[:], 0.0)
    P.memset(SGL2[:], 0.0)
    P.memset(ONEB[:], 0.0)
    P.memset(ONEB[0:64, 0:64], 1.0)
    P.memset(ONEB[64:128, 64:128], 1.0)
    P.act(ONEB64[:], ONEB[:], AF.Identity, scale=1.0 / 64)

    class Cur:
        PV = None
        npass = 0

    def layer_setup(l):
        PV = PVs[0]
        Cur.PV = PV
        Cur.npass += 1
        P.dma(PV[:], pv_d[l])
        P.dma(LWT[:], lw_d[l], eng="pool")
        P.dma(G2A[:], g2_d[l][0:128, :], eng="pool")
        P.dma(G2B[0:32, :], g2_d[l][128:160, :], eng="pool")
        P.ts(DV[:, CLGH:CLGH + 4], PV[:, CLG:CLG + 4], 0.5, ALU.mult)
        P.ts(DV[:, CLBH:CLBH + 4], PV[:, CLB:CLB + 4], 0.5, ALU.mult)
        P.ts(DV[:, W0H:W0H + 8], PV[:, W0:W0 + 8], 0.5, ALU.mult)
        P.ts(DV[:, A0H:A0H + 8], PV[:, A0:A0 + 8], 0.5, ALU.mult)
        P.ts(DV[:, KAH:KAH + 8], PV[:, KA:KA + 8], 0.5, ALU.mult)
        P.ts(DV[:, KAB:KAB + 8], PV[:, KA:KA + 8], -0.5, ALU.mult, 1.0, ALU.add)
        P.ts(DV[:, OMM:OMM + 27], PV[:, MU:MU + 27], -1.0, ALU.mult, 1.0, ALU.add)

    def shift_chunk(l, g, ps_view, q, M, out_xs, zs_tile, d_tile):
        N, L = g.N, g.L
        PV = Cur.PV
        z3 = hv(zs_tile, M, g, 1)
        if g.kind == "p":
            if g.first:
                P.memset(z3[:, :, 0:1], 0.0)
            else:
                P.copy(z3[:, :, 0:1], CSH[:M, l, q:q + 1, :], eng="act")
        else:
            P.copy(z3[:, :, 0:1], STS[:M, q, :].un(2), eng="act")
        P.act(z3[:, :, 1:1 + L], v3(ps_view, g), AF.Identity)
        d3 = v3(d_tile[:M, :N], g)
        P.act(d3, v3(ps_view, g), AF.Identity, scale=DV[:M, OMM + q:OMM + q + 1])
        P.stt(v3(out_xs, g), z3[:, :, 0:L], PV[:M, MU + q:MU + q + 1], d3, ALU.mult, ALU.add)
        if g.kind == "p":
            P.copy(CSH[:M, l, q:q + 1, :], z3[:, :, L:L + 1], eng="act")
        else:
            P.copy(STS[:M, q, :].un(2), z3[:, :, L:L + 1], eng="act")

    def wkv(l, g, hp, S, XV, BHF, KHF, WC):
        N, nblk, Q = g.N, g.nblk, g.nseq
        HS = [slice(0, 64), slice(64, 128)]
        prompt = g.kind == "p"
        nlev = 5 if prompt else 1
        MASK = mask_p if prompt else mask_s
        AR, BK, TOK, AM, CH, TTa, TTb, RHSb, Ub, PSB = S.AR, S.BK, S.TOK, S.AM, S.CH, S.TTa, S.TTb, S.RHSb, S.Ub, S.PSB
        Tt = S.T
        for b in range(nblk):
            cb = slice(b * 64, (b + 1) * 64)
            pa = PSB[b % 2]
            ptb = PSB[2 + (b % 2)][:, 0:96].bitcast(BF16)
            for qi, src in enumerate((XV, BHF, KHF)):
                for hs in HS:
                    P.tr(ptb[hs, qi * 64:(qi + 1) * 64], src[hs, cb], IDENTB[hs, hs])
            for hs in HS:
                P.mm(pa[hs, 0:128], BK[hs, b, 0:64], AR[hs, b, :])
            for hs in HS:
                P.mm(pa[hs, 128:256], BK[hs, b, 64:128], AR[hs, b, :])
            for hs in HS:
                P.mm(pa[hs, 256:320], AR[hs, b, 0:64], BK[hs, b, 0:64])
            P.copy(TOK[:, b, :, :], ptb.r("p (q i) -> p q i", q=3), eng="act")
            P.tt(AM[:, b, :], pa[:, 0:320], MASK, ALU.mult)
            if b % 2 == 1:
                yield
        yield
        if stop == "W1":
            return
        P.tt(TTa[:, 0:nblk, :], AM[:, 0:nblk, 0:64], IDBb[:].un(1).bc([128, nblk, 64]), ALU.add)
        Xp = AM[:, :, 0:64]
        Pp = AM[:, :, 256:320]
        TTp, TTn = TTa, TTb
        for k in range(1, nlev + 1):
            Pk = CH[(k % 2) * 2]
            Xk = CH[(k % 2) * 2 + 1]
            for b in range(nblk):
                cb = slice(b * 64, (b + 1) * 64)
                for hs in HS:
                    P.mm(PSB[0][hs, cb], Xp[hs, b, :], Pp[hs, b, :])
                if k < nlev:
                    for hs in HS:
                        P.mm(PSB[1][hs, cb], Pp[hs, b, :], Xp[hs, b, :])
            P.copy(Pk[:, 0:nblk, :], PSB[0][:, 0:nblk * 64].r("p (b i) -> p b i", b=nblk), eng="act")
            if k < nlev:
                P.copy(Xk[:, 0:nblk, :], PSB[1][:, 0:nblk * 64].r("p (b i) -> p b i", b=nblk))
            yield
            for b in range(nblk):
                cb = slice(b * 64, (b + 1) * 64)
                for hs in HS:
                    P.mm(PSB[2][hs, cb], Pk[hs, b, :], TTp[hs, b, :])
            P.tt(TTn[:, 0:nblk, :], TTp[:, 0:nblk, :],
                 PSB[2][:, 0:nblk * 64].r("p (b i) -> p b i", b=nblk), ALU.add)
            yield
            Xp, Pp = Xk, Pk
            TTp, TTn = TTn, TTp
        TTf = TTp
        if stop == "W2":
            return
        if not prompt:
            def bfv(t):
                return t[:, 0:512].bitcast(BF16).r("p (q i) -> p q i", q=16)
            AMSK, RMSK, BHM, KHM = bfv(Tt[0]), bfv(Tt[1]), bfv(Tt[2]), bfv(Tt[3])
            S0w = [Tt[11][:, 0:512].r("p (q i) -> p q i", q=8), Tt[5][:, 0:512].r("p (q i) -> p q i", q=8)]
            P.dma(S0[:], stw_d[l][:, hp])
            P.act(S0b[:], S0[:], AF.Identity)
            P.tt(AMSK, AR[:, 0, 0:64].un(1).bc([128, 16, 64]), seqm, ALU.mult)
            P.tt(RMSK, AR[:, 0, 64:128].un(1).bc([128, 16, 64]), seqm, ALU.mult)
            P.tt(BHM, TOK[:, 0, 1, :].un(1).bc([128, 16, 64]), seqmt.un(2).bc([128, 16, 64]), ALU.mult)
            P.tt(KHM, TOK[:, 0, 2, :].un(1).bc([128, 16, 64]), seqmt.un(2).bc([128, 16, 64]), ALU.mult)
            wc3 = WC.r("p (s t) -> p s t", s=16)
            for rnd in range(2):
                P.tt(S0w[rnd], S0[:, rnd * 8:rnd * 8 + 8, :], wc3[:, rnd * 8:rnd * 8 + 8, 3:4].bc([128, 8, 64]), ALU.mult)
        stv = STp.v((slice(None), l, hp, slice(None)), (l, hp))
        spb = STpb.v((slice(None), hp, slice(None)), hp)
        LB = PSB[0]
        for b in range(nblk):
            cb = slice(b * 64, (b + 1) * 64)

            def ops(h2):
                hs = HS[h2]
                if prompt:
                    return ([AR[hs, b, 0:64]], [AR[hs, b, 64:128]], [TOK[hs, b, 1, :]], [TOK[hs, b, 2, :]],
                            [STpb.v((hs, hp, slice(None)), hp)])
                return ([AMSK[hs, q, :] for q in range(Q)], [RMSK[hs, q, :] for q in range(Q)],
                        [BHM[hs, q, :] for q in range(Q)], [KHM[hs, q, :] for q in range(Q)],
                        [S0b[hs, q, :] for q in range(Q)])
            seqs = []
            for h2 in range(2):
                hs = HS[h2]
                a_, r_, bh_, kh_, s_ = ops(h2)
                sq = [(LB[hs, 0:64], a_[q], s_[q], q == 0, False) for q in range(Q)]
                sq.append((LB[hs, 0:64], AM[hs, b, 128:192], TOK[hs, b, 0, :], False, True))
                seqs.append(sq)
            for i in range(len(seqs[0])):
                for sq in seqs:
                    o_, l_, r2_, st_, sp_ = sq[i]
                    P.mm(o_, l_, r2_, start=st_, stop=sp_)
            P.copy(RHSb[:], LB[:, 0:64], eng="act")
            yield
            for hs in HS:
                P.mm(LB[hs, 64:128], TTf[hs, b, :], RHSb[hs, :])
            P.copy(Ub[:], LB[:, 64:128], eng="act")
            yield
            seqs = []
            for h2 in range(2):
                hs = HS[h2]
                a_, r_, bh_, kh_, s_ = ops(h2)
                YTb = PSB[3][hs, cb]
                Vt = TOK[hs, b, 0, :]
                sq = [(YTb, s_[q], r_[q], q == 0, False) for q in range(Q)]
                sq.append((YTb, Ub[hs, :], AM[hs, b, 64:128], False, False))
                sq.append((YTb, Vt, AM[hs, b, 192:256], False, True))
                seqs.append(sq)
            for i in range(len(seqs[0])):
                for sq in seqs:
                    o_, l_, r2_, st_, sp_ = sq[i]
                    P.mm(o_, l_, r2_, start=st_, stop=sp_)
            for rnd in range((Q + 7) // 8):
                nq = min(8, Q - rnd * 8)
                SNB = LB if prompt else PS[4]
                c0 = 128 if prompt else 0
                hops = [ops(h2) for h2 in range(2)]
                for qq in range(nq):
                    q = rnd * 8 + qq
                    for h2 in range(2):
                        hs = HS[h2]
                        P.mm(SNB[hs, c0 + qq * 64:c0 + (qq + 1) * 64], hops[h2][2][q], Ub[hs, :], start=True, stop=False)
                    for h2 in range(2):
                        hs = HS[h2]
                        P.mm(SNB[hs, c0 + qq * 64:c0 + (qq + 1) * 64], hops[h2][3][q], TOK[hs, b, 0, :], start=False, stop=True)
                if prompt:
                    if b < nblk - 1:
                        P.stt(spb, stv, WC[:, b * 64 + 63:b * 64 + 64], SNB[:, 128:192], ALU.mult, ALU.add)
                    P.stt(stv, stv, WC[:, b * 64 + 63:b * 64 + 64], SNB[:, 128:192], ALU.mult, ALU.add)
                else:
                    P.tt(S0[:, rnd * 8:rnd * 8 + nq, :], S0w[rnd],
                         SNB[:, 0:nq * 64].r("p (q i) -> p q i", q=nq), ALU.add)
            yield
        if not prompt:
            P.dma(o_swkv[l][:, hp], S0[:])

    def hp_gen(l, g, hp, S):
        N = g.N
        prompt = g.kind == "p"
        PV = Cur.PV
        Tt, TBs, AR, BK, VBF, PSB = S.T, S.TB, S.AR, S.BK, S.VBF, S.PSB
        hc = slice(hp * 128, (hp + 1) * 128)
        XR, XK, XV = Tt[2][:, :N], Tt[3][:, :N], Tt[4][:, :N]
        if prompt:
            stv_ = STp.v((slice(None), l, hp, slice(None)), (l, hp))
            spb_ = STpb.v((slice(None), hp, slice(None)), hp)
            if g.first:
                P.memset(stv_, 0.0)
                P.memset(spb_, 0.0)
            else:
                P.copy(spb_, stv_, eng="act")
        for i, (q, xs) in enumerate(((hp, XR), (8 + hp, XK), (16 + hp, XV))):
            w = w_next("in", l, 1024 + q * 128, 128)
            ps = PSB[i % 2]
            proj(ps[:, :N], w, 128, N)
            shift_chunk(l, g, ps[:, :N], q, 128, xs, Tt[0], Tt[5])
            yield
        if stop == "Cb":
            return
        P.mm(PSB[0][:, :N], LWT[0:64, hc], TWL[0:64, :N])
        SG = Tt[5][:, :N]
        P.act(SG, PSB[0][:, :N], AF.Tanh, scale=0.5, bias=DV[:, W0H + hp:W0H + hp + 1])
        P.ts(SG, SG, 1.0, ALU.add)
        CS = Tt[7][:, :N]
        P.scan(CS, (rm_p if prompt else rm_s)[:, :N], SG, 0.0, ALU.mult, ALU.add)
        cs3 = CS.r("p (s t) -> p s t", s=g.nb)
        CSE = Tt[8][:, :N]
        P.tt(CSE.r("p (s t) -> p s t", s=g.nb), cs3[:, :, g.Lb - 1:g.Lb].bc([128, g.nb, g.Lb]), cs3, ALU.subtract)
        P.tt(SG, CS, SG, ALU.subtract)
        WI = Tt[9][:, :N]
        P.act(WI, CS, AF.Exp, scale=H0)
        P.act(CS, CS, AF.Exp, scale=-H0)
        P.act(SG, SG, AF.Exp, scale=-H0)
        P.act(CSE, CSE, AF.Exp, scale=-H0)
        WC, WM, WE = CS, SG, CSE
        yield
        P.mm(PSB[1][:, :N], LWT[64:128, hc], TWL[64:128, :N])
        THA = Tt[10][:, :N]
        P.act(THA, PSB[1][:, :N], AF.Tanh, scale=0.5, bias=DV[:, A0H + hp:A0H + hp + 1])
        ksq = TBs[0][:, :N]
        P.act(ksq, XK, AF.Square, scale=PV[:, KK + hp:KK + hp + 1])
        P.mm(PSB[1][:, :N], ONEB[:], ksq)
        NR = Tt[0][:, :N]
        P.act(NR, PSB[1][:, :N], AF.Ln, bias=1e-24)
        P.act(NR, NR, AF.Exp, scale=-0.5)
        KKN = Tt[6][:, :N]
        P.stt(KKN, XK, PV[:, KK + hp:KK + hp + 1], NR, ALU.mult, ALU.mult)
        yield
        nbk = g.nblk
        ar3a = AR[:, 0:nbk, 0:64]
        ar3r = AR[:, 0:nbk, 64:128]
        bk3b = BK[:, 0:nbk, 0:64]
        bk3k = BK[:, 0:nbk, 64:128]

        def b3(v):
            return v.r("p (b t) -> p b t", b=nbk)
        P.stt(ar3a, b3(KKN), -1.0, b3(WM), ALU.mult, ALU.mult)
        P.tt(ar3r, b3(XR), b3(WC), ALU.mult, eng="pool")
        B2 = Tt[5][:, :N]
        P.stt(B2, THA, 1.0, KKN, ALU.add, ALU.mult)
        P.stt(bk3b, b3(B2), 0.5, b3(WI), ALU.mult, ALU.mult)
        BHF = VBF[1][:, :N]
        P.stt(BHF, B2, 0.5, WE, ALU.mult, ALU.mult)
        yield
        KF = Tt[5][:, :N]
        P.act(KF, THA, AF.Identity, scale=DV[:, KAH + hp:KAH + hp + 1], bias=DV[:, KAB + hp:KAB + hp + 1])
        P.tt(KF, KF, XK, ALU.mult, eng="pool")
        P.tt(bk3k, b3(KF), b3(WI), ALU.mult, eng="pool")
        KHF = VBF[2][:, :N]
        P.tt(KHF, KF, WE, ALU.mult)
        VB = VBF[0][:, :N]
        P.act(VB, XV, AF.Identity)
        rkb = TBs[1][:, :N]
        P.stt(rkb, XR, PV[:, RK + hp:RK + hp + 1], KF, ALU.mult, ALU.mult)
        P.mm(PSB[0][:, :N], ONEB[:], rkb)
        BON = Tt[9][:, :N]
        P.tt(BON, PSB[0][:, :N], XV, ALU.mult)
        P.mm(PSB[1][:, :N], G2A[:, hc], SGL[:, :N], start=True, stop=False)
        P.mm(PSB[1][:, :N], G2B[:, hc], SGL2[:, :N], start=False, stop=True)
        GP = Tt[10][:, :N]
        P.act(GP, PSB[1][:, :N], AF.Identity)
        yield
        for _ in wkv(l, g, hp, S, VB, BHF, KHF, WC):
            yield
        if stop in ("W1", "W2"):
            return
        YS = Tt[5][:, :N]
        P.act(YS, PSB[3][:, :N], AF.Identity)
        if stop == "C1" and hp == 0:
            dumpv("YS", YS, [128, N])
            return
        yb_, y2_ = TBs[0][:, :N], TBs[1][:, :N]
        P.act(yb_, YS, AF.Identity)
        P.act(y2_, YS, AF.Square)
        P.mm(PSB[0][:, :N], ONEB64[:], yb_)
        P.mm(PSB[1][:, :N], ONEB64[:], y2_)
        yield
        VR = Tt[6][:, :N]
        MS = Tt[7][:, :N]
        P.act(MS, PSB[0][:, :N], AF.Identity)
        P.tt(VR, MS, MS, ALU.mult, eng="pool")
        P.tt(VR, PSB[1][:, :N], VR, ALU.subtract)
        P.ts(VR, VR, 0.0, ALU.max)
        P.act(VR, VR, AF.Ln, bias=GN_EPS)
        P.act(VR, VR, AF.Exp, scale=-0.5)
        P.tt(YS, YS, MS, ALU.subtract, eng="pool")
        P.tt(YS, YS, VR, ALU.mult)
        P.act(YS, YS, AF.Identity, scale=PV[:, LNG + hp:LNG + hp + 1], bias=PV[:, LNB + hp:LNB + hp + 1])
        P.tt(YS, YS, BON, ALU.add, eng="pool")
        P.stt(BF.v((slice(None), 8 + hp, slice(0, N)), 8 + hp), YS, 0.5, GP, ALU.mult, ALU.mult)
        yield

    def run_skewed(fns, sets, lag):
        pending = list(fns)
        free = list(sets)
        active = []
        while pending or active:
            if pending and free and (not active or min(a[2] for a in active) >= lag):
                S = free.pop(0)
                active.append([pending.pop(0)(S), S, 0])
            for a in list(active):
                try:
                    next(a[0])
                    a[2] += 1
                except StopIteration:
                    active.remove(a)
                    free.append(a[1])

    def run_gens(gens):
        active = list(gens)
        while active:
            for gen in list(active):
                try:
                    next(gen)
                except StopIteration:
                    active.remove(gen)

    def run_pass(l, g):
        N, L, ti = g.N, g.L, g.ti
        prompt = g.kind == "p"
        PV = Cur.PV
        xc = [XT.v((slice(None), c, slice(0, N)), c) for c in range(8)]
        if not prompt:
            P.dma(STS[:], sts_d[l])
            P.dma(STF[:], stf_d[l])
        rstd = rmsnorm(xc, N)
        for c in range(8):
            P.stt(H[:, c, :N], xc[c], PV[:, NMG + c:NMG + c + 1], rstd, ALU.mult, ALU.mult)
        if stop == "A":
            return
        def conv_proj(cc):
            pa_, pb_ = PS[2 * (cc % 2)], PS[2 * (cc % 2) + 1]
            wa = w_next("in", l, cc * 128, 128)
            proj(pa_[:, :N], wa, 128, N)
            wb = w_next("in", l, 512 + cc * 128, 128)
            proj(pb_[:, :N], wb, 128, N)

        conv_proj(0)
        for cc in range(4):
            pa_, pb_ = PS[2 * (cc % 2)], PS[2 * (cc % 2) + 1]
            th, zah = T[4 + (cc % 2)][:, :N], T[6 + (cc % 2)][:, :N]
            P.act(th, pb_[:, :N], AF.Tanh, scale=0.5)
            P.act(zah, pa_[:, :N], AF.Identity, scale=0.5)
            if cc < 3:
                conv_proj(cc + 1)
            u3 = hv(UB[cc], 128, g, 30)
            if prompt:
                if g.first:
                    P.memset(u3[:, :, 0:30], 0.0)
                else:
                    P.copy(u3[:, :, 0:30], CHALO[:, l, cc, :].un(1), eng="act")
            else:
                P.dma(u3[:, :, 0:30], stc_d[l][:, cc])
            P.stt(u3[:, :, 30:30 + L], v3(th, g), 1.0, v3(zah, g), ALU.add, ALU.mult)
            a3 = v3(T[cc][:, :N], g)
            if prompt:
                ubf = T[8 + (cc % 2)][:, 0:272].bitcast(BF16)
                P.act(ubf[:, 0:30 + L], UB[cc][:, 0:30 + L], AF.Identity)
                for j in range(31):
                    dg = DIAG[j % 8]
                    P.ts(dg[:], IDENTB[:], PV[:, CDW + cc * 31 + j:CDW + cc * 31 + j + 1], ALU.mult)
                    P.mm(PS[4 + (cc % 2)][:, :N], dg[:], ubf[:, j:j + L], start=(j == 0), stop=(j == 30))
                P.act(T[cc][:, :N], PS[4 + (cc % 2)][:, :N], AF.Identity, bias=PV[:, CDB + cc:CDB + cc + 1])
            else:
                P.ts(a3, u3[:, :, 0:L], PV[:, CDW + cc * 31:CDW + cc * 31 + 1], ALU.mult,
                     PV[:, CDB + cc:CDB + cc + 1], ALU.add)
                for j in range(1, 31):
                    P.stt(a3, u3[:, :, j:j + L], PV[:, CDW + cc * 31 + j:CDW + cc * 31 + j + 1], a3, ALU.mult, ALU.add)
            cb_, c2_ = TB[0][:, :N], TB[1][:, :N]
            P.act(cb_, T[cc][:, :N], AF.Identity)
            P.act(c2_, T[cc][:, :N], AF.Square)
            P.mm(PS[6][:, :N], ONES[:], cb_, start=(cc == 0), stop=(cc == 3))
            P.mm(PS[7][:, :N], ONES[:], c2_, start=(cc == 0), stop=(cc == 3))
            if prompt:
                P.copy(CHALO[:, l, cc, :].un(1), u3[:, :, L:L + 30], eng="act")
            else:
                P.dma(o_sconv[l][:, cc], u3[:, :, 4:34])
        if prompt and g.last:
            P.dma(o_pconv[l], CHALO[:, l])
        mean, var = T[6][:, :N], T[7][:, :N]
        P.ts(mean, PS[6][:, :N], 1.0 / 512, ALU.mult)
        P.tt(var, mean, mean, ALU.mult)
        P.stt(var, PS[7][:, :N], 1.0 / 512, var, ALU.mult, ALU.subtract)
        P.ts(var, var, 0.0, ALU.max)
        P.act(var, var, AF.Ln, bias=LN_EPS)
        P.act(var, var, AF.Exp, scale=-0.5)
        for cc in range(4):
            a = T[cc][:, :N]
            P.tt(a, a, mean, ALU.subtract)
            P.tt(a, a, var, ALU.mult)
            P.act(a, a, AF.Identity, scale=DV[:, CLGH + cc:CLGH + cc + 1], bias=DV[:, CLBH + cc:CLBH + cc + 1])
            th = T[4 + (cc % 2)][:, :N]
            P.act(th, a, AF.Tanh)
            P.stt(CA[:, cc, :N], th, 1.0, a, ALU.add, ALU.mult)
        if stop == "B":
            dumpv("CA", CA[:, :, :N], [128, 4, N])
            return
        for q, M in ((24, 128), (25, 128), (26, 32)):
            w = w_next("in", l, 1024 + q * 128, 128)
            ps = PS[q % 2]
            proj(ps[:M, :N], w, M, N)
            xs = T[2][:M, :N]
            shift_chunk(l, g, ps[:M, :N], q, M, xs, T[0], T[1])
            if q == 24:
                P.act(TWL[0:64, :N], T[2][0:64, :N], AF.Tanh)
                P.act(TWL[64:128, :N], T[2][64:128, :N], AF.Identity)
            elif q == 25:
                P.act(T[3][:, :N], xs, AF.Tanh, scale=0.5)
                P.ts(SGL[:, :N], T[3][:, :N], 1.0, ALU.add)
            else:
                P.act(T[3][0:32, :N], xs, AF.Tanh, scale=0.5)
                P.ts(SGL2[0:32, :N], T[3][0:32, :N], 1.0, ALU.add)
        if stop == "Ca":
            return
        if prompt:
            run_skewed([(lambda S, hp=hp: hp_gen(l, g, hp, S)) for hp in range(8)], [SET0, SET1], HP_LAG)
        else:
            SS = HPS()
            SS.__dict__.update(SET0.__dict__)
            SS.PSB = PS[0:4]
            for hp in range(8):
                run_gens([hp_gen(l, g, hp, SS)])
        if prompt and g.last:
            P.dma(o_pshift[l], CSH[:, l, :, 0])
            P.dma(o_pwkv[l], STp[:, l])
        if not prompt:
            P.dma(o_sshift[l], STS[:])
        if stop in ("C", "C1", "W1", "W2", "Cb"):
            dumpv("YF", BF[:, 8:16, :N], [128, 8, N])
            dumpv("STp", STp[:, l], [128, 8, 64])
            return
        for m in range(8):
            o = 4 * (m % 2)
            wco = w_next("co", l, m * 128)
            for k in range(4):
                P.mm(PS[o][:, :N], wco[:, k, :], CA[:, k, :N], start=(k == 0), stop=(k == 3))
            wro = w_next("ro", l, m * 128)
            for k in range(8):
                P.mm(PS[o + 1][:, :N], wro[:, k, :], BF.v((slice(None), 8 + k, slice(0, N)), 8 + k), start=(k == 0), stop=(k == 7))
            wg1 = w_next("in", l, 4384 + m * 128, 128)
            proj(PS[o + 2][:, :N], wg1, 128, N)
            wg2 = w_next("in", l, 5408 + m * 128, 128)
            proj(PS[o + 3][:, :N], wg2, 128, N)
            t1, t2 = T[(m % 2) * 2][:, :N], T[(m % 2) * 2 + 1][:, :N]
            P.act(t1, PS[o + 2][:, :N], AF.Tanh, scale=0.5)
            P.act(t2, PS[o + 3][:, :N], AF.Tanh, scale=0.5)
            P.stt(t1, t1, 1.0, PS[o][:, :N], ALU.add, ALU.mult)
            P.stt(t2, t2, 1.0, PS[o + 1][:, :N], ALU.add, ALU.mult)
            P.tt(BF.v((slice(None), m, slice(0, N)), m), t1, t2, ALU.add, eng="pool")
        for mo in range(8):
            w = w_next("mx", l, mo * 128)
            ps = PS[mo % 2]
            for k in range(8):
                P.mm(ps[:, :N], w[:, k, :], BF.v((slice(None), k, slice(0, N)), k), start=(k == 0), stop=(k == 7))
            P.stt(xc[mo], ps[:, :N], 0.5, xc[mo], ALU.mult, ALU.add)
        if stop == "M":
            return
        rstd = rmsnorm(xc, N)
        for c in range(8):
            P.stt(H[:, c, :N], xc[c], PV[:, NFG + c:NFG + c + 1], rstd, ALU.mult, ALU.mult)
        for part in range(3):
            for jj in range(8):
                j = part * 8 + jj
                cus = []
                for half, q in ((0, j), (1, 24 + j)):
                    w = w_next("up", l, q * 128)
                    ps = PS[half + 2 * (jj % 2)]
                    proj(ps[:, :N], w, 128, N)
                    upb = T[half + 2 * (jj % 2)]
                    u3 = hv(upb, 128, g, 2)
                    if prompt:
                        if g.first:
                            P.memset(u3[:, :, 0:2], 0.0)
                        else:
                            P.copy(u3[:, :, 0:2], CFF[:, l, q, :].un(1), eng="act")
                    else:
                        P.copy(u3[:, :, 0:2], STF[:, q, :, :], eng="act")
                    P.act(u3[:, :, 2:2 + L], v3(ps[:, :N], g), AF.Identity)
                    cu = T[4 + half + 2 * (jj % 2)][:, :N]
                    c3 = v3(cu, g)
                    fw0 = FDW + q * 3
                    P.act(c3, v3(ps[:, :N], g), AF.Identity, scale=PV[:, fw0 + 2:fw0 + 3], bias=PV[:, FDB + q:FDB + q + 1])
                    P.stt(c3, u3[:, :, 1:1 + L], PV[:, fw0 + 1:fw0 + 2], c3, ALU.mult, ALU.add)
                    P.stt(c3, u3[:, :, 0:L], PV[:, fw0:fw0 + 1], c3, ALU.mult, ALU.add)
                    if prompt:
                        P.copy(CFF[:, l, q, :].un(1), u3[:, :, L:L + 2], eng="act")
                    else:
                        P.copy(STF[:, q, :, :], u3[:, :, L:L + 2], eng="act")
                    cus.append(cu)
                ga = T[8 + (jj % 2)][:, :N]
                P.act(ga, cus[0], AF.Gelu_apprx_tanh)
                P.tt(BF.v((slice(None), jj, slice(0, N)), jj), ga, cus[1], ALU.mult, eng="pool")
            for mo in range(8):
                w = w_next("dn", l, part, mo * 128)
                ps = PS[4 + (mo % 2)]
                for k in range(8):
                    P.mm(ps[:, :N], w[:, k, :], BF.v((slice(None), k, slice(0, N)), k), start=(k == 0), stop=(k == 7))
                P.tt(xc[mo], xc[mo], ps[:, :N], ALU.add)
        if prompt and g.last:
            P.dma(o_pffn[l], CFF[:, l])
        if not prompt:
            P.dma(o_sffn[l], STF[:])

    dbg_x = dout("dbg_xT", [128, 8, NTOK]) if (dump is not None and stop is not None) else None
    if dbg_x is not None:
        dumps["xT"] = [128, 8, NTOK]
    for g in tiles:
        N = g.N
        P.dma(XT[:, :, :N], xT_d[:, :, g.t0:g.t0 + N])
        for l in range(depth):
            layer_setup(l)
            run_pass(l, g)
        xc = [XT.v((slice(None), c, slice(0, N)), c) for c in range(8)]
        if stop is None:
            rstd = rmsnorm(xc, N)
            for c in range(8):
                yo = T[c % 4][:, :N]
                P.stt(yo, xc[c], PVF[:, c:c + 1], rstd, ALU.mult, ALU.mult)
                P.dma(o_y[:, c, g.t0:g.t0 + N], yo)
        elif dbg_x is not None:
            P.dma(dbg_x[:, :, g.t0:g.t0 + N], XT[:, :, :N])
    P.finish()
    if not dry:
        P.emit()
    return nc, P, dumps


def _pm(v):
    return np.ascontiguousarray(v.reshape(-1, 128).T)


def pack_params(inp):
    pv = np.zeros((DEPTH, 128, NPV), np.float32)
    for l in range(DEPTH):
        pv[l, :, NMG:NMG + 8] = _pm(inp["norm_mix_g"][l])
        pv[l, :, NFG:NFG + 8] = _pm(inp["norm_ffn_g"][l])
        cw = inp["conv_dw_w"][l]
        pv[l, :, CDW:CDW + 124] = cw.reshape(31, 4, 128).transpose(2, 1, 0).reshape(128, 124)
        pv[l, :, CDB:CDB + 4] = _pm(inp["conv_dw_b"][l])
        pv[l, :, CLG:CLG + 4] = _pm(inp["conv_ln_g"][l])
        pv[l, :, CLB:CLB + 4] = _pm(inp["conv_ln_b"][l])
        mu = np.zeros(27 * 128, np.float32)
        mu[:3360] = inp["rw_mu"][l]
        pv[l, :, MU:MU + 27] = _pm(mu)
        pv[l, :, W0:W0 + 8] = _pm(inp["rw_w0"][l])
        pv[l, :, A0:A0 + 8] = _pm(inp["rw_a0"][l])
        pv[l, :, KK:KK + 8] = _pm(inp["rw_k_k"][l])
        pv[l, :, KA:KA + 8] = _pm(inp["rw_k_a"][l])
        pv[l, :, RK:RK + 8] = _pm(inp["rw_r_k"][l].reshape(-1))
        pv[l, :, LNG:LNG + 8] = _pm(inp["rw_ln_g"][l])
        pv[l, :, LNB:LNB + 8] = _pm(inp["rw_ln_b"][l])
        fw_ = inp["ffn_dw_w"][l]
        pv[l, :, FDW:FDW + 144] = fw_.reshape(3, 48, 128).transpose(2, 1, 0).reshape(128, 144)
        pv[l, :, FDB:FDB + 48] = _pm(inp["ffn_dw_b"][l])
    return pv


def make_in_maps(inp, cores=None):
    cst, cstb = make_consts()
    pv = pack_params(inp)
    pvf = _pm(inp["norm_final_g"])
    lw = np.ascontiguousarray(np.concatenate([inp["rw_w2"], inp["rw_a2"]], axis=1))
    L_ = DEPTH
    wi = inp["w_in"]
    w_in_u = np.zeros((L_, 51, 128, 8, 128), np.float32)
    w_in_u[:, 0:34] = wi[:, :, 0:4352].reshape(L_, 8, 128, 34, 128).transpose(0, 3, 2, 1, 4)
    w_in_u[:, 34, :, :, 0:32] = wi[:, :, 4352:4384].reshape(L_, 8, 128, 32).transpose(0, 2, 1, 3)
    w_in_u[:, 35:51] = wi[:, :, 4384:6432].reshape(L_, 8, 128, 16, 128).transpose(0, 3, 2, 1, 4)

    def units(w, K):
        n = w.shape[2] // 128
        return np.ascontiguousarray(w.reshape(L_, K, 128, n, 128).transpose(0, 3, 2, 1, 4)).reshape(L_, n, 128, K * 128)
    w_dn_u = np.ascontiguousarray(inp["w_down"].reshape(L_, 3, 8, 128, 8, 128).transpose(0, 1, 4, 3, 2, 5)).reshape(L_, 24, 128, 1024)
    shared = dict(cst=cst, cstb=cstb, pv=pv, pvf=pvf, lw=lw, g2=inp["rw_g2"],
                  w_in_u=w_in_u.reshape(L_, 51, 128, 1024),
                  w_co_u=units(inp["w_conv_out"], 4), w_ro_u=units(inp["w_rw_out"], 8),
                  w_mx_u=units(inp["w_mix_out"], 8), w_up_u=units(inp["w_up"], 8), w_dn_u=w_dn_u)
    maps = []
    for c in (range(NCORE) if cores is None else cores):
        sb = slice(16 * c, 16 * c + 16)
        xs = np.concatenate([inp["x_prompt"][c], inp["x_sample"][sb].reshape(64, D)], axis=0)
        xT = np.ascontiguousarray(xs.T.reshape(8, 128, NTOK).transpose(1, 0, 2))
        stc = np.ascontiguousarray(inp["state_conv"][:, sb].reshape(DEPTH, 16, 30, 4, 128).transpose(0, 4, 3, 1, 2))
        ss = np.zeros((DEPTH, 16, 27 * 128), np.float32)
        ss[:, :, :3360] = inp["state_shift"][:, sb]
        sts = np.ascontiguousarray(ss.reshape(DEPTH, 16, 27, 128).transpose(0, 3, 2, 1))
        sw = inp["state_wkv"][:, sb].reshape(DEPTH, 16, 8, 2, 64, 64)
        stw = np.ascontiguousarray(sw.transpose(0, 3, 5, 2, 1, 4).reshape(DEPTH, 128, 8, 16, 64))
        stf = np.ascontiguousarray(inp["state_ffn"][:, sb].reshape(DEPTH, 16, 2, 48, 128).transpose(0, 4, 3, 1, 2))
        m = dict(shared)
        m.update(xT=xT, stc=stc, sts=sts, stw=stw, stf=stf)
        maps.append(m)
    return maps


def assemble(results):
    L_ = DEPTH
    y_prompt = np.zeros((8, SEQ, D), np.float32)
    y_sample = np.zeros((128, 4, D), np.float32)
    p_conv = np.zeros((L_, 8, 30, 512), np.float32)
    p_shift = np.zeros((L_, 8, 3360), np.float32)
    p_wkv = np.zeros((L_, 8, 16, 64, 64), np.float32)
    p_ffn = np.zeros((L_, 8, 2, 6144), np.float32)
    s_conv = np.zeros((L_, 128, 30, 512), np.float32)
    s_shift = np.zeros((L_, 128, 3360), np.float32)
    s_wkv = np.zeros((L_, 128, 16, 64, 64), np.float32)
    s_ffn = np.zeros((L_, 128, 2, 6144), np.float32)
    for c, r in enumerate(results):
        sb = slice(16 * c, 16 * c + 16)
        yT = r["o_y"].transpose(1, 0, 2).reshape(D, NTOK)
        y_prompt[c] = yT[:, :SEQ].T
        y_sample[sb] = yT[:, SEQ:].T.reshape(16, 4, D)
        p_conv[:, c] = r["o_pconv"].transpose(0, 3, 2, 1).reshape(L_, 30, 512)
        s_conv[:, sb] = r["o_sconv"].transpose(0, 3, 4, 2, 1).reshape(L_, 16, 30, 512)
        p_shift[:, c] = r["o_pshift"].transpose(0, 2, 1).reshape(L_, 27 * 128)[:, :3360]
        s_shift[:, sb] = r["o_sshift"].transpose(0, 3, 2, 1).reshape(L_, 16, 27 * 128)[:, :, :3360]
        pw = r["o_pwkv"].reshape(L_, 2, 64, 8, 64)
        p_wkv[:, c] = pw.transpose(0, 3, 1, 4, 2).reshape(L_, 16, 64, 64)
        sw = r["o_swkv"].reshape(L_, 2, 64, 8, 16, 64)
        s_wkv[:, sb] = sw.transpose(0, 4, 3, 1, 5, 2).reshape(L_, 16, 16, 64, 64)
        p_ffn[:, c] = r["o_pffn"].transpose(0, 3, 2, 1).reshape(L_, 2, 6144)
        s_ffn[:, sb] = r["o_sffn"].transpose(0, 3, 4, 2, 1).reshape(L_, 16, 2, 6144)
    return (y_prompt, y_sample, p_conv, p_shift, p_wkv, p_ffn, s_conv, s_shift, s_wkv, s_ffn)


def kernel(**inputs):
    inp = {k: np.asarray(v) for k, v in inputs.items()}
    nc, P, _ = build()
    maps = make_in_maps(inp)
    res = run_bass_kernel_spmd(nc, maps, core_ids=list(range(NCORE)))
    return assemble(res.results)
```

```python
import numpy as np
import concourse.bass as bass
import concourse.mybir as mybir

F32 = mybir.dt.float32
BF16 = mybir.dt.bfloat16
AF = mybir.ActivationFunctionType
ALU = mybir.AluOpType


ATTACH_WAIT = 1
SAME_ENG_MASK = 7


class V:
    __slots__ = ("tile", "ap", "sub")

    def __init__(self, tile, ap, sub=None):
        self.tile = tile
        self.ap = ap
        self.sub = sub

    def __getitem__(self, idx):
        return V(self.tile, self.ap[idx], self.sub)

    def k(self, sub):
        return V(self.tile, self.ap, sub)

    def r(self, pat, **kw):
        return V(self.tile, self.ap.rearrange(pat, **kw), self.sub)

    def bc(self, shape):
        return V(self.tile, self.ap.broadcast_to(shape), self.sub)

    def un(self, axis):
        return V(self.tile, self.ap.unsqueeze(axis), self.sub)

    def bitcast(self, dt):
        return V(self.tile, self.ap.bitcast(dt), self.sub)


class Tile:
    def __init__(self, name, handle):
        self.name = name
        self.h = handle

    def __getitem__(self, idx):
        return V(self.name, self.h[idx], None)

    def v(self, idx, sub):
        return V(self.name, self.h[idx], sub)


class Op:
    __slots__ = ("id", "eng", "fn", "dma", "deps", "sem", "semval", "signal", "rank", "prewait")

    def __init__(self, id, eng, fn, dma):
        self.id = id
        self.eng = eng
        self.fn = fn
        self.dma = dma
        self.deps = {}
        self.sem = None
        self.semval = 0
        self.signal = False
        self.rank = 0
        self.prewait = None


class Prog:
    ENGS = ("pe", "act", "dve", "pool", "sp")

    def __init__(self, nc, n_dma_sems=40):
        self.nc = nc
        self.ops = []
        self.state = {}
        self.n_dma_sems = n_dma_sems
        self.sb_bytes = 0

    def sb(self, name, shape, dtype):
        h = self.nc.alloc_sbuf_tensor("sb_" + name, list(shape), dtype)
        n = 1
        for s in shape[1:]:
            n *= s
        self.sb_bytes += n * (4 if dtype == F32 else 2)
        return Tile(name, h)

    def ps(self, name, shape, dtype=F32):
        h = self.nc.alloc_psum_tensor("ps_" + name, list(shape), dtype)
        return Tile(name, h)

    def _collect(self, v, is_write, deps):
        if v is None or not isinstance(v, V) or v.tile is None:
            return
        st = self.state.setdefault(v.tile, {})
        subs = list(st.keys()) if v.sub is None else [s for s in (v.sub, None) if s in st]
        for s in subs:
            w, rs = st[s]
            if w is not None:
                deps[w] = deps.get(w, 0) | (2 if is_write else 1)
            if is_write:
                for r in rs:
                    deps[r] = deps.get(r, 0) | 4

    def _update(self, v, is_write, opid):
        if v is None or not isinstance(v, V) or v.tile is None:
            return
        st = self.state.setdefault(v.tile, {})
        if is_write:
            if v.sub is None:
                st.clear()
            st[v.sub] = [opid, []]
        else:
            if v.sub not in st:
                st[v.sub] = [None, []]
            st[v.sub][1].append(opid)

    def add(self, eng, fn, reads, writes, dma=False):
        op = Op(len(self.ops), eng, fn, dma)
        deps = {}
        for v in reads:
            self._collect(v, False, deps)
        for v in writes:
            self._collect(v, True, deps)
        for v in reads:
            self._update(v, False, op.id)
        for v in writes:
            self._update(v, True, op.id)
        deps.pop(op.id, None)
        op.deps = deps
        self.ops.append(op)
        return op

    @staticmethod
    def _a(x):
        return x.ap if isinstance(x, V) else x

    def mm(self, out, lhsT, rhs, start=True, stop=True):
        a = self._a
        return self.add("pe", lambda e: e.matmul(a(out), a(lhsT), a(rhs), start=start, stop=stop),
                        [lhsT, rhs], [out])

    def tr(self, out, in_, ident):
        a = self._a
        return self.add("pe", lambda e: e.transpose(a(out), a(in_), a(ident)), [in_, ident], [out])

    def act(self, out, in_, func, scale=1.0, bias=0.0, eng="act"):
        a = self._a
        return self.add(eng, lambda e: e.activation(a(out), a(in_), func, bias=a(bias), scale=a(scale)),
                        [in_, scale, bias], [out])

    def tt(self, out, in0, in1, op, eng="dve"):
        a = self._a
        return self.add(eng, lambda e: e.tensor_tensor(a(out), a(in0), a(in1), op), [in0, in1], [out])

    def ts(self, out, in0, s1, op0, s2=None, op1=None, eng="dve"):
        a = self._a
        if op1 is None:
            return self.add(eng, lambda e: e.tensor_scalar(a(out), a(in0), a(s1), None, op0), [in0, s1], [out])
        return self.add(eng, lambda e: e.tensor_scalar(a(out), a(in0), a(s1), a(s2), op0, op1),
                        [in0, s1, s2], [out])

    def stt(self, out, in0, scalar, in1, op0, op1):
        a = self._a
        return self.add("dve", lambda e: e.scalar_tensor_tensor(a(out), a(in0), a(scalar), a(in1), op0, op1),
                        [in0, scalar, in1], [out])

    def scan(self, out, d0, d1, init, op0, op1):
        a = self._a
        return self.add("dve", lambda e: e.tensor_tensor_scan(a(out), a(d0), a(d1), a(init), op0, op1),
                        [d0, d1, init], [out])

    def copy(self, out, in_, eng="dve"):
        a = self._a
        if eng == "act":
            return self.add("act", lambda e: e.copy(a(out), a(in_)), [in_], [out])
        return self.add(eng, lambda e: e.tensor_copy(a(out), a(in_)), [in_], [out])

    def recip(self, out, in_):
        a = self._a
        return self.add("dve", lambda e: e.reciprocal(a(out), a(in_)), [in_], [out])

    def memset(self, out, val, eng="dve"):
        a = self._a
        return self.add(eng, lambda e: e.memset(a(out), val), [], [out])

    def dma(self, out, in_, eng="sp"):
        a = self._a
        return self.add(eng, lambda e: e.dma_start(out=a(out), in_=a(in_)), [in_], [out], dma=True)

    def finish(self, eng="sp"):
        op = Op(len(self.ops), eng, lambda e: None, False)
        op.deps = {o.id: 1 for o in self.ops if o.dma}
        self.ops.append(op)

    def emit(self):
        nc = self.nc
        ops = self.ops
        from contextlib import ExitStack
        with ExitStack() as es:
            eng_sem = {e: es.enter_context(nc.semaphore("s_" + e)) for e in self.ENGS}
            dma_sems = [es.enter_context(nc.semaphore("d%d" % i)) for i in range(self.n_dma_sems)]
            nsw = (self.n_dma_sems * 3) // 5
            pools = {True: list(range(0, nsw)), False: list(range(nsw, self.n_dma_sems))}
            dma_cnt = [0] * self.n_dma_sems
            dma_last = [None] * self.n_dma_sems
            kk_ = {True: 0, False: 0}
            for op in ops:
                if op.dma:
                    sw = op.eng == "pool"
                    pl = pools[sw]
                    i = pl[kk_[sw] % len(pl)]
                    kk_[sw] += 1
                    op.prewait = dma_last[i]
                    dma_cnt[i] += 16
                    op.sem = dma_sems[i]
                    op.semval = dma_cnt[i]
                    dma_last[i] = op.id
            for op in ops:
                for d, kind in op.deps.items():
                    dop = ops[d]
                    if dop.dma:
                        continue
                    if dop.eng == op.eng:
                        if op.eng == "pe":
                            continue
                        if not (kind & SAME_ENG_MASK):
                            continue
                    dop.signal = True
            rank = {e: 0 for e in self.ENGS}
            for op in ops:
                if not op.dma and op.signal:
                    rank[op.eng] += 1
                    op.rank = rank[op.eng]
            clocks = [None] * len(ops)
            eng_clock = {e: {} for e in self.ENGS}
            dma_known = {e: set() for e in self.ENGS}
            plans = {e: [] for e in self.ENGS}
            for op in ops:
                ck = eng_clock[op.eng]
                waits = []
                deplist = list(op.deps.items())
                if op.prewait is not None:
                    deplist.append((op.prewait, 3))
                for d, kind in sorted(deplist):
                    dop = ops[d]
                    if dop.dma:
                        if d in dma_known[op.eng]:
                            continue
                        waits.append((dop.sem, dop.semval))
                        dma_known[op.eng].add(d)
                    else:
                        if dop.eng == op.eng and (op.eng == "pe" or not (kind & SAME_ENG_MASK)):
                            continue
                        key = dop.eng
                        if ck.get(key, 0) >= dop.rank:
                            continue
                        waits.append((eng_sem[dop.eng], dop.rank))
                    for kk, vv in clocks[d].items():
                        if ck.get(kk, 0) < vv:
                            ck[kk] = vv
                best = {}
                for s, val in waits:
                    kid = id(s)
                    if kid not in best or best[kid][1] < val:
                        best[kid] = (s, val)
                myck = dict(ck)
                if (not op.dma) and op.signal:
                    myck[op.eng] = op.rank
                clocks[op.id] = myck
                plans[op.eng].append((op, list(best.values())))
            self.n_waits = sum(len(w) for e in plans for _, w in plans[e])
            handles = {"pe": "tensor", "act": "scalar", "dve": "vector", "pool": "gpsimd", "sp": "sync"}
            with nc.Block() as block:
                def run(eng_name):
                    def body(e):
                        for op, waits in plans[eng_name]:
                            attach = None
                            if ATTACH_WAIT and waits and not op.dma and op.eng != "sp":
                                attach = waits[-1]
                                waits = waits[:-1]
                            for s, val in waits:
                                e.wait_ge(s, val)
                            ins = op.fn(e)
                            if ins is None:
                                if attach is not None:
                                    e.wait_ge(*attach)
                                continue
                            if attach is not None:
                                ins._wait_ge(attach[0], attach[1])
                            if op.dma:
                                ins.then_inc(op.sem, 16)
                            elif op.signal:
                                ins.then_inc(eng_sem[eng_name], 1)
                    return body
                block.tensor(run("pe"))
                block.scalar(run("act"))
                block.vector(run("dve"))
                block.gpsimd(run("pool"))
                block.sync(run("sp"))


from concourse.bass_utils import run_bass_kernel_spmd

DEPTH = 4
NCORE = 8
HP_LAG = 0
D = 1024
SEQ = 2048
NTOK = 2112
RMS_EPS = 1e-6
LN_EPS = 1e-5
GN_EPS = 64e-5
H0 = 0.5 * float(np.exp(-0.5))

NMG, NFG, CDW, CDB, CLG, CLB, MU, W0, A0, KK, KA, RK, LNG, LNB, FDW, FDB, NPV = (
    0, 8, 16, 140, 144, 148, 152, 179, 187, 195, 203, 211, 219, 227, 235, 379, 427)
CLGH, CLBH, W0H, A0H, KAH, KAB, OMM, NDV = 0, 4, 8, 16, 24, 32, 40, 67
C_ID, C_IDB, NCST = 0, 128, 192
C_MP, C_MS, C_RMP, C_RMS, C_SEQM, C_SEQMT, NCSTB = 0, 320, 640, 1152, 1216, 2240, 2256


def make_consts():
    import ml_dtypes
    c0 = np.zeros((128, NCST), np.float32)
    c0[:, C_ID:C_ID + 128] = np.eye(128, dtype=np.float32)
    p = np.arange(128) % 64
    c0[:, C_IDB:C_IDB + 64] = (p[:, None] == np.arange(64)[None, :])
    c = np.zeros((128, NCSTB), np.float32)
    t = np.arange(64)[None, :]
    s = p[:, None]
    for base, same in ((C_MP, np.ones((128, 64), bool)), (C_MS, (s // 4) == (t // 4))):
        su = (s < t) & same
        iu = (s <= t) & same
        sl = (s > t) & same
        c[:, base:base + 320] = np.concatenate([su, iu, su, iu, sl], axis=1)
    c[:, C_RMP:C_RMP + 512] = (np.arange(512) % 64 != 0)[None, :]
    c[:, C_RMS:C_RMS + 64] = (np.arange(64) % 4 != 0)[None, :]
    sm = (np.arange(16)[:, None] == (np.arange(64) // 4)[None, :]).astype(np.float32)
    c[:, C_SEQM:C_SEQM + 1024] = sm.reshape(1, 1024)
    c[:, C_SEQMT:C_SEQMT + 16] = ((p // 4)[:, None] == np.arange(16)[None, :])
    return c0, c.astype(ml_dtypes.bfloat16)


class Geo:
    def __init__(self, kind, t0, ti, first=False, last=False):
        self.kind, self.t0, self.ti, self.first, self.last = kind, t0, ti, first, last
        if kind == "p":
            self.N, self.nseq, self.L, self.nb, self.Lb, self.nblk = 512, 1, 512, 8, 64, 8
        else:
            self.N, self.nseq, self.L, self.nb, self.Lb, self.nblk = 64, 16, 4, 16, 4, 1


def build(depth=DEPTH, tiles=None, dump=None, stop=None):
    sched = []
    _build(depth, tiles, None, stop, sched, True)
    return _build(depth, tiles, dump, stop, sched, False)


def _build(depth, tiles, dump, stop, sched, dry):
    nc = bass.Bass("TRN2", target_bir_lowering=False)
    P = Prog(nc)
    L_ = DEPTH

    def din(name, shape, dt=F32):
        return nc.dram_tensor(name, list(shape), dt, kind="ExternalInput").ap()

    def dout(name, shape):
        return nc.dram_tensor(name, list(shape), F32, kind="ExternalOutput").ap()

    xT_d = din("xT", [128, 8, NTOK])
    cst_d = din("cst", [128, NCST])
    cstb_d = din("cstb", [128, NCSTB], BF16)
    pv_d = din("pv", [L_, 128, NPV])
    pvf_d = din("pvf", [128, 8])
    stc_d = din("stc", [L_, 128, 4, 16, 30])
    sts_d = din("sts", [L_, 128, 27, 16])
    stw_d = din("stw", [L_, 128, 8, 16, 64])
    stf_d = din("stf", [L_, 128, 48, 16, 2])
    w_in_d = din("w_in_u", [L_, 51, 128, 1024])
    wco_d = din("w_co_u", [L_, 8, 128, 512])
    lw_d = din("lw", [L_, 128, 1024])
    g2_d = din("g2", [L_, 160, 1024])
    wro_d = din("w_ro_u", [L_, 8, 128, 1024])
    wmx_d = din("w_mx_u", [L_, 8, 128, 1024])
    wup_d = din("w_up_u", [L_, 48, 128, 1024])
    wdn_d = din("w_dn_u", [L_, 24, 128, 1024])

    o_y = dout("o_y", [128, 8, NTOK])
    o_pconv = dout("o_pconv", [L_, 128, 4, 30])
    o_sconv = dout("o_sconv", [L_, 128, 4, 16, 30])
    o_pshift = dout("o_pshift", [L_, 128, 27])
    o_sshift = dout("o_sshift", [L_, 128, 27, 16])
    o_pwkv = dout("o_pwkv", [L_, 128, 8, 64])
    o_swkv = dout("o_swkv", [L_, 128, 8, 16, 64])
    o_pffn = dout("o_pffn", [L_, 128, 48, 2])
    o_sffn = dout("o_sffn", [L_, 128, 48, 16, 2])

    if tiles is None:
        tiles = [Geo("p", 512 * i, i, first=(i == 0), last=(i == 3)) for i in range(4)] + [Geo("s", 2048, 4)]

    XT = P.sb("XT", [128, 8, 512], F32)
    H = P.sb("H", [128, 8, 512], BF16)
    NSLOT = 8
    WB = [P.sb("WB%d" % i, [128, 1024], BF16) for i in range(NSLOT)]
    UB = [P.sb("UB%d" % i, [128, 544], F32) for i in range(4)]
    CA = P.sb("CA", [128, 4, 512], BF16)
    BF = P.sb("BF", [128, 16, 512], BF16)
    LWT = P.sb("LWT", [128, 1024], BF16)
    G2A = P.sb("G2A", [128, 1024], BF16)
    G2B = P.sb("G2B", [128, 1024], BF16)
    TWL = P.sb("TWL", [128, 512], BF16)
    SGL = P.sb("SGL", [128, 512], BF16)
    SGL2 = P.sb("SGL2", [128, 512], BF16)
    T = [P.sb("T%d" % i, [128, 544], F32) for i in range(12)]
    TB = [P.sb("TBh%d" % i, [128, 512], BF16) for i in range(2)]

    class HPS:
        pass

    def mk_set(i, Tl, psb):
        S = HPS()
        S.T = Tl
        S.TB = [P.sb("hTB%d_%d" % (i, j), [128, 512], BF16) for j in range(2)] if i else TB
        S.AR = P.sb("AR%d" % i, [128, 8, 128], BF16)
        S.BK = P.sb("BK%d" % i, [128, 8, 128], BF16)
        S.TOK = P.sb("TOK%d" % i, [128, 8, 3, 64], BF16)
        S.AM = P.sb("AM%d" % i, [128, 8, 320], BF16)
        S.CH = [P.sb("CH%d_%d" % (i, j), [128, 8, 64], BF16) for j in range(4)]
        S.TTa = P.sb("TTa%d" % i, [128, 8, 64], BF16)
        S.TTb = P.sb("TTb%d" % i, [128, 8, 64], BF16)
        S.RHSb = P.sb("RHSb%d" % i, [128, 64], BF16)
        S.Ub = P.sb("Ub%d" % i, [128, 64], BF16)
        S.VBF = [P.sb("VBF%d_%d" % (i, j), [128, 512], BF16) for j in range(3)]
        S.PSB = psb
        return S

    PS = [P.ps("PS%d" % i, [128, 512], F32) for i in range(8)]
    T1 = {j: P.sb("hT1_%d" % j, [128, 544], F32) for j in (0, 2, 3, 4, 5, 6, 7, 8, 9, 10)}
    SET0 = mk_set(0, {j: T[j] for j in range(12)}, PS[0:4])
    SET1 = mk_set(1, T1, PS[4:8])
    SETS = mk_set
    STpb = P.sb("STpb", [128, 8, 64], BF16)
    S0 = P.sb("S0", [128, 16, 64], F32)
    S0b = P.sb("S0b", [128, 16, 64], BF16)
    STS = P.sb("STS", [128, 27, 16], F32)
    STF = P.sb("STF", [128, 48, 16, 2], F32)
    CHALO = P.sb("CHALO", [128, L_, 4, 30], F32)
    CSH = P.sb("CSH", [128, L_, 27, 1], F32)
    STp = P.sb("STp", [128, L_, 8, 64], F32)
    CFF = P.sb("CFF", [128, L_, 48, 2], F32)
    PVs = [P.sb("PV%d" % i, [128, NPV], F32) for i in range(1)]
    DV = P.sb("DV", [128, NDV], F32)
    PVF = P.sb("PVF", [128, 8], F32)
    CST = P.sb("CST", [128, NCST], F32)
    CSTB = P.sb("CSTB", [128, NCSTB], BF16)
    IDBb = P.sb("IDBb", [128, 64], BF16)
    IDENTB = P.sb("IDENTB", [128, 128], BF16)
    DIAG = [P.sb("DIAG%d" % i, [128, 128], BF16) for i in range(8)]
    ONES = P.sb("ONES", [128, 128], BF16)
    ONEB = P.sb("ONEB", [128, 128], BF16)
    ONEB64 = P.sb("ONEB64", [128, 128], BF16)

    mask_p = CSTB[:, C_MP:C_MP + 320]
    mask_s = CSTB[:, C_MS:C_MS + 320]
    rm_p = CSTB[:, C_RMP:C_RMP + 512]
    rm_s = CSTB[:, C_RMS:C_RMS + 64]
    seqm = CSTB[:, C_SEQM:C_SEQM + 1024].r("p (q t) -> p q t", q=16)
    seqmt = CSTB[:, C_SEQMT:C_SEQMT + 16]

    dumps = {}

    def dumpv(name, view, shape):
        if dump is None or name not in dump:
            return
        d = dout("dbg_" + name, shape)
        P.dma(d, view, eng="pool")
        dumps[name] = shape

    def wsrc(kind, l, a, b=None):
        if kind == "in":
            idx = a // 128 if a < 4352 else (34 if a == 4352 else 35 + (a - 4384) // 128)
            return w_in_d[l, idx].rearrange("p (k m) -> p k m", k=8), 8, 128
        if kind == "co":
            return wco_d[l, a // 128].rearrange("p (k m) -> p k m", k=4), 4, 128
        if kind == "ro":
            return wro_d[l, a // 128].rearrange("p (k m) -> p k m", k=8), 8, 128
        if kind == "mx":
            return wmx_d[l, a // 128].rearrange("p (k m) -> p k m", k=8), 8, 128
        if kind == "up":
            return wup_d[l, a // 128].rearrange("p (k m) -> p k m", k=8), 8, 128
        if kind == "dn":
            return wdn_d[l, a * 8 + b // 128].rearrange("p (k m) -> p k m", k=8), 8, 128
        raise ValueError(kind)

    class WS:
        issued = 0
        taken = 0

    def w_issue():
        i = WS.issued
        if i >= len(sched):
            return
        src, K, M = wsrc(*sched[i])
        dst = WB[i % NSLOT][:, 0:K * M].r("p (k m) -> p k m", k=K)
        P.dma(dst, src, eng="pool")
        WS.issued += 1

    def w_next(*unit):
        i = WS.taken
        if dry:
            sched.append(unit)
        else:
            assert sched[i] == unit, (i, sched[i], unit)
            while WS.issued < min(len(sched), i + NSLOT - 1):
                w_issue()
        _, K, M = wsrc(*unit)
        WS.taken += 1
        return WB[i % NSLOT][:, 0:K * M].r("p (k m) -> p k m", k=K)

    def v3(v, g):
        return v.r("p (s t) -> p s t", s=g.nseq)

    def hv(tile, M, g, h):
        return tile[:M, 0:g.nseq * (h + g.L)].r("p (s t) -> p s t", s=g.nseq)

    def proj(ps_view, w, M, N):
        for k in range(8):
            P.mm(ps_view, w[:, k, 0:M], H[:, k, :N], start=(k == 0), stop=(k == 7))

    def rmsnorm(xc, N):
        for c in range(8):
            sq = TB[c % 2][:, :N]
            P.act(sq, xc[c], AF.Square)
            P.mm(PS[2][:, :N], ONES[:], sq, start=(c == 0), stop=(c == 7))
        t = T[10][:, :N]
        P.act(t, PS[2][:, :N], AF.Ln, scale=1.0 / D, bias=RMS_EPS)
        P.act(t, t, AF.Exp, scale=-0.5)
        return t

    P.dma(CST[:], cst_d)
    P.dma(CSTB[:], cstb_d)
    P.dma(PVF[:], pvf_d)
    P.act(IDBb[:], CST[:, C_IDB:C_IDB + 64], AF.Identity)
    P.act(IDENTB[:], CST[:, C_ID:C_ID + 128], AF.Identity)
    P.memset(ONES[:], 1.0)
    P.memset(CSH[:], 0.0)
    P.memset(G2B[:], 0.0)
    P.memset(SGL2[:], 0.0)
    P.memset(ONEB[:], 0.0)
    P.memset(ONEB[0:64, 0:64], 1.0)
    P.memset(ONEB[64:128, 64:128], 1.0)
    P.act(ONEB64[:], ONEB[:], AF.Identity, scale=1.0 / 64)

    class Cur:
        PV = None
        npass = 0

    def layer_setup(l):
        PV = PVs[0]
        Cur.PV = PV
        Cur.npass += 1
        P.dma(PV[:], pv_d[l])
        P.dma(LWT[:], lw_d[l], eng="pool")
        P.dma(G2A[:], g2_d[l][0:128, :], eng="pool")
        P.dma(G2B[0:32, :], g2_d[l][128:160, :], eng="pool")
        P.ts(DV[:, CLGH:CLGH + 4], PV[:, CLG:CLG + 4], 0.5, ALU.mult)
        P.ts(DV[:, CLBH:CLBH + 4], PV[:, CLB:CLB + 4], 0.5, ALU.mult)
        P.ts(DV[:, W0H:W0H + 8], PV[:, W0:W0 + 8], 0.5, ALU.mult)
        P.ts(DV[:, A0H:A0H + 8], PV[:, A0:A0 + 8], 0.5, ALU.mult)
        P.ts(DV[:, KAH:KAH + 8], PV[:, KA:KA + 8], 0.5, ALU.mult)
        P.ts(DV[:, KAB:KAB + 8], PV[:, KA:KA + 8], -0.5, ALU.mult, 1.0, ALU.add)
        P.ts(DV[:, OMM:OMM + 27], PV[:, MU:MU + 27], -1.0, ALU.mult, 1.0, ALU.add)

    def shift_chunk(l, g, ps_view, q, M, out_xs, zs_tile, d_tile):
        N, L = g.N, g.L
        PV = Cur.PV
        z3 = hv(zs_tile, M, g, 1)
        if g.kind == "p":
            if g.first:
                P.memset(z3[:, :, 0:1], 0.0)
            else:
                P.copy(z3[:, :, 0:1], CSH[:M, l, q:q + 1, :], eng="act")
        else:
            P.copy(z3[:, :, 0:1], STS[:M, q, :].un(2), eng="act")
        P.act(z3[:, :, 1:1 + L], v3(ps_view, g), AF.Identity)
        d3 = v3(d_tile[:M, :N], g)
        P.act(d3, v3(ps_view, g), AF.Identity, scale=DV[:M, OMM + q:OMM + q + 1])
        P.stt(v3(out_xs, g), z3[:, :, 0:L], PV[:M, MU + q:MU + q + 1], d3, ALU.mult, ALU.add)
        if g.kind == "p":
            P.copy(CSH[:M, l, q:q + 1, :], z3[:, :, L:L + 1], eng="act")
        else:
            P.copy(STS[:M, q, :].un(2), z3[:, :, L:L + 1], eng="act")

    def wkv(l, g, hp, S, XV, BHF, KHF, WC):
        N, nblk, Q = g.N, g.nblk, g.nseq
        HS = [slice(0, 64), slice(64, 128)]
        prompt = g.kind == "p"
        nlev = 5 if prompt else 1
        MASK = mask_p if prompt else mask_s
        AR, BK, TOK, AM, CH, TTa, TTb, RHSb, Ub, PSB = S.AR, S.BK, S.TOK, S.AM, S.CH, S.TTa, S.TTb, S.RHSb, S.Ub, S.PSB
        Tt = S.T
        for b in range(nblk):
            cb = slice(b * 64, (b + 1) * 64)
            pa = PSB[b % 2]
            ptb = PSB[2 + (b % 2)][:, 0:96].bitcast(BF16)
            for qi, src in enumerate((XV, BHF, KHF)):
                for hs in HS:
                    P.tr(ptb[hs, qi * 64:(qi + 1) * 64], src[hs, cb], IDENTB[hs, hs])
            for hs in HS:
                P.mm(pa[hs, 0:128], BK[hs, b, 0:64], AR[hs, b, :])
            for hs in HS:
                P.mm(pa[hs, 128:256], BK[hs, b, 64:128], AR[hs, b, :])
            for hs in HS:
                P.mm(pa[hs, 256:320], AR[hs, b, 0:64], BK[hs, b, 0:64])
            P.copy(TOK[:, b, :, :], ptb.r("p (q i) -> p q i", q=3), eng="act")
            P.tt(AM[:, b, :], pa[:, 0:320], MASK, ALU.mult)
            if b % 2 == 1:
                yield
        yield
        if stop == "W1":
            return
        P.tt(TTa[:, 0:nblk, :], AM[:, 0:nblk, 0:64], IDBb[:].un(1).bc([128, nblk, 64]), ALU.add)
        Xp = AM[:, :, 0:64]
        Pp = AM[:, :, 256:320]
        TTp, TTn = TTa, TTb
        for k in range(1, nlev + 1):
            Pk = CH[(k % 2) * 2]
            Xk = CH[(k % 2) * 2 + 1]
            for b in range(nblk):
                cb = slice(b * 64, (b + 1) * 64)
                for hs in HS:
                    P.mm(PSB[0][hs, cb], Xp[hs, b, :], Pp[hs, b, :])
                if k < nlev:
                    for hs in HS:
                        P.mm(PSB[1][hs, cb], Pp[hs, b, :], Xp[hs, b, :])
            P.copy(Pk[:, 0:nblk, :], PSB[0][:, 0:nblk * 64].r("p (b i) -> p b i", b=nblk), eng="act")
            if k < nlev:
                P.copy(Xk[:, 0:nblk, :], PSB[1][:, 0:nblk * 64].r("p (b i) -> p b i", b=nblk))
            yield
            for b in range(nblk):
                cb = slice(b * 64, (b + 1) * 64)
                for hs in HS:
                    P.mm(PSB[2][hs, cb], Pk[hs, b, :], TTp[hs, b, :])
            P.tt(TTn[:, 0:nblk, :], TTp[:, 0:nblk, :],
                 PSB[2][:, 0:nblk * 64].r("p (b i) -> p b i", b=nblk), ALU.add)
            yield
            Xp, Pp = Xk, Pk
            TTp, TTn = TTn, TTp
        TTf = TTp
        if stop == "W2":
            return
        if not prompt:
            def bfv(t):
                return t[:, 0:512].bitcast(BF16).r("p (q i) -> p q i", q=16)
            AMSK, RMSK, BHM, KHM = bfv(Tt[0]), bfv(Tt[1]), bfv(Tt[2]), bfv(Tt[3])
            S0w = [Tt[11][:, 0:512].r("p (q i) -> p q i", q=8), Tt[5][:, 0:512].r("p (q i) -> p q i", q=8)]
            P.dma(S0[:], stw_d[l][:, hp])
            P.act(S0b[:], S0[:], AF.Identity)
            P.tt(AMSK, AR[:, 0, 0:64].un(1).bc([128, 16, 64]), seqm, ALU.mult)
            P.tt(RMSK, AR[:, 0, 64:128].un(1).bc([128, 16, 64]), seqm, ALU.mult)
            P.tt(BHM, TOK[:, 0, 1, :].un(1).bc([128, 16, 64]), seqmt.un(2).bc([128, 16, 64]), ALU.mult)
            P.tt(KHM, TOK[:, 0, 2, :].un(1).bc([128, 16, 64]), seqmt.un(2).bc([128, 16, 64]), ALU.mult)
            wc3 = WC.r("p (s t) -> p s t", s=16)
            for rnd in range(2):
                P.tt(S0w[rnd], S0[:, rnd * 8:rnd * 8 + 8, :], wc3[:, rnd * 8:rnd * 8 + 8, 3:4].bc([128, 8, 64]), ALU.mult)
        stv = STp.v((slice(None), l, hp, slice(None)), (l, hp))
        spb = STpb.v((slice(None), hp, slice(None)), hp)
        LB = PSB[0]
        for b in range(nblk):
            cb = slice(b * 64, (b + 1) * 64)

            def ops(h2):
                hs = HS[h2]
                if prompt:
                    return ([AR[hs, b, 0:64]], [AR[hs, b, 64:128]], [TOK[hs, b, 1, :]], [TOK[hs, b, 2, :]],
                            [STpb.v((hs, hp, slice(None)), hp)])
                return ([AMSK[hs, q, :] for q in range(Q)], [RMSK[hs, q, :] for q in range(Q)],
                        [BHM[hs, q, :] for q in range(Q)], [KHM[hs, q, :] for q in range(Q)],
                        [S0b[hs, q, :] for q in range(Q)])
            seqs = []
            for h2 in range(2):
                hs = HS[h2]
                a_, r_, bh_, kh_, s_ = ops(h2)
                sq = [(LB[hs, 0:64], a_[q], s_[q], q == 0, False) for q in range(Q)]
                sq.append((LB[hs, 0:64], AM[hs, b, 128:192], TOK[hs, b, 0, :], False, True))
                seqs.append(sq)
            for i in range(len(seqs[0])):
                for sq in seqs:
                    o_, l_, r2_, st_, sp_ = sq[i]
                    P.mm(o_, l_, r2_, start=st_, stop=sp_)
            P.copy(RHSb[:], LB[:, 0:64], eng="act")
            yield
            for hs in HS:
                P.mm(LB[hs, 64:128], TTf[hs, b, :], RHSb[hs, :])
            P.copy(Ub[:], LB[:, 64:128], eng="act")
            yield
            seqs = []
            for h2 in range(2):
                hs = HS[h2]
                a_, r_, bh_, kh_, s_ = ops(h2)
                YTb = PSB[3][hs, cb]
                Vt = TOK[hs, b, 0, :]
                sq = [(YTb, s_[q], r_[q], q == 0, False) for q in range(Q)]
                sq.append((YTb, Ub[hs, :], AM[hs, b, 64:128], False, False))
                sq.append((YTb, Vt, AM[hs, b, 192:256], False, True))
                seqs.append(sq)
            for i in range(len(seqs[0])):
                for sq in seqs:
                    o_, l_, r2_, st_, sp_ = sq[i]
                    P.mm(o_, l_, r2_, start=st_, stop=sp_)
            for rnd in range((Q + 7) // 8):
                nq = min(8, Q - rnd * 8)
                SNB = LB if prompt else PS[4]
                c0 = 128 if prompt else 0
                hops = [ops(h2) for h2 in range(2)]
                for qq in range(nq):
                    q = rnd * 8 + qq
                    for h2 in range(2):
                        hs = HS[h2]
                        P.mm(SNB[hs, c0 + qq * 64:c0 + (qq + 1) * 64], hops[h2][2][q], Ub[hs, :], start=True, stop=False)
                    for h2 in range(2):
                        hs = HS[h2]
                        P.mm(SNB[hs, c0 + qq * 64:c0 + (qq + 1) * 64], hops[h2][3][q], TOK[hs, b, 0, :], start=False, stop=True)
                if prompt:
                    if b < nblk - 1:
                        P.stt(spb, stv, WC[:, b * 64 + 63:b * 64 + 64], SNB[:, 128:192], ALU.mult, ALU.add)
                    P.stt(stv, stv, WC[:, b * 64 + 63:b * 64 + 64], SNB[:, 128:192], ALU.mult, ALU.add)
                else:
                    P.tt(S0[:, rnd * 8:rnd * 8 + nq, :], S0w[rnd],
                         SNB[:, 0:nq * 64].r("p (q i) -> p q i", q=nq), ALU.add)
            yield
        if not prompt:
            P.dma(o_swkv[l][:, hp], S0[:])

    def hp_gen(l, g, hp, S):
        N = g.N
        prompt = g.kind == "p"
        PV = Cur.PV
        Tt, TBs, AR, BK, VBF, PSB = S.T, S.TB, S.AR, S.BK, S.VBF, S.PSB
        hc = slice(hp * 128, (hp + 1) * 128)
        XR, XK, XV = Tt[2][:, :N], Tt[3][:, :N], Tt[4][:, :N]
        if prompt:
            stv_ = STp.v((slice(None), l, hp, slice(None)), (l, hp))
            spb_ = STpb.v((slice(None), hp, slice(None)), hp)
            if g.first:
                P.memset(stv_, 0.0)
                P.memset(spb_, 0.0)
            else:
                P.copy(spb_, stv_, eng="act")
        for i, (q, xs) in enumerate(((hp, XR), (8 + hp, XK), (16 + hp, XV))):
            w = w_next("in", l, 1024 + q * 128, 128)
            ps = PSB[i % 2]
            proj(ps[:, :N], w, 128, N)
            shift_chunk(l, g, ps[:, :N], q, 128, xs, Tt[0], Tt[5])
            yield
        if stop == "Cb":
            return
        P.mm(PSB[0][:, :N], LWT[0:64, hc], TWL[0:64, :N])
        SG = Tt[5][:, :N]
        P.act(SG, PSB[0][:, :N], AF.Tanh, scale=0.5, bias=DV[:, W0H + hp:W0H + hp + 1])
        P.ts(SG, SG, 1.0, ALU.add)
        CS = Tt[7][:, :N]
        P.scan(CS, (rm_p if prompt else rm_s)[:, :N], SG, 0.0, ALU.mult, ALU.add)
        cs3 = CS.r("p (s t) -> p s t", s=g.nb)
        CSE = Tt[8][:, :N]
        P.tt(CSE.r("p (s t) -> p s t", s=g.nb), cs3[:, :, g.Lb - 1:g.Lb].bc([128, g.nb, g.Lb]), cs3, ALU.subtract)
        P.tt(SG, CS, SG, ALU.subtract)
        WI = Tt[9][:, :N]
        P.act(WI, CS, AF.Exp, scale=H0)
        P.act(CS, CS, AF.Exp, scale=-H0)
        P.act(SG, SG, AF.Exp, scale=-H0)
        P.act(CSE, CSE, AF.Exp, scale=-H0)
        WC, WM, WE = CS, SG, CSE
        yield
        P.mm(PSB[1][:, :N], LWT[64:128, hc], TWL[64:128, :N])
        THA = Tt[10][:, :N]
        P.act(THA, PSB[1][:, :N], AF.Tanh, scale=0.5, bias=DV[:, A0H + hp:A0H + hp + 1])
        ksq = TBs[0][:, :N]
        P.act(ksq, XK, AF.Square, scale=PV[:, KK + hp:KK + hp + 1])
        P.mm(PSB[1][:, :N], ONEB[:], ksq)
        NR = Tt[0][:, :N]
        P.act(NR, PSB[1][:, :N], AF.Ln, bias=1e-24)
        P.act(NR, NR, AF.Exp, scale=-0.5)
        KKN = Tt[6][:, :N]
        P.stt(KKN, XK, PV[:, KK + hp:KK + hp + 1], NR, ALU.mult, ALU.mult)
        yield
        nbk = g.nblk
        ar3a = AR[:, 0:nbk, 0:64]
        ar3r = AR[:, 0:nbk, 64:128]
        bk3b = BK[:, 0:nbk, 0:64]
        bk3k = BK[:, 0:nbk, 64:128]

        def b3(v):
            return v.r("p (b t) -> p b t", b=nbk)
        P.stt(ar3a, b3(KKN), -1.0, b3(WM), ALU.mult, ALU.mult)
        P.tt(ar3r, b3(XR), b3(WC), ALU.mult)
        B2 = Tt[5][:, :N]
        P.stt(B2, THA, 1.0, KKN, ALU.add, ALU.mult)
        P.stt(bk3b, b3(B2), 0.5, b3(WI), ALU.mult, ALU.mult)
        BHF = VBF[1][:, :N]
        P.stt(BHF, B2, 0.5, WE, ALU.mult, ALU.mult)
        yield
        KF = Tt[5][:, :N]
        P.act(KF, THA, AF.Identity, scale=DV[:, KAH + hp:KAH + hp + 1], bias=DV[:, KAB + hp:KAB + hp + 1])
        P.tt(KF, KF, XK, ALU.mult)
        P.tt(bk3k, b3(KF), b3(WI), ALU.mult)
        KHF = VBF[2][:, :N]
        P.tt(KHF, KF, WE, ALU.mult)
        VB = VBF[0][:, :N]
        P.act(VB, XV, AF.Identity)
        rkb = TBs[1][:, :N]
        P.stt(rkb, XR, PV[:, RK + hp:RK + hp + 1], KF, ALU.mult, ALU.mult)
        P.mm(PSB[0][:, :N], ONEB[:], rkb)
        BON = Tt[9][:, :N]
        P.tt(BON, PSB[0][:, :N], XV, ALU.mult)
        P.mm(PSB[1][:, :N], G2A[:, hc], SGL[:, :N], start=True, stop=False)
        P.mm(PSB[1][:, :N], G2B[:, hc], SGL2[:, :N], start=False, stop=True)
        GP = Tt[10][:, :N]
        P.act(GP, PSB[1][:, :N], AF.Identity)
        yield
        for _ in wkv(l, g, hp, S, VB, BHF, KHF, WC):
            yield
        if stop in ("W1", "W2"):
            return
        YS = Tt[5][:, :N]
        P.act(YS, PSB[3][:, :N], AF.Identity)
        if stop == "C1" and hp == 0:
            dumpv("YS", YS, [128, N])
            return
        yb_, y2_ = TBs[0][:, :N], TBs[1][:, :N]
        P.act(yb_, YS, AF.Identity)
        P.act(y2_, YS, AF.Square)
        P.mm(PSB[0][:, :N], ONEB64[:], yb_)
        P.mm(PSB[1][:, :N], ONEB64[:], y2_)
        yield
        VR = Tt[6][:, :N]
        MS = Tt[7][:, :N]
        P.act(MS, PSB[0][:, :N], AF.Identity)
        P.tt(VR, MS, MS, ALU.mult)
        P.tt(VR, PSB[1][:, :N], VR, ALU.subtract)
        P.act(VR, VR, AF.Ln, bias=GN_EPS)
        P.act(VR, VR, AF.Exp, scale=-0.5)
        P.tt(YS, YS, MS, ALU.subtract)
        P.tt(YS, YS, VR, ALU.mult)
        P.act(YS, YS, AF.Identity, scale=PV[:, LNG + hp:LNG + hp + 1], bias=PV[:, LNB + hp:LNB + hp + 1])
        P.tt(YS, YS, BON, ALU.add)
        P.stt(BF.v((slice(None), 8 + hp, slice(0, N)), 8 + hp), YS, 0.5, GP, ALU.mult, ALU.mult)
        yield

    def run_skewed(fns, sets, lag):
        pending = list(fns)
        free = list(sets)
        active = []
        while pending or active:
            if pending and free and (not active or min(a[2] for a in active) >= lag):
                S = free.pop(0)
                active.append([pending.pop(0)(S), S, 0])
            for a in list(active):
                try:
                    next(a[0])
                    a[2] += 1
                except StopIteration:
                    active.remove(a)
                    free.append(a[1])

    def run_gens(gens):
        active = list(gens)
        while active:
            for gen in list(active):
                try:
                    next(gen)
                except StopIteration:
                    active.remove(gen)

    def run_pass(l, g):
        N, L, ti = g.N, g.L, g.ti
        prompt = g.kind == "p"
        PV = Cur.PV
        xc = [XT.v((slice(None), c, slice(0, N)), c) for c in range(8)]
        if not prompt:
            P.dma(STS[:], sts_d[l])
            P.dma(STF[:], stf_d[l])
        rstd = rmsnorm(xc, N)
        for c in range(8):
            P.stt(H[:, c, :N], xc[c], PV[:, NMG + c:NMG + c + 1], rstd, ALU.mult, ALU.mult)
        if stop == "A":
            return
        def conv_proj(cc):
            pa_, pb_ = PS[2 * (cc % 2)], PS[2 * (cc % 2) + 1]
            wa = w_next("in", l, cc * 128, 128)
            proj(pa_[:, :N], wa, 128, N)
            wb = w_next("in", l, 512 + cc * 128, 128)
            proj(pb_[:, :N], wb, 128, N)

        conv_proj(0)
        for cc in range(4):
            pa_, pb_ = PS[2 * (cc % 2)], PS[2 * (cc % 2) + 1]
            th, zah = T[4 + (cc % 2)][:, :N], T[6 + (cc % 2)][:, :N]
            P.act(th, pb_[:, :N], AF.Tanh, scale=0.5)
            P.act(zah, pa_[:, :N], AF.Identity, scale=0.5)
            if cc < 3:
                conv_proj(cc + 1)
            u3 = hv(UB[cc], 128, g, 30)
            if prompt:
                if g.first:
                    P.memset(u3[:, :, 0:30], 0.0)
                else:
                    P.copy(u3[:, :, 0:30], CHALO[:, l, cc, :].un(1), eng="act")
            else:
                P.dma(u3[:, :, 0:30], stc_d[l][:, cc])
            P.stt(u3[:, :, 30:30 + L], v3(th, g), 1.0, v3(zah, g), ALU.add, ALU.mult)
            a3 = v3(T[cc][:, :N], g)
            if prompt:
                ubf = T[8 + (cc % 2)][:, 0:272].bitcast(BF16)
                P.act(ubf[:, 0:30 + L], UB[cc][:, 0:30 + L], AF.Identity)
                for j in range(31):
                    dg = DIAG[j % 8]
                    P.ts(dg[:], IDENTB[:], PV[:, CDW + cc * 31 + j:CDW + cc * 31 + j + 1], ALU.mult)
                    P.mm(PS[4 + (cc % 2)][:, :N], dg[:], ubf[:, j:j + L], start=(j == 0), stop=(j == 30))
                P.act(T[cc][:, :N], PS[4 + (cc % 2)][:, :N], AF.Identity, bias=PV[:, CDB + cc:CDB + cc + 1])
            else:
                P.ts(a3, u3[:, :, 0:L], PV[:, CDW + cc * 31:CDW + cc * 31 + 1], ALU.mult,
                     PV[:, CDB + cc:CDB + cc + 1], ALU.add)
                for j in range(1, 31):
                    P.stt(a3, u3[:, :, j:j + L], PV[:, CDW + cc * 31 + j:CDW + cc * 31 + j + 1], a3, ALU.mult, ALU.add)
            cb_, c2_ = TB[0][:, :N], TB[1][:, :N]
            P.act(cb_, T[cc][:, :N], AF.Identity)
            P.act(c2_, T[cc][:, :N], AF.Square)
            P.mm(PS[6][:, :N], ONES[:], cb_, start=(cc == 0), stop=(cc == 3))
            P.mm(PS[7][:, :N], ONES[:], c2_, start=(cc == 0), stop=(cc == 3))
            if prompt:
                P.copy(CHALO[:, l, cc, :].un(1), u3[:, :, L:L + 30], eng="act")
            else:
                P.dma(o_sconv[l][:, cc], u3[:, :, 4:34])
        if prompt and g.last:
            P.dma(o_pconv[l], CHALO[:, l])
        mean, var = T[6][:, :N], T[7][:, :N]
        P.ts(mean, PS[6][:, :N], 1.0 / 512, ALU.mult)
        P.tt(var, mean, mean, ALU.mult)
        P.stt(var, PS[7][:, :N], 1.0 / 512, var, ALU.mult, ALU.subtract)
        P.act(var, var, AF.Ln, bias=LN_EPS)
        P.act(var, var, AF.Exp, scale=-0.5)
        for cc in range(4):
            a = T[cc][:, :N]
            P.tt(a, a, mean, ALU.subtract)
            P.tt(a, a, var, ALU.mult)
            P.act(a, a, AF.Identity, scale=DV[:, CLGH + cc:CLGH + cc + 1], bias=DV[:, CLBH + cc:CLBH + cc + 1])
            th = T[4 + (cc % 2)][:, :N]
            P.act(th, a, AF.Tanh)
            P.stt(CA[:, cc, :N], th, 1.0, a, ALU.add, ALU.mult)
        if stop == "B":
            dumpv("CA", CA[:, :, :N], [128, 4, N])
            return
        for q, M in ((24, 128), (25, 128), (26, 32)):
            w = w_next("in", l, 1024 + q * 128, 128)
            ps = PS[q % 2]
            proj(ps[:M, :N], w, M, N)
            xs = T[2][:M, :N]
            shift_chunk(l, g, ps[:M, :N], q, M, xs, T[0], T[1])
            if q == 24:
                P.act(TWL[0:64, :N], T[2][0:64, :N], AF.Tanh)
                P.act(TWL[64:128, :N], T[2][64:128, :N], AF.Identity)
            elif q == 25:
                P.act(T[3][:, :N], xs, AF.Tanh, scale=0.5)
                P.ts(SGL[:, :N], T[3][:, :N], 1.0, ALU.add)
            else:
                P.act(T[3][0:32, :N], xs, AF.Tanh, scale=0.5)
                P.ts(SGL2[0:32, :N], T[3][0:32, :N], 1.0, ALU.add)
        if stop == "Ca":
            return
        if prompt:
            run_skewed([(lambda S, hp=hp: hp_gen(l, g, hp, S)) for hp in range(8)], [SET0, SET1], HP_LAG)
        else:
            SS = HPS()
            SS.__dict__.update(SET0.__dict__)
            SS.PSB = PS[0:4]
            for hp in range(8):
                run_gens([hp_gen(l, g, hp, SS)])
        if prompt and g.last:
            P.dma(o_pshift[l], CSH[:, l, :, 0])
            P.dma(o_pwkv[l], STp[:, l])
        if not prompt:
            P.dma(o_sshift[l], STS[:])
        if stop in ("C", "C1", "W1", "W2", "Cb"):
            dumpv("YF", BF[:, 8:16, :N], [128, 8, N])
            dumpv("STp", STp[:, l], [128, 8, 64])
            return
        for m in range(8):
            o = 4 * (m % 2)
            wco = w_next("co", l, m * 128)
            for k in range(4):
                P.mm(PS[o][:, :N], wco[:, k, :], CA[:, k, :N], start=(k == 0), stop=(k == 3))
            wro = w_next("ro", l, m * 128)
            for k in range(8):
                P.mm(PS[o + 1][:, :N], wro[:, k, :], BF.v((slice(None), 8 + k, slice(0, N)), 8 + k), start=(k == 0), stop=(k == 7))
            wg1 = w_next("in", l, 4384 + m * 128, 128)
            proj(PS[o + 2][:, :N], wg1, 128, N)
            wg2 = w_next("in", l, 5408 + m * 128, 128)
            proj(PS[o + 3][:, :N], wg2, 128, N)
            t1, t2 = T[(m % 2) * 2][:, :N], T[(m % 2) * 2 + 1][:, :N]
            P.act(t1, PS[o + 2][:, :N], AF.Tanh, scale=0.5)
            P.act(t2, PS[o + 3][:, :N], AF.Tanh, scale=0.5)
            P.stt(t1, t1, 1.0, PS[o][:, :N], ALU.add, ALU.mult)
            P.stt(t2, t2, 1.0, PS[o + 1][:, :N], ALU.add, ALU.mult)
            P.tt(BF.v((slice(None), m, slice(0, N)), m), t1, t2, ALU.add)
        for mo in range(8):
            w = w_next("mx", l, mo * 128)
            ps = PS[mo % 2]
            for k in range(8):
                P.mm(ps[:, :N], w[:, k, :], BF.v((slice(None), k, slice(0, N)), k), start=(k == 0), stop=(k == 7))
            P.stt(xc[mo], ps[:, :N], 0.5, xc[mo], ALU.mult, ALU.add)
        if stop == "M":
            return
        rstd = rmsnorm(xc, N)
        for c in range(8):
            P.stt(H[:, c, :N], xc[c], PV[:, NFG + c:NFG + c + 1], rstd, ALU.mult, ALU.mult)
        for part in range(3):
            for jj in range(8):
                j = part * 8 + jj
                cus = []
                for half, q in ((0, j), (1, 24 + j)):
                    w = w_next("up", l, q * 128)
                    ps = PS[half + 2 * (jj % 2)]
                    proj(ps[:, :N], w, 128, N)
                    upb = T[half + 2 * (jj % 2)]
                    u3 = hv(upb, 128, g, 2)
                    if prompt:
                        if g.first:
                            P.memset(u3[:, :, 0:2], 0.0)
                        else:
                            P.copy(u3[:, :, 0:2], CFF[:, l, q, :].un(1), eng="act")
                    else:
                        P.copy(u3[:, :, 0:2], STF[:, q, :, :], eng="act")
                    P.act(u3[:, :, 2:2 + L], v3(ps[:, :N], g), AF.Identity)
                    cu = T[4 + half + 2 * (jj % 2)][:, :N]
                    c3 = v3(cu, g)
                    fw0 = FDW + q * 3
                    P.act(c3, v3(ps[:, :N], g), AF.Identity, scale=PV[:, fw0 + 2:fw0 + 3], bias=PV[:, FDB + q:FDB + q + 1])
                    P.stt(c3, u3[:, :, 1:1 + L], PV[:, fw0 + 1:fw0 + 2], c3, ALU.mult, ALU.add)
                    P.stt(c3, u3[:, :, 0:L], PV[:, fw0:fw0 + 1], c3, ALU.mult, ALU.add)
                    if prompt:
                        P.copy(CFF[:, l, q, :].un(1), u3[:, :, L:L + 2], eng="act")
                    else:
                        P.copy(STF[:, q, :, :], u3[:, :, L:L + 2], eng="act")
                    cus.append(cu)
                ga = T[8 + (jj % 2)][:, :N]
                P.act(ga, cus[0], AF.Gelu_apprx_tanh)
                P.tt(BF.v((slice(None), jj, slice(0, N)), jj), ga, cus[1], ALU.mult)
            for mo in range(8):
                w = w_next("dn", l, part, mo * 128)
                ps = PS[4 + (mo % 2)]
                for k in range(8):
                    P.mm(ps[:, :N], w[:, k, :], BF.v((slice(None), k, slice(0, N)), k), start=(k == 0), stop=(k == 7))
                P.tt(xc[mo], xc[mo], ps[:, :N], ALU.add)
        if prompt and g.last:
            P.dma(o_pffn[l], CFF[:, l])
        if not prompt:
            P.dma(o_sffn[l], STF[:])

    dbg_x = dout("dbg_xT", [128, 8, NTOK]) if (dump is not None and stop is not None) else None
    if dbg_x is not None:
        dumps["xT"] = [128, 8, NTOK]
    for g in tiles:
        N = g.N
        P.dma(XT[:, :, :N], xT_d[:, :, g.t0:g.t0 + N])
        for l in range(depth):
            layer_setup(l)
            run_pass(l, g)
        xc = [XT.v((slice(None), c, slice(0, N)), c) for c in range(8)]
        if stop is None:
            rstd = rmsnorm(xc, N)
            for c in range(8):
                yo = T[c % 4][:, :N]
                P.stt(yo, xc[c], PVF[:, c:c + 1], rstd, ALU.mult, ALU.mult)
                P.dma(o_y[:, c, g.t0:g.t0 + N], yo)
        elif dbg_x is not None:
            P.dma(dbg_x[:, :, g.t0:g.t0 + N], XT[:, :, :N])
    P.finish()
    if not dry:
        P.emit()
    return nc, P, dumps


def _pm(v):
    return np.ascontiguousarray(v.reshape(-1, 128).T)


def pack_params(inp):
    pv = np.zeros((DEPTH, 128, NPV), np.float32)
    for l in range(DEPTH):
        pv[l, :, NMG:NMG + 8] = _pm(inp["norm_mix_g"][l])
        pv[l, :, NFG:NFG + 8] = _pm(inp["norm_ffn_g"][l])
        cw = inp["conv_dw_w"][l]
        pv[l, :, CDW:CDW + 124] = cw.reshape(31, 4, 128).transpose(2, 1, 0).reshape(128, 124)
        pv[l, :, CDB:CDB + 4] = _pm(inp["conv_dw_b"][l])
        pv[l, :, CLG:CLG + 4] = _pm(inp["conv_ln_g"][l])
        pv[l, :, CLB:CLB + 4] = _pm(inp["conv_ln_b"][l])
        mu = np.zeros(27 * 128, np.float32)
        mu[:3360] = inp["rw_mu"][l]
        pv[l, :, MU:MU + 27] = _pm(mu)
        pv[l, :, W0:W0 + 8] = _pm(inp["rw_w0"][l])
        pv[l, :, A0:A0 + 8] = _pm(inp["rw_a0"][l])
        pv[l, :, KK:KK + 8] = _pm(inp["rw_k_k"][l])
        pv[l, :, KA:KA + 8] = _pm(inp["rw_k_a"][l])
        pv[l, :, RK:RK + 8] = _pm(inp["rw_r_k"][l].reshape(-1))
        pv[l, :, LNG:LNG + 8] = _pm(inp["rw_ln_g"][l])
        pv[l, :, LNB:LNB + 8] = _pm(inp["rw_ln_b"][l])
        fw_ = inp["ffn_dw_w"][l]
        pv[l, :, FDW:FDW + 144] = fw_.reshape(3, 48, 128).transpose(2, 1, 0).reshape(128, 144)
        pv[l, :, FDB:FDB + 48] = _pm(inp["ffn_dw_b"][l])
    return pv


def make_in_maps(inp, cores=None):
    cst, cstb = make_consts()
    pv = pack_params(inp)
    pvf = _pm(inp["norm_final_g"])
    lw = np.ascontiguousarray(np.concatenate([inp["rw_w2"], inp["rw_a2"]], axis=1))
    L_ = DEPTH
    wi = inp["w_in"]
    w_in_u = np.zeros((L_, 51, 128, 8, 128), np.float32)
    w_in_u[:, 0:34] = wi[:, :, 0:4352].reshape(L_, 8, 128, 34, 128).transpose(0, 3, 2, 1, 4)
    w_in_u[:, 34, :, :, 0:32] = wi[:, :, 4352:4384].reshape(L_, 8, 128, 32).transpose(0, 2, 1, 3)
    w_in_u[:, 35:51] = wi[:, :, 4384:6432].reshape(L_, 8, 128, 16, 128).transpose(0, 3, 2, 1, 4)

    def units(w, K):
        n = w.shape[2] // 128
        return np.ascontiguousarray(w.reshape(L_, K, 128, n, 128).transpose(0, 3, 2, 1, 4)).reshape(L_, n, 128, K * 128)
    w_dn_u = np.ascontiguousarray(inp["w_down"].reshape(L_, 3, 8, 128, 8, 128).transpose(0, 1, 4, 3, 2, 5)).reshape(L_, 24, 128, 1024)
    shared = dict(cst=cst, cstb=cstb, pv=pv, pvf=pvf, lw=lw, g2=inp["rw_g2"],
                  w_in_u=w_in_u.reshape(L_, 51, 128, 1024),
                  w_co_u=units(inp["w_conv_out"], 4), w_ro_u=units(inp["w_rw_out"], 8),
                  w_mx_u=units(inp["w_mix_out"], 8), w_up_u=units(inp["w_up"], 8), w_dn_u=w_dn_u)
    maps = []
    for c in (range(NCORE) if cores is None else cores):
        sb = slice(16 * c, 16 * c + 16)
        xs = np.concatenate([inp["x_prompt"][c], inp["x_sample"][sb].reshape(64, D)], axis=0)
        xT = np.ascontiguousarray(xs.T.reshape(8, 128, NTOK).transpose(1, 0, 2))
        stc = np.ascontiguousarray(inp["state_conv"][:, sb].reshape(DEPTH, 16, 30, 4, 128).transpose(0, 4, 3, 1, 2))
        ss = np.zeros((DEPTH, 16, 27 * 128), np.float32)
        ss[:, :, :3360] = inp["state_shift"][:, sb]
        sts = np.ascontiguousarray(ss.reshape(DEPTH, 16, 27, 128).transpose(0, 3, 2, 1))
        sw = inp["state_wkv"][:, sb].reshape(DEPTH, 16, 8, 2, 64, 64)
        stw = np.ascontiguousarray(sw.transpose(0, 3, 5, 2, 1, 4).reshape(DEPTH, 128, 8, 16, 64))
        stf = np.ascontiguousarray(inp["state_ffn"][:, sb].reshape(DEPTH, 16, 2, 48, 128).transpose(0, 4, 3, 1, 2))
        m = dict(shared)
        m.update(xT=xT, stc=stc, sts=sts, stw=stw, stf=stf)
        maps.append(m)
    return maps


def assemble(results):
    L_ = DEPTH
    y_prompt = np.zeros((8, SEQ, D), np.float32)
    y_sample = np.zeros((128, 4, D), np.float32)
    p_conv = np.zeros((L_, 8, 30, 512), np.float32)
    p_shift = np.zeros((L_, 8, 3360), np.float32)
    p_wkv = np.zeros((L_, 8, 16, 64, 64), np.float32)
    p_ffn = np.zeros((L_, 8, 2, 6144), np.float32)
    s_conv = np.zeros((L_, 128, 30, 512), np.float32)
    s_shift = np.zeros((L_, 128, 3360), np.float32)
    s_wkv = np.zeros((L_, 128, 16, 64, 64), np.float32)
    s_ffn = np.zeros((L_, 128, 2, 6144), np.float32)
    for c, r in enumerate(results):
        sb = slice(16 * c, 16 * c + 16)
        yT = r["o_y"].transpose(1, 0, 2).reshape(D, NTOK)
        y_prompt[c] = yT[:, :SEQ].T
        y_sample[sb] = yT[:, SEQ:].T.reshape(16, 4, D)
        p_conv[:, c] = r["o_pconv"].transpose(0, 3, 2, 1).reshape(L_, 30, 512)
        s_conv[:, sb] = r["o_sconv"].transpose(0, 3, 4, 2, 1).reshape(L_, 16, 30, 512)
        p_shift[:, c] = r["o_pshift"].transpose(0, 2, 1).reshape(L_, 27 * 128)[:, :3360]
        s_shift[:, sb] = r["o_sshift"].transpose(0, 3, 2, 1).reshape(L_, 16, 27 * 128)[:, :, :3360]
        pw = r["o_pwkv"].reshape(L_, 2, 64, 8, 64)
        p_wkv[:, c] = pw.transpose(0, 3, 1, 4, 2).reshape(L_, 16, 64, 64)
        sw = r["o_swkv"].reshape(L_, 2, 64, 8, 16, 64)
        s_wkv[:, sb] = sw.transpose(0, 4, 3, 1, 5, 2).reshape(L_, 16, 16, 64, 64)
        p_ffn[:, c] = r["o_pffn"].transpose(0, 3, 2, 1).reshape(L_, 2, 6144)
        s_ffn[:, sb] = r["o_sffn"].transpose(0, 3, 4, 2, 1).reshape(L_, 16, 2, 6144)
    return (y_prompt, y_sample, p_conv, p_shift, p_wkv, p_ffn, s_conv, s_shift, s_wkv, s_ffn)


def kernel(**inputs):
    inp = {k: np.asarray(v) for k, v in inputs.items()}
    nc, P, _ = build()
    maps = make_in_maps(inp)
    res = run_bass_kernel_spmd(nc, maps, core_ids=list(range(NCORE)))
    return assemble(res.results)
```

```python
import numpy as np
import concourse.bass as bass
import concourse.mybir as mybir

F32 = mybir.dt.float32
BF16 = mybir.dt.bfloat16
AF = mybir.ActivationFunctionType
ALU = mybir.AluOpType


ATTACH_WAIT = 1
SAME_ENG_MASK = 7


class V:
    __slots__ = ("tile", "ap", "sub")

    def __init__(self, tile, ap, sub=None):
        self.tile = tile
        self.ap = ap
        self.sub = sub

    def __getitem__(self, idx):
        return V(self.tile, self.ap[idx], self.sub)

    def k(self, sub):
        return V(self.tile, self.ap, sub)

    def r(self, pat, **kw):
        return V(self.tile, self.ap.rearrange(pat, **kw), self.sub)

    def bc(self, shape):
        return V(self.tile, self.ap.broadcast_to(shape), self.sub)

    def un(self, axis):
        return V(self.tile, self.ap.unsqueeze(axis), self.sub)

    def bitcast(self, dt):
        return V(self.tile, self.ap.bitcast(dt), self.sub)


class Tile:
    def __init__(self, name, handle):
        self.name = name
        self.h = handle

    def __getitem__(self, idx):
        return V(self.name, self.h[idx], None)

    def v(self, idx, sub):
        return V(self.name, self.h[idx], sub)


class Op:
    __slots__ = ("id", "eng", "fn", "dma", "deps", "sem", "semval", "signal", "rank", "prewait")

    def __init__(self, id, eng, fn, dma):
        self.id = id
        self.eng = eng
        self.fn = fn
        self.dma = dma
        self.deps = {}
        self.sem = None
        self.semval = 0
        self.signal = False
        self.rank = 0
        self.prewait = None


class Prog:
    ENGS = ("pe", "act", "dve", "pool", "sp")

    def __init__(self, nc, n_dma_sems=40):
        self.nc = nc
        self.ops = []
        self.state = {}
        self.n_dma_sems = n_dma_sems
        self.sb_bytes = 0

    def sb(self, name, shape, dtype):
        h = self.nc.alloc_sbuf_tensor("sb_" + name, list(shape), dtype)
        n = 1
        for s in shape[1:]:
            n *= s
        self.sb_bytes += n * (4 if dtype == F32 else 2)
        return Tile(name, h)

    def ps(self, name, shape, dtype=F32):
        h = self.nc.alloc_psum_tensor("ps_" + name, list(shape), dtype)
        return Tile(name, h)

    def _collect(self, v, is_write, deps):
        if v is None or not isinstance(v, V) or v.tile is None:
            return
        st = self.state.setdefault(v.tile, {})
        subs = list(st.keys()) if v.sub is None else [s for s in (v.sub, None) if s in st]
        for s in subs:
            w, rs = st[s]
            if w is not None:
                deps[w] = deps.get(w, 0) | (2 if is_write else 1)
            if is_write:
                for r in rs:
                    deps[r] = deps.get(r, 0) | 4

    def _update(self, v, is_write, opid):
        if v is None or not isinstance(v, V) or v.tile is None:
            return
        st = self.state.setdefault(v.tile, {})
        if is_write:
            if v.sub is None:
                st.clear()
            st[v.sub] = [opid, []]
        else:
            if v.sub not in st:
                st[v.sub] = [None, []]
            st[v.sub][1].append(opid)

    def add(self, eng, fn, reads, writes, dma=False):
        op = Op(len(self.ops), eng, fn, dma)
        deps = {}
        for v in reads:
            self._collect(v, False, deps)
        for v in writes:
            self._collect(v, True, deps)
        for v in reads:
            self._update(v, False, op.id)
        for v in writes:
            self._update(v, True, op.id)
        deps.pop(op.id, None)
        op.deps = deps
        self.ops.append(op)
        return op

    @staticmethod
    def _a(x):
        return x.ap if isinstance(x, V) else x

    def mm(self, out, lhsT, rhs, start=True, stop=True):
        a = self._a
        return self.add("pe", lambda e: e.matmul(a(out), a(lhsT), a(rhs), start=start, stop=stop),
                        [lhsT, rhs], [out])

    def tr(self, out, in_, ident):
        a = self._a
        return self.add("pe", lambda e: e.transpose(a(out), a(in_), a(ident)), [in_, ident], [out])

    def act(self, out, in_, func, scale=1.0, bias=0.0, eng="act"):
        a = self._a
        return self.add(eng, lambda e: e.activation(a(out), a(in_), func, bias=a(bias), scale=a(scale)),
                        [in_, scale, bias], [out])

    def tt(self, out, in0, in1, op, eng="dve"):
        a = self._a
        return self.add(eng, lambda e: e.tensor_tensor(a(out), a(in0), a(in1), op), [in0, in1], [out])

    def ts(self, out, in0, s1, op0, s2=None, op1=None, eng="dve"):
        a = self._a
        if op1 is None:
            return self.add(eng, lambda e: e.tensor_scalar(a(out), a(in0), a(s1), None, op0), [in0, s1], [out])
        return self.add(eng, lambda e: e.tensor_scalar(a(out), a(in0), a(s1), a(s2), op0, op1),
                        [in0, s1, s2], [out])

    def stt(self, out, in0, scalar, in1, op0, op1):
        a = self._a
        return self.add("dve", lambda e: e.scalar_tensor_tensor(a(out), a(in0), a(scalar), a(in1), op0, op1),
                        [in0, scalar, in1], [out])

    def scan(self, out, d0, d1, init, op0, op1):
        a = self._a
        return self.add("dve", lambda e: e.tensor_tensor_scan(a(out), a(d0), a(d1), a(init), op0, op1),
                        [d0, d1, init], [out])

    def copy(self, out, in_, eng="dve"):
        a = self._a
        if eng == "act":
            return self.add("act", lambda e: e.copy(a(out), a(in_)), [in_], [out])
        return self.add(eng, lambda e: e.tensor_copy(a(out), a(in_)), [in_], [out])

    def recip(self, out, in_):
        a = self._a
        return self.add("dve", lambda e: e.reciprocal(a(out), a(in_)), [in_], [out])

    def memset(self, out, val, eng="dve"):
        a = self._a
        return self.add(eng, lambda e: e.memset(a(out), val), [], [out])

    def dma(self, out, in_, eng="sp"):
        a = self._a
        return self.add(eng, lambda e: e.dma_start(out=a(out), in_=a(in_)), [in_], [out], dma=True)

    def finish(self, eng="sp"):
        op = Op(len(self.ops), eng, lambda e: None, False)
        op.deps = {o.id: 1 for o in self.ops if o.dma}
        self.ops.append(op)

    def emit(self):
        nc = self.nc
        ops = self.ops
        from contextlib import ExitStack
        with ExitStack() as es:
            eng_sem = {e: es.enter_context(nc.semaphore("s_" + e)) for e in self.ENGS}
            dma_sems = [es.enter_context(nc.semaphore("d%d" % i)) for i in range(self.n_dma_sems)]
            nsw = (self.n_dma_sems * 3) // 5
            pools = {True: list(range(0, nsw)), False: list(range(nsw, self.n_dma_sems))}
            dma_cnt = [0] * self.n_dma_sems
            dma_last = [None] * self.n_dma_sems
            kk_ = {True: 0, False: 0}
            for op in ops:
                if op.dma:
                    sw = op.eng == "pool"
                    pl = pools[sw]
                    i = pl[kk_[sw] % len(pl)]
                    kk_[sw] += 1
                    op.prewait = dma_last[i]
                    dma_cnt[i] += 16
                    op.sem = dma_sems[i]
                    op.semval = dma_cnt[i]
                    dma_last[i] = op.id
            for op in ops:
                for d, kind in op.deps.items():
                    dop = ops[d]
                    if dop.dma:
                        continue
                    if dop.eng == op.eng:
                        if op.eng == "pe":
                            continue
                        if not (kind & SAME_ENG_MASK):
                            continue
                    dop.signal = True
            rank = {e: 0 for e in self.ENGS}
            for op in ops:
                if not op.dma and op.signal:
                    rank[op.eng] += 1
                    op.rank = rank[op.eng]
            clocks = [None] * len(ops)
            eng_clock = {e: {} for e in self.ENGS}
            dma_known = {e: set() for e in self.ENGS}
            plans = {e: [] for e in self.ENGS}
            for op in ops:
                ck = eng_clock[op.eng]
                waits = []
                deplist = list(op.deps.items())
                if op.prewait is not None:
                    deplist.append((op.prewait, 3))
                for d, kind in sorted(deplist):
                    dop = ops[d]
                    if dop.dma:
                        if d in dma_known[op.eng]:
                            continue
                        waits.append((dop.sem, dop.semval))
                        dma_known[op.eng].add(d)
                    else:
                        if dop.eng == op.eng and (op.eng == "pe" or not (kind & SAME_ENG_MASK)):
                            continue
                        key = dop.eng
                        if ck.get(key, 0) >= dop.rank:
                            continue
                        waits.append((eng_sem[dop.eng], dop.rank))
                    for kk, vv in clocks[d].items():
                        if ck.get(kk, 0) < vv:
                            ck[kk] = vv
                best = {}
                for s, val in waits:
                    kid = id(s)
                    if kid not in best or best[kid][1] < val:
                        best[kid] = (s, val)
                myck = dict(ck)
                if (not op.dma) and op.signal:
                    myck[op.eng] = op.rank
                clocks[op.id] = myck
                plans[op.eng].append((op, list(best.values())))
            self.n_waits = sum(len(w) for e in plans for _, w in plans[e])
            handles = {"pe": "tensor", "act": "scalar", "dve": "vector", "pool": "gpsimd", "sp": "sync"}
            with nc.Block() as block:
                def run(eng_name):
                    def body(e):
                        for op, waits in plans[eng_name]:
                            attach = None
                            if ATTACH_WAIT and waits and not op.dma and op.eng != "sp":
                                attach = waits[-1]
                                waits = waits[:-1]
                            for s, val in waits:
                                e.wait_ge(s, val)
                            ins = op.fn(e)
                            if ins is None:
                                if attach is not None:
                                    e.wait_ge(*attach)
                                continue
                            if attach is not None:
                                ins._wait_ge(attach[0], attach[1])
                            if op.dma:
                                ins.then_inc(op.sem, 16)
                            elif op.signal:
                                ins.then_inc(eng_sem[eng_name], 1)
                    return body
                block.tensor(run("pe"))
                block.scalar(run("act"))
                block.vector(run("dve"))
                block.gpsimd(run("pool"))
                block.sync(run("sp"))


from concourse.bass_utils import run_bass_kernel_spmd

DEPTH = 4
NCORE = 8
HP_LAG = 0
D = 1024
SEQ = 2048
NTOK = 2112
RMS_EPS = 1e-6
LN_EPS = 1e-5
GN_EPS = 64e-5
H0 = 0.5 * float(np.exp(-0.5))

NMG, NFG, CDW, CDB, CLG, CLB, MU, W0, A0, KK, KA, RK, LNG, LNB, FDW, FDB, NPV = (
    0, 8, 16, 140, 144, 148, 152, 179, 187, 195, 203, 211, 219, 227, 235, 379, 427)
CLGH, CLBH, W0H, A0H, KAH, KAB, OMM, NDV = 0, 4, 8, 16, 24, 32, 40, 67
C_ID, C_IDB, NCST = 0, 128, 192
C_MP, C_MS, C_RMP, C_RMS, C_SEQM, C_SEQMT, NCSTB = 0, 320, 640, 1152, 1216, 2240, 2256


def make_consts():
    import ml_dtypes
    c0 = np.zeros((128, NCST), np.float32)
    c0[:, C_ID:C_ID + 128] = np.eye(128, dtype=np.float32)
    p = np.arange(128) % 64
    c0[:, C_IDB:C_IDB + 64] = (p[:, None] == np.arange(64)[None, :])
    c = np.zeros((128, NCSTB), np.float32)
    t = np.arange(64)[None, :]
    s = p[:, None]
    for base, same in ((C_MP, np.ones((128, 64), bool)), (C_MS, (s // 4) == (t // 4))):
        su = (s < t) & same
        iu = (s <= t) & same
        sl = (s > t) & same
        c[:, base:base + 320] = np.concatenate([su, iu, su, iu, sl], axis=1)
    c[:, C_RMP:C_RMP + 512] = (np.arange(512) % 64 != 0)[None, :]
    c[:, C_RMS:C_RMS + 64] = (np.arange(64) % 4 != 0)[None, :]
    sm = (np.arange(16)[:, None] == (np.arange(64) // 4)[None, :]).astype(np.float32)
    c[:, C_SEQM:C_SEQM + 1024] = sm.reshape(1, 1024)
    c[:, C_SEQMT:C_SEQMT + 16] = ((p // 4)[:, None] == np.arange(16)[None, :])
    return c0, c.astype(ml_dtypes.bfloat16)


class Geo:
    def __init__(self, kind, t0, ti, first=False, last=False):
        self.kind, self.t0, self.ti, self.first, self.last = kind, t0, ti, first, last
        if kind == "p":
            self.N, self.nseq, self.L, self.nb, self.Lb, self.nblk = 512, 1, 512, 8, 64, 8
        else:
            self.N, self.nseq, self.L, self.nb, self.Lb, self.nblk = 64, 16, 4, 16, 4, 1


def build(depth=DEPTH, tiles=None, dump=None, stop=None):
    sched = []
    _build(depth, tiles, None, stop, sched, True)
    return _build(depth, tiles, dump, stop, sched, False)


def _build(depth, tiles, dump, stop, sched, dry):
    nc = bass.Bass("TRN2", target_bir_lowering=False)
    P = Prog(nc)
    L_ = DEPTH

    def din(name, shape, dt=F32):
        return nc.dram_tensor(name, list(shape), dt, kind="ExternalInput").ap()

    def dout(name, shape):
        return nc.dram_tensor(name, list(shape), F32, kind="ExternalOutput").ap()

    xT_d = din("xT", [128, 8, NTOK])
    cst_d = din("cst", [128, NCST])
    cstb_d = din("cstb", [128, NCSTB], BF16)
    pv_d = din("pv", [L_, 128, NPV])
    pvf_d = din("pvf", [128, 8])
    stc_d = din("stc", [L_, 128, 4, 16, 30])
    sts_d = din("sts", [L_, 128, 27, 16])
    stw_d = din("stw", [L_, 128, 8, 16, 64])
    stf_d = din("stf", [L_, 128, 48, 16, 2])
    w_in_d = din("w_in_u", [L_, 51, 128, 1024])
    wco_d = din("w_co_u", [L_, 8, 128, 512])
    lw_d = din("lw", [L_, 128, 1024])
    g2_d = din("g2", [L_, 160, 1024])
    wro_d = din("w_ro_u", [L_, 8, 128, 1024])
    wmx_d = din("w_mx_u", [L_, 8, 128, 1024])
    wup_d = din("w_up_u", [L_, 48, 128, 1024])
    wdn_d = din("w_dn_u", [L_, 24, 128, 1024])

    o_y = dout("o_y", [128, 8, NTOK])
    o_pconv = dout("o_pconv", [L_, 128, 4, 30])
    o_sconv = dout("o_sconv", [L_, 128, 4, 16, 30])
    o_pshift = dout("o_pshift", [L_, 128, 27])
    o_sshift = dout("o_sshift", [L_, 128, 27, 16])
    o_pwkv = dout("o_pwkv", [L_, 128, 8, 64])
    o_swkv = dout("o_swkv", [L_, 128, 8, 16, 64])
    o_pffn = dout("o_pffn", [L_, 128, 48, 2])
    o_sffn = dout("o_sffn", [L_, 128, 48, 16, 2])

    if tiles is None:
        tiles = [Geo("p", 512 * i, i, first=(i == 0), last=(i == 3)) for i in range(4)] + [Geo("s", 2048, 4)]

    XT = P.sb("XT", [128, 8, 512], F32)
    H = P.sb("H", [128, 8, 512], BF16)
    NSLOT = 8
    WB = [P.sb("WB%d" % i, [128, 1024], BF16) for i in range(NSLOT)]
    UB = [P.sb("UB%d" % i, [128, 544], F32) for i in range(4)]
    CA = P.sb("CA", [128, 4, 512], BF16)
    BF = P.sb("BF", [128, 16, 512], BF16)
    LWT = P.sb("LWT", [128, 1024], BF16)
    G2A = P.sb("G2A", [128, 1024], BF16)
    G2B = P.sb("G2B", [128, 1024], BF16)
    TWL = P.sb("TWL", [128, 512], BF16)
    SGL = P.sb("SGL", [128, 512], BF16)
    SGL2 = P.sb("SGL2", [128, 512], BF16)
    T = [P.sb("T%d" % i, [128, 544], F32) for i in range(12)]
    TB = [P.sb("TBh%d" % i, [128, 512], BF16) for i in range(2)]

    class HPS:
        pass

    def mk_set(i, Tl, psb):
        S = HPS()
        S.T = Tl
        S.TB = [P.sb("hTB%d_%d" % (i, j), [128, 512], BF16) for j in range(2)] if i else TB
        S.AR = P.sb("AR%d" % i, [128, 8, 128], BF16)
        S.BK = P.sb("BK%d" % i, [128, 8, 128], BF16)
        S.TOK = P.sb("TOK%d" % i, [128, 8, 3, 64], BF16)
        S.AM = P.sb("AM%d" % i, [128, 8, 320], BF16)
        S.CH = [P.sb("CH%d_%d" % (i, j), [128, 8, 64], BF16) for j in range(4)]
        S.TTa = P.sb("TTa%d" % i, [128, 8, 64], BF16)
        S.TTb = P.sb("TTb%d" % i, [128, 8, 64], BF16)
        S.RHSb = P.sb("RHSb%d" % i, [128, 64], BF16)
        S.Ub = P.sb("Ub%d" % i, [128, 64], BF16)
        S.VBF = [P.sb("VBF%d_%d" % (i, j), [128, 512], BF16) for j in range(3)]
        S.PSB = psb
        return S

    PS = [P.ps("PS%d" % i, [128, 512], F32) for i in range(8)]
    T1 = {j: P.sb("hT1_%d" % j, [128, 544], F32) for j in (0, 2, 3, 4, 5, 6, 7, 8, 9, 10)}
    SET0 = mk_set(0, {j: T[j] for j in range(12)}, PS[0:4])
    SET1 = mk_set(1, T1, PS[4:8])
    SETS = mk_set
    STpb = P.sb("STpb", [128, 8, 64], BF16)
    S0 = P.sb("S0", [128, 16, 64], F32)
    S0b = P.sb("S0b", [128, 16, 64], BF16)
    STS = P.sb("STS", [128, 27, 16], F32)
    STF = P.sb("STF", [128, 48, 16, 2], F32)
    CHALO = P.sb("CHALO", [128, L_, 4, 30], F32)
    CSH = P.sb("CSH", [128, L_, 27, 1], F32)
    STp = P.sb("STp", [128, L_, 8, 64], F32)
    CFF = P.sb("CFF", [128, L_, 48, 2], F32)
    PVs = [P.sb("PV%d" % i, [128, NPV], F32) for i in range(1)]
    DV = P.sb("DV", [128, NDV], F32)
    PVF = P.sb("PVF", [128, 8], F32)
    CST = P.sb("CST", [128, NCST], F32)
    CSTB = P.sb("CSTB", [128, NCSTB], BF16)
    IDBb = P.sb("IDBb", [128, 64], BF16)
    IDENTB = P.sb("IDENTB", [128, 128], BF16)
    DIAG = [P.sb("DIAG%d" % i, [128, 128], BF16) for i in range(8)]
    ONES = P.sb("ONES", [128, 128], BF16)
    ONEB = P.sb("ONEB", [128, 128], BF16)
    ONEB64 = P.sb("ONEB64", [128, 128], BF16)

    mask_p = CSTB[:, C_MP:C_MP + 320]
    mask_s = CSTB[:, C_MS:C_MS + 320]
    rm_p = CSTB[:, C_RMP:C_RMP + 512]
    rm_s = CSTB[:, C_RMS:C_RMS + 64]
    seqm = CSTB[:, C_SEQM:C_SEQM + 1024].r("p (q t) -> p q t", q=16)
    seqmt = CSTB[:, C_SEQMT:C_SEQMT + 16]

    dumps = {}

    def dumpv(name, view, shape):
        if dump is None or name not in dump:
            return
        d = dout("dbg_" + name, shape)
        P.dma(d, view, eng="pool")
        dumps[name] = shape

    def wsrc(kind, l, a, b=None):
        if kind == "in":
            idx = a // 128 if a < 4352 else (34 if a == 4352 else 35 + (a - 4384) // 128)
            return w_in_d[l, idx].rearrange("p (k m) -> p k m", k=8), 8, 128
        if kind == "co":
            return wco_d[l, a // 128].rearrange("p (k m) -> p k m", k=4), 4, 128
        if kind == "ro":
            return wro_d[l, a // 128].rearrange("p (k m) -> p k m", k=8), 8, 128
        if kind == "mx":
            return wmx_d[l, a // 128].rearrange("p (k m) -> p k m", k=8), 8, 128
        if kind == "up":
            return wup_d[l, a // 128].rearrange("p (k m) -> p k m", k=8), 8, 128
        if kind == "dn":
            return wdn_d[l, a * 8 + b // 128].rearrange("p (k m) -> p k m", k=8), 8, 128
        raise ValueError(kind)

    class WS:
        issued = 0
        taken = 0

    def w_issue():
        i = WS.issued
        if i >= len(sched):
            return
        src, K, M = wsrc(*sched[i])
        dst = WB[i % NSLOT][:, 0:K * M].r("p (k m) -> p k m", k=K)
        P.dma(dst, src, eng="pool")
        WS.issued += 1

    def w_next(*unit):
        i = WS.taken
        if dry:
            sched.append(unit)
        else:
            assert sched[i] == unit, (i, sched[i], unit)
            while WS.issued < min(len(sched), i + NSLOT - 1):
                w_issue()
        _, K, M = wsrc(*unit)
        WS.taken += 1
        return WB[i % NSLOT][:, 0:K * M].r("p (k m) -> p k m", k=K)

    def v3(v, g):
        return v.r("p (s t) -> p s t", s=g.nseq)

    def hv(tile, M, g, h):
        return tile[:M, 0:g.nseq * (h + g.L)].r("p (s t) -> p s t", s=g.nseq)

    def proj(ps_view, w, M, N):
        for k in range(8):
            P.mm(ps_view, w[:, k, 0:M], H[:, k, :N], start=(k == 0), stop=(k == 7))

    def rmsnorm(xc, N):
        for c in range(8):
            sq = TB[c % 2][:, :N]
            P.act(sq, xc[c], AF.Square)
            P.mm(PS[2][:, :N], ONES[:], sq, start=(c == 0), stop=(c == 7))
        t = T[10][:, :N]
        P.act(t, PS[2][:, :N], AF.Ln, scale=1.0 / D, bias=RMS_EPS)
        P.act(t, t, AF.Exp, scale=-0.5)
        return t

    P.dma(CST[:], cst_d)
    P.dma(CSTB[:], cstb_d)
    P.dma(PVF[:], pvf_d)
    P.act(IDBb[:], CST[:, C_IDB:C_IDB + 64], AF.Identity)
    P.act(IDENTB[:], CST[:, C_ID:C_ID + 128], AF.Identity)
    P.memset(ONES[:], 1.0)
    P.memset(CSH[:], 0.0)
    P.memset(G2B[:], 0.0)
    P.memset(SGL2[:], 0.0)
    P.memset(ONEB[:], 0.0)
    P.memset(ONEB[0:64, 0:64], 1.0)
    P.memset(ONEB[64:128, 64:128], 1.0)
    P.act(ONEB64[:], ONEB[:], AF.Identity, scale=1.0 / 64)

    class Cur:
        PV = None
        npass = 0

    def layer_setup(l):
        PV = PVs[0]
        Cur.PV = PV
        Cur.npass += 1
        P.dma(PV[:], pv_d[l])
        P.dma(LWT[:], lw_d[l], eng="pool")
        P.dma(G2A[:], g2_d[l][0:128, :], eng="pool")
        P.dma(G2B[0:32, :], g2_d[l][128:160, :], eng="pool")
        P.ts(DV[:, CLGH:CLGH + 4], PV[:, CLG:CLG + 4], 0.5, ALU.mult)
        P.ts(DV[:, CLBH:CLBH + 4], PV[:, CLB:CLB + 4], 0.5, ALU.mult)
        P.ts(DV[:, W0H:W0H + 8], PV[:, W0:W0 + 8], 0.5, ALU.mult)
        P.ts(DV[:, A0H:A0H + 8], PV[:, A0:A0 + 8], 0.5, ALU.mult)
        P.ts(DV[:, KAH:KAH + 8], PV[:, KA:KA + 8], 0.5, ALU.mult)
        P.ts(DV[:, KAB:KAB + 8], PV[:, KA:KA + 8], -0.5, ALU.mult, 1.0, ALU.add)
        P.ts(DV[:, OMM:OMM + 27], PV[:, MU:MU + 27], -1.0, ALU.mult, 1.0, ALU.add)

    def shift_chunk(l, g, ps_view, q, M, out_xs, zs_tile, d_tile):
        N, L = g.N, g.L
        PV = Cur.PV
        z3 = hv(zs_tile, M, g, 1)
        if g.kind == "p":
            if g.first:
                P.memset(z3[:, :, 0:1], 0.0)
            else:
                P.copy(z3[:, :, 0:1], CSH[:M, l, q:q + 1, :], eng="act")
        else:
            P.copy(z3[:, :, 0:1], STS[:M, q, :].un(2), eng="act")
        P.act(z3[:, :, 1:1 + L], v3(ps_view, g), AF.Identity)
        d3 = v3(d_tile[:M, :N], g)
        P.act(d3, v3(ps_view, g), AF.Identity, scale=DV[:M, OMM + q:OMM + q + 1])
        P.stt(v3(out_xs, g), z3[:, :, 0:L], PV[:M, MU + q:MU + q + 1], d3, ALU.mult, ALU.add)
        if g.kind == "p":
            P.copy(CSH[:M, l, q:q + 1, :], z3[:, :, L:L + 1], eng="act")
        else:
            P.copy(STS[:M, q, :].un(2), z3[:, :, L:L + 1], eng="act")

    def wkv(l, g, hp, S, XV, BHF, KHF, WC):
        N, nblk, Q = g.N, g.nblk, g.nseq
        HS = [slice(0, 64), slice(64, 128)]
        prompt = g.kind == "p"
        nlev = 5 if prompt else 1
        MASK = mask_p if prompt else mask_s
        AR, BK, TOK, AM, CH, TTa, TTb, RHSb, Ub, PSB = S.AR, S.BK, S.TOK, S.AM, S.CH, S.TTa, S.TTb, S.RHSb, S.Ub, S.PSB
        Tt = S.T
        for b in range(nblk):
            cb = slice(b * 64, (b + 1) * 64)
            pa = PSB[b % 2]
            ptb = PSB[2 + (b % 2)][:, 0:96].bitcast(BF16)
            for qi, src in enumerate((XV, BHF, KHF)):
                for hs in HS:
                    P.tr(ptb[hs, qi * 64:(qi + 1) * 64], src[hs, cb], IDENTB[hs, hs])
            for hs in HS:
                P.mm(pa[hs, 0:128], BK[hs, b, 0:64], AR[hs, b, :])
            for hs in HS:
                P.mm(pa[hs, 128:256], BK[hs, b, 64:128], AR[hs, b, :])
            for hs in HS:
                P.mm(pa[hs, 256:320], AR[hs, b, 0:64], BK[hs, b, 0:64])
            P.copy(TOK[:, b, :, :], ptb.r("p (q i) -> p q i", q=3), eng="act")
            P.tt(AM[:, b, :], pa[:, 0:320], MASK, ALU.mult)
            if b % 2 == 1:
                yield
        yield
        if stop == "W1":
            return
        P.tt(TTa[:, 0:nblk, :], AM[:, 0:nblk, 0:64], IDBb[:].un(1).bc([128, nblk, 64]), ALU.add)
        Xp = AM[:, :, 0:64]
        Pp = AM[:, :, 256:320]
        TTp, TTn = TTa, TTb
        for k in range(1, nlev + 1):
            Pk = CH[(k % 2) * 2]
            Xk = CH[(k % 2) * 2 + 1]
            for b in range(nblk):
                cb = slice(b * 64, (b + 1) * 64)
                for hs in HS:
                    P.mm(PSB[0][hs, cb], Xp[hs, b, :], Pp[hs, b, :])
                if k < nlev:
                    for hs in HS:
                        P.mm(PSB[1][hs, cb], Pp[hs, b, :], Xp[hs, b, :])
            P.copy(Pk[:, 0:nblk, :], PSB[0][:, 0:nblk * 64].r("p (b i) -> p b i", b=nblk), eng="act")
            if k < nlev:
                P.copy(Xk[:, 0:nblk, :], PSB[1][:, 0:nblk * 64].r("p (b i) -> p b i", b=nblk))
            yield
            for b in range(nblk):
                cb = slice(b * 64, (b + 1) * 64)
                for hs in HS:
                    P.mm(PSB[2][hs, cb], Pk[hs, b, :], TTp[hs, b, :])
            P.tt(TTn[:, 0:nblk, :], TTp[:, 0:nblk, :],
                 PSB[2][:, 0:nblk * 64].r("p (b i) -> p b i", b=nblk), ALU.add)
            yield
            Xp, Pp = Xk, Pk
            TTp, TTn = TTn, TTp
        TTf = TTp
        if stop == "W2":
            return
        if not prompt:
            def bfv(t):
                return t[:, 0:512].bitcast(BF16).r("p (q i) -> p q i", q=16)
            AMSK, RMSK, BHM, KHM = bfv(Tt[0]), bfv(Tt[1]), bfv(Tt[2]), bfv(Tt[3])
            S0w = [Tt[11][:, 0:512].r("p (q i) -> p q i", q=8), Tt[5][:, 0:512].r("p (q i) -> p q i", q=8)]
            P.dma(S0[:], stw_d[l][:, hp])
            P.act(S0b[:], S0[:], AF.Identity)
            P.tt(AMSK, AR[:, 0, 0:64].un(1).bc([128, 16, 64]), seqm, ALU.mult)
            P.tt(RMSK, AR[:, 0, 64:128].un(1).bc([128, 16, 64]), seqm, ALU.mult)
            P.tt(BHM, TOK[:, 0, 1, :].un(1).bc([128, 16, 64]), seqmt.un(2).bc([128, 16, 64]), ALU.mult)
            P.tt(KHM, TOK[:, 0, 2, :].un(1).bc([128, 16, 64]), seqmt.un(2).bc([128, 16, 64]), ALU.mult)
            wc3 = WC.r("p (s t) -> p s t", s=16)
            for rnd in range(2):
                P.tt(S0w[rnd], S0[:, rnd * 8:rnd * 8 + 8, :], wc3[:, rnd * 8:rnd * 8 + 8, 3:4].bc([128, 8, 64]), ALU.mult)
        stv = STp.v((slice(None), l, hp, slice(None)), (l, hp))
        spb = STpb.v((slice(None), hp, slice(None)), hp)
        LB = PSB[0]
        for b in range(nblk):
            cb = slice(b * 64, (b + 1) * 64)

            def ops(h2):
                hs = HS[h2]
                if prompt:
                    return ([AR[hs, b, 0:64]], [AR[hs, b, 64:128]], [TOK[hs, b, 1, :]], [TOK[hs, b, 2, :]],
                            [STpb.v((hs, hp, slice(None)), hp)])
                return ([AMSK[hs, q, :] for q in range(Q)], [RMSK[hs, q, :] for q in range(Q)],
                        [BHM[hs, q, :] for q in range(Q)], [KHM[hs, q, :] for q in range(Q)],
                        [S0b[hs, q, :] for q in range(Q)])
            seqs = []
            for h2 in range(2):
                hs = HS[h2]
                a_, r_, bh_, kh_, s_ = ops(h2)
                sq = [(LB[hs, 0:64], a_[q], s_[q], q == 0, False) for q in range(Q)]
                sq.append((LB[hs, 0:64], AM[hs, b, 128:192], TOK[hs, b, 0, :], False, True))
                seqs.append(sq)
            for i in range(len(seqs[0])):
                for sq in seqs:
                    o_, l_, r2_, st_, sp_ = sq[i]
                    P.mm(o_, l_, r2_, start=st_, stop=sp_)
            P.copy(RHSb[:], LB[:, 0:64], eng="act")
            yield
            for hs in HS:
                P.mm(LB[hs, 64:128], TTf[hs, b, :], RHSb[hs, :])
            P.copy(Ub[:], LB[:, 64:128], eng="act")
            yield
            seqs = []
            for h2 in range(2):
                hs = HS[h2]
                a_, r_, bh_, kh_, s_ = ops(h2)
                YTb = PSB[3][hs, cb]
                Vt = TOK[hs, b, 0, :]
                sq = [(YTb, s_[q], r_[q], q == 0, False) for q in range(Q)]
                sq.append((YTb, Ub[hs, :], AM[hs, b, 64:128], False, False))
                sq.append((YTb, Vt, AM[hs, b, 192:256], False, True))
                seqs.append(sq)
            for i in range(len(seqs[0])):
                for sq in seqs:
                    o_, l_, r2_, st_, sp_ = sq[i]
                    P.mm(o_, l_, r2_, start=st_, stop=sp_)
            for rnd in range((Q + 7) // 8):
                nq = min(8, Q - rnd * 8)
                SNB = LB if prompt else PS[4]
                c0 = 128 if prompt else 0
                hops = [ops(h2) for h2 in range(2)]
                for qq in range(nq):
                    q = rnd * 8 + qq
                    for h2 in range(2):
                        hs = HS[h2]
                        P.mm(SNB[hs, c0 + qq * 64:c0 + (qq + 1) * 64], hops[h2][2][q], Ub[hs, :], start=True, stop=False)
                    for h2 in range(2):
                        hs = HS[h2]
                        P.mm(SNB[hs, c0 + qq * 64:c0 + (qq + 1) * 64], hops[h2][3][q], TOK[hs, b, 0, :], start=False, stop=True)
                if prompt:
                    if b < nblk - 1:
                        P.stt(spb, stv, WC[:, b * 64 + 63:b * 64 + 64], SNB[:, 128:192], ALU.mult, ALU.add)
                    P.stt(stv, stv, WC[:, b * 64 + 63:b * 64 + 64], SNB[:, 128:192], ALU.mult, ALU.add)
                else:
                    P.tt(S0[:, rnd * 8:rnd * 8 + nq, :], S0w[rnd],
                         SNB[:, 0:nq * 64].r("p (q i) -> p q i", q=nq), ALU.add)
            yield
        if not prompt:
            P.dma(o_swkv[l][:, hp], S0[:])

    def hp_gen(l, g, hp, S):
        N = g.N
        prompt = g.kind == "p"
        PV = Cur.PV
        Tt, TBs, AR, BK, VBF, PSB = S.T, S.TB, S.AR, S.BK, S.VBF, S.PSB
        hc = slice(hp * 128, (hp + 1) * 128)
        XR, XK, XV = Tt[2][:, :N], Tt[3][:, :N], Tt[4][:, :N]
        if prompt:
            stv_ = STp.v((slice(None), l, hp, slice(None)), (l, hp))
            spb_ = STpb.v((slice(None), hp, slice(None)), hp)
            if g.first:
                P.memset(stv_, 0.0)
                P.memset(spb_, 0.0)
            else:
                P.copy(spb_, stv_, eng="act")
        for i, (q, xs) in enumerate(((hp, XR), (8 + hp, XK), (16 + hp, XV))):
            w = w_next("in", l, 1024 + q * 128, 128)
            ps = PSB[i % 2]
            proj(ps[:, :N], w, 128, N)
            shift_chunk(l, g, ps[:, :N], q, 128, xs, Tt[0], Tt[5])
            yield
        if stop == "Cb":
            return
        P.mm(PSB[0][:, :N], LWT[0:64, hc], TWL[0:64, :N])
        SG = Tt[5][:, :N]
        P.act(SG, PSB[0][:, :N], AF.Tanh, scale=0.5, bias=DV[:, W0H + hp:W0H + hp + 1])
        P.ts(SG, SG, 1.0, ALU.add)
        CS = Tt[7][:, :N]
        P.scan(CS, (rm_p if prompt else rm_s)[:, :N], SG, 0.0, ALU.mult, ALU.add)
        cs3 = CS.r("p (s t) -> p s t", s=g.nb)
        CSE = Tt[8][:, :N]
        P.tt(CSE.r("p (s t) -> p s t", s=g.nb), cs3[:, :, g.Lb - 1:g.Lb].bc([128, g.nb, g.Lb]), cs3, ALU.subtract)
        P.tt(SG, CS, SG, ALU.subtract)
        WI = Tt[9][:, :N]
        P.act(WI, CS, AF.Exp, scale=H0)
        P.act(CS, CS, AF.Exp, scale=-H0)
        P.act(SG, SG, AF.Exp, scale=-H0)
        P.act(CSE, CSE, AF.Exp, scale=-H0)
        WC, WM, WE = CS, SG, CSE
        yield
        P.mm(PSB[1][:, :N], LWT[64:128, hc], TWL[64:128, :N])
        THA = Tt[10][:, :N]
        P.act(THA, PSB[1][:, :N], AF.Tanh, scale=0.5, bias=DV[:, A0H + hp:A0H + hp + 1])
        ksq = TBs[0][:, :N]
        P.act(ksq, XK, AF.Square, scale=PV[:, KK + hp:KK + hp + 1])
        P.mm(PSB[1][:, :N], ONEB[:], ksq)
        NR = Tt[0][:, :N]
        P.act(NR, PSB[1][:, :N], AF.Ln, bias=1e-24)
        P.act(NR, NR, AF.Exp, scale=-0.5)
        KKN = Tt[6][:, :N]
        P.stt(KKN, XK, PV[:, KK + hp:KK + hp + 1], NR, ALU.mult, ALU.mult)
        yield
        nbk = g.nblk
        ar3a = AR[:, 0:nbk, 0:64]
        ar3r = AR[:, 0:nbk, 64:128]
        bk3b = BK[:, 0:nbk, 0:64]
        bk3k = BK[:, 0:nbk, 64:128]

        def b3(v):
            return v.r("p (b t) -> p b t", b=nbk)
        P.stt(ar3a, b3(KKN), -1.0, b3(WM), ALU.mult, ALU.mult)
        P.tt(ar3r, b3(XR), b3(WC), ALU.mult)
        B2 = Tt[5][:, :N]
        P.stt(B2, THA, 1.0, KKN, ALU.add, ALU.mult)
        P.stt(bk3b, b3(B2), 0.5, b3(WI), ALU.mult, ALU.mult)
        BHF = VBF[1][:, :N]
        P.stt(BHF, B2, 0.5, WE, ALU.mult, ALU.mult)
        yield
        KF = Tt[5][:, :N]
        P.act(KF, THA, AF.Identity, scale=DV[:, KAH + hp:KAH + hp + 1], bias=DV[:, KAB + hp:KAB + hp + 1])
        P.tt(KF, KF, XK, ALU.mult)
        P.tt(bk3k, b3(KF), b3(WI), ALU.mult)
        KHF = VBF[2][:, :N]
        P.tt(KHF, KF, WE, ALU.mult)
        VB = VBF[0][:, :N]
        P.act(VB, XV, AF.Identity)
        rkb = TBs[1][:, :N]
        P.stt(rkb, XR, PV[:, RK + hp:RK + hp + 1], KF, ALU.mult, ALU.mult)
        P.mm(PSB[0][:, :N], ONEB[:], rkb)
        BON = Tt[9][:, :N]
        P.tt(BON, PSB[0][:, :N], XV, ALU.mult)
        P.mm(PSB[1][:, :N], G2A[:, hc], SGL[:, :N], start=True, stop=False)
        P.mm(PSB[1][:, :N], G2B[:, hc], SGL2[:, :N], start=False, stop=True)
        GP = Tt[10][:, :N]
        P.act(GP, PSB[1][:, :N], AF.Identity)
        yield
        for _ in wkv(l, g, hp, S, VB, BHF, KHF, WC):
            yield
        if stop in ("W1", "W2"):
            return
        YS = Tt[5][:, :N]
        P.act(YS, PSB[3][:, :N], AF.Identity)
        if stop == "C1" and hp == 0:
            dumpv("YS", YS, [128, N])
            return
        yb_, y2_ = TBs[0][:, :N], TBs[1][:, :N]
        P.act(yb_, YS, AF.Identity)
        P.act(y2_, YS, AF.Square)
        P.mm(PSB[0][:, :N], ONEB64[:], yb_)
        P.mm(PSB[1][:, :N], ONEB64[:], y2_)
        yield
        VR = Tt[6][:, :N]
        MS = Tt[7][:, :N]
        P.act(MS, PSB[0][:, :N], AF.Identity)
        P.tt(VR, MS, MS, ALU.mult)
        P.tt(VR, PSB[1][:, :N], VR, ALU.subtract)
        P.act(VR, VR, AF.Ln, bias=GN_EPS)
        P.act(VR, VR, AF.Exp, scale=-0.5)
        P.tt(YS, YS, MS, ALU.subtract)
        P.tt(YS, YS, VR, ALU.mult)
        P.act(YS, YS, AF.Identity, scale=PV[:, LNG + hp:LNG + hp + 1], bias=PV[:, LNB + hp:LNB + hp + 1])
        P.tt(YS, YS, BON, ALU.add)
        P.stt(BF.v((slice(None), 8 + hp, slice(0, N)), 8 + hp), YS, 0.5, GP, ALU.mult, ALU.mult)
        yield

    def run_skewed(fns, sets, lag):
        pending = list(fns)
        free = list(sets)
        active = []
        while pending or active:
            if pending and free and (not active or min(a[2] for a in active) >= lag):
                S = free.pop(0)
                active.append([pending.pop(0)(S), S, 0])
            for a in list(active):
                try:
                    next(a[0])
                    a[2] += 1
                except StopIteration:
                    active.remove(a)
                    free.append(a[1])

    def run_gens(gens):
        active = list(gens)
        while active:
            for gen in list(active):
                try:
                    next(gen)
                except StopIteration:
                    active.remove(gen)

    def run_pass(l, g):
        N, L, ti = g.N, g.L, g.ti
        prompt = g.kind == "p"
        PV = Cur.PV
        xc = [XT.v((slice(None), c, slice(0, N)), c) for c in range(8)]
        if not prompt:
            P.dma(STS[:], sts_d[l])
            P.dma(STF[:], stf_d[l])
        rstd = rmsnorm(xc, N)
        for c in range(8):
            P.stt(H[:, c, :N], xc[c], PV[:, NMG + c:NMG + c + 1], rstd, ALU.mult, ALU.mult)
        if stop == "A":
            return
        def conv_proj(cc):
            pa_, pb_ = PS[2 * (cc % 2)], PS[2 * (cc % 2) + 1]
            wa = w_next("in", l, cc * 128, 128)
            proj(pa_[:, :N], wa, 128, N)
            wb = w_next("in", l, 512 + cc * 128, 128)
            proj(pb_[:, :N], wb, 128, N)

        conv_proj(0)
        for cc in range(4):
            pa_, pb_ = PS[2 * (cc % 2)], PS[2 * (cc % 2) + 1]
            th, zah = T[4 + (cc % 2)][:, :N], T[6 + (cc % 2)][:, :N]
            P.act(th, pb_[:, :N], AF.Tanh, scale=0.5)
            P.act(zah, pa_[:, :N], AF.Identity, scale=0.5)
            if cc < 3:
                conv_proj(cc + 1)
            u3 = hv(UB[cc], 128, g, 30)
            if prompt:
                if g.first:
                    P.memset(u3[:, :, 0:30], 0.0)
                else:
                    P.copy(u3[:, :, 0:30], CHALO[:, l, cc, :].un(1), eng="act")
            else:
                P.dma(u3[:, :, 0:30], stc_d[l][:, cc])
            P.stt(u3[:, :, 30:30 + L], v3(th, g), 1.0, v3(zah, g), ALU.add, ALU.mult)
            a3 = v3(T[cc][:, :N], g)
            if prompt:
                ubf = T[8 + (cc % 2)][:, 0:272].bitcast(BF16)
                P.act(ubf[:, 0:30 + L], UB[cc][:, 0:30 + L], AF.Identity)
                for j in range(31):
                    dg = DIAG[j % 8]
                    P.ts(dg[:], IDENTB[:], PV[:, CDW + cc * 31 + j:CDW + cc * 31 + j + 1], ALU.mult)
                    P.mm(PS[4 + (cc % 2)][:, :N], dg[:], ubf[:, j:j + L], start=(j == 0), stop=(j == 30))
                P.act(T[cc][:, :N], PS[4 + (cc % 2)][:, :N], AF.Identity, bias=PV[:, CDB + cc:CDB + cc + 1])
            else:
                P.ts(a3, u3[:, :, 0:L], PV[:, CDW + cc * 31:CDW + cc * 31 + 1], ALU.mult,
                     PV[:, CDB + cc:CDB + cc + 1], ALU.add)
                for j in range(1, 31):
                    P.stt(a3, u3[:, :, j:j + L], PV[:, CDW + cc * 31 + j:CDW + cc * 31 + j + 1], a3, ALU.mult, ALU.add)
            cb_, c2_ = TB[0][:, :N], TB[1][:, :N]
            P.act(cb_, T[cc][:, :N], AF.Identity)
            P.act(c2_, T[cc][:, :N], AF.Square)
            P.mm(PS[6][:, :N], ONES[:], cb_, start=(cc == 0), stop=(cc == 3))
            P.mm(PS[7][:, :N], ONES[:], c2_, start=(cc == 0), stop=(cc == 3))
            if prompt:
                P.copy(CHALO[:, l, cc, :].un(1), u3[:, :, L:L + 30], eng="act")
            else:
                P.dma(o_sconv[l][:, cc], u3[:, :, 4:34])
        if prompt and g.last:
            P.dma(o_pconv[l], CHALO[:, l])
        mean, var = T[6][:, :N], T[7][:, :N]
        P.ts(mean, PS[6][:, :N], 1.0 / 512, ALU.mult)
        P.tt(var, mean, mean, ALU.mult)
        P.stt(var, PS[7][:, :N], 1.0 / 512, var, ALU.mult, ALU.subtract)
        P.act(var, var, AF.Ln, bias=LN_EPS)
        P.act(var, var, AF.Exp, scale=-0.5)
        for cc in range(4):
            a = T[cc][:, :N]
            P.tt(a, a, mean, ALU.subtract)
            P.tt(a, a, var, ALU.mult)
            P.act(a, a, AF.Identity, scale=DV[:, CLGH + cc:CLGH + cc + 1], bias=DV[:, CLBH + cc:CLBH + cc + 1])
            th = T[4 + (cc % 2)][:, :N]
            P.act(th, a, AF.Tanh)
            P.stt(CA[:, cc, :N], th, 1.0, a, ALU.add, ALU.mult)
        if stop == "B":
            dumpv("CA", CA[:, :, :N], [128, 4, N])
            return
        for q, M in ((24, 128), (25, 128), (26, 32)):
            w = w_next("in", l, 1024 + q * 128, 128)
            ps = PS[q % 2]
            proj(ps[:M, :N], w, M, N)
            xs = T[2][:M, :N]
            shift_chunk(l, g, ps[:M, :N], q, M, xs, T[0], T[1])
            if q == 24:
                P.act(TWL[0:64, :N], T[2][0:64, :N], AF.Tanh)
                P.act(TWL[64:128, :N], T[2][64:128, :N], AF.Identity)
            elif q == 25:
                P.act(T[3][:, :N], xs, AF.Tanh, scale=0.5)
                P.ts(SGL[:, :N], T[3][:, :N], 1.0, ALU.add)
            else:
                P.act(T[3][0:32, :N], xs, AF.Tanh, scale=0.5)
                P.ts(SGL2[0:32, :N], T[3][0:32, :N], 1.0, ALU.add)
        if stop == "Ca":
            return
        if prompt:
            run_skewed([(lambda S, hp=hp: hp_gen(l, g, hp, S)) for hp in range(8)], [SET0, SET1], HP_LAG)
        else:
            SS = HPS()
            SS.__dict__.update(SET0.__dict__)
            SS.PSB = PS[0:4]
            for hp in range(8):
                run_gens([hp_gen(l, g, hp, SS)])
        if prompt and g.last:
            P.dma(o_pshift[l], CSH[:, l, :, 0])
            P.dma(o_pwkv[l], STp[:, l])
        if not prompt:
            P.dma(o_sshift[l], STS[:])
        if stop in ("C", "C1", "W1", "W2", "Cb"):
            dumpv("YF", BF[:, 8:16, :N], [128, 8, N])
            dumpv("STp", STp[:, l], [128, 8, 64])
            return
        for m in range(8):
            o = 4 * (m % 2)
            wco = w_next("co", l, m * 128)
            for k in range(4):
                P.mm(PS[o][:, :N], wco[:, k, :], CA[:, k, :N], start=(k == 0), stop=(k == 3))
            wro = w_next("ro", l, m * 128)
            for k in range(8):
                P.mm(PS[o + 1][:, :N], wro[:, k, :], BF.v((slice(None), 8 + k, slice(0, N)), 8 + k), start=(k == 0), stop=(k == 7))
            wg1 = w_next("in", l, 4384 + m * 128, 128)
            proj(PS[o + 2][:, :N], wg1, 128, N)
            wg2 = w_next("in", l, 5408 + m * 128, 128)
            proj(PS[o + 3][:, :N], wg2, 128, N)
            t1, t2 = T[(m % 2) * 2][:, :N], T[(m % 2) * 2 + 1][:, :N]
            P.act(t1, PS[o + 2][:, :N], AF.Tanh, scale=0.5)
            P.act(t2, PS[o + 3][:, :N], AF.Tanh, scale=0.5)
            P.stt(t1, t1, 1.0, PS[o][:, :N], ALU.add, ALU.mult)
            P.stt(t2, t2, 1.0, PS[o + 1][:, :N], ALU.add, ALU.mult)
            P.tt(BF.v((slice(None), m, slice(0, N)), m), t1, t2, ALU.add)
        for mo in range(8):
            w = w_next("mx", l, mo * 128)
            ps = PS[mo % 2]
            for k in range(8):
                P.mm(ps[:, :N], w[:, k, :], BF.v((slice(None), k, slice(0, N)), k), start=(k == 0), stop=(k == 7))
            P.stt(xc[mo], ps[:, :N], 0.5, xc[mo], ALU.mult, ALU.add)
        if stop == "M":
            return
        rstd = rmsnorm(xc, N)
        for c in range(8):
            P.stt(H[:, c, :N], xc[c], PV[:, NFG + c:NFG + c + 1], rstd, ALU.mult, ALU.mult)
        def bfrow(r):
            return BF.v((slice(None), r, slice(0, N)), r)

        def up_gen(part, r0):
            for jj in range(8):
                j = part * 8 + jj
                cus = []
                for half, q in ((0, j), (1, 24 + j)):
                    w = w_next("up", l, q * 128)
                    ps = PS[half + 2 * (jj % 2)]
                    proj(ps[:, :N], w, 128, N)
                    upb = T[half + 2 * (jj % 2)]
                    u3 = hv(upb, 128, g, 2)
                    if prompt:
                        if g.first:
                            P.memset(u3[:, :, 0:2], 0.0)
                        else:
                            P.copy(u3[:, :, 0:2], CFF[:, l, q, :].un(1), eng="act")
                    else:
                        P.copy(u3[:, :, 0:2], STF[:, q, :, :], eng="act")
                    P.act(u3[:, :, 2:2 + L], v3(ps[:, :N], g), AF.Identity)
                    cu = T[4 + half + 2 * (jj % 2)][:, :N]
                    c3 = v3(cu, g)
                    fw0 = FDW + q * 3
                    P.act(c3, v3(ps[:, :N], g), AF.Identity, scale=PV[:, fw0 + 2:fw0 + 3], bias=PV[:, FDB + q:FDB + q + 1])
                    P.stt(c3, u3[:, :, 1:1 + L], PV[:, fw0 + 1:fw0 + 2], c3, ALU.mult, ALU.add)
                    P.stt(c3, u3[:, :, 0:L], PV[:, fw0:fw0 + 1], c3, ALU.mult, ALU.add)
                    if prompt:
                        P.copy(CFF[:, l, q, :].un(1), u3[:, :, L:L + 2], eng="act")
                    else:
                        P.copy(STF[:, q, :, :], u3[:, :, L:L + 2], eng="act")
                    cus.append(cu)
                ga = T[8 + (jj % 2)][:, :N]
                P.act(ga, cus[0], AF.Gelu_apprx_tanh)
                P.tt(bfrow(r0 + jj), ga, cus[1], ALU.mult)
                yield

        def dn_gen(part, r0):
            for mo in range(8):
                w = w_next("dn", l, part, mo * 128)
                ps = PS[4 + (mo % 2)]
                for k in range(8):
                    P.mm(ps[:, :N], w[:, k, :], bfrow(r0 + k), start=(k == 0), stop=(k == 7))
                P.tt(xc[mo], xc[mo], ps[:, :N], ALU.add)
                yield

        run_gens([up_gen(0, 0)])
        for part in range(3):
            gens = [dn_gen(part, (part % 2) * 8)]
            if part < 2:
                gens.append(up_gen(part + 1, ((part + 1) % 2) * 8))
            run_gens(gens)
        if prompt and g.last:
            P.dma(o_pffn[l], CFF[:, l])
        if not prompt:
            P.dma(o_sffn[l], STF[:])

    dbg_x = dout("dbg_xT", [128, 8, NTOK]) if (dump is not None and stop is not None) else None
    if dbg_x is not None:
        dumps["xT"] = [128, 8, NTOK]
    for g in tiles:
        N = g.N
        P.dma(XT[:, :, :N], xT_d[:, :, g.t0:g.t0 + N])
        for l in range(depth):
            layer_setup(l)
            run_pass(l, g)
        xc = [XT.v((slice(None), c, slice(0, N)), c) for c in range(8)]
        if stop is None:
            rstd = rmsnorm(xc, N)
            for c in range(8):
                yo = T[c % 4][:, :N]
                P.stt(yo, xc[c], PVF[:, c:c + 1], rstd, ALU.mult, ALU.mult)
                P.dma(o_y[:, c, g.t0:g.t0 + N], yo)
        elif dbg_x is not None:
            P.dma(dbg_x[:, :, g.t0:g.t0 + N], XT[:, :, :N])
    P.finish()
    if not dry:
        P.emit()
    return nc, P, dumps


def _pm(v):
    return np.ascontiguousarray(v.reshape(-1, 128).T)


def pack_params(inp):
    pv = np.zeros((DEPTH, 128, NPV), np.float32)
    for l in range(DEPTH):
        pv[l, :, NMG:NMG + 8] = _pm(inp["norm_mix_g"][l])
        pv[l, :, NFG:NFG + 8] = _pm(inp["norm_ffn_g"][l])
        cw = inp["conv_dw_w"][l]
        pv[l, :, CDW:CDW + 124] = cw.reshape(31, 4, 128).transpose(2, 1, 0).reshape(128, 124)
        pv[l, :, CDB:CDB + 4] = _pm(inp["conv_dw_b"][l])
        pv[l, :, CLG:CLG + 4] = _pm(inp["conv_ln_g"][l])
        pv[l, :, CLB:CLB + 4] = _pm(inp["conv_ln_b"][l])
        mu = np.zeros(27 * 128, np.float32)
        mu[:3360] = inp["rw_mu"][l]
        pv[l, :, MU:MU + 27] = _pm(mu)
        pv[l, :, W0:W0 + 8] = _pm(inp["rw_w0"][l])
        pv[l, :, A0:A0 + 8] = _pm(inp["rw_a0"][l])
        pv[l, :, KK:KK + 8] = _pm(inp["rw_k_k"][l])
        pv[l, :, KA:KA + 8] = _pm(inp["rw_k_a"][l])
        pv[l, :, RK:RK + 8] = _pm(inp["rw_r_k"][l].reshape(-1))
        pv[l, :, LNG:LNG + 8] = _pm(inp["rw_ln_g"][l])
        pv[l, :, LNB:LNB + 8] = _pm(inp["rw_ln_b"][l])
        fw_ = inp["ffn_dw_w"][l]
        pv[l, :, FDW:FDW + 144] = fw_.reshape(3, 48, 128).transpose(2, 1, 0).reshape(128, 144)
        pv[l, :, FDB:FDB + 48] = _pm(inp["ffn_dw_b"][l])
    return pv


def make_in_maps(inp, cores=None):
    cst, cstb = make_consts()
    pv = pack_params(inp)
    pvf = _pm(inp["norm_final_g"])
    lw = np.ascontiguousarray(np.concatenate([inp["rw_w2"], inp["rw_a2"]], axis=1))
    L_ = DEPTH
    wi = inp["w_in"]
    w_in_u = np.zeros((L_, 51, 128, 8, 128), np.float32)
    w_in_u[:, 0:34] = wi[:, :, 0:4352].reshape(L_, 8, 128, 34, 128).transpose(0, 3, 2, 1, 4)
    w_in_u[:, 34, :, :, 0:32] = wi[:, :, 4352:4384].reshape(L_, 8, 128, 32).transpose(0, 2, 1, 3)
    w_in_u[:, 35:51] = wi[:, :, 4384:6432].reshape(L_, 8, 128, 16, 128).transpose(0, 3, 2, 1, 4)

    def units(w, K):
        n = w.shape[2] // 128
        return np.ascontiguousarray(w.reshape(L_, K, 128, n, 128).transpose(0, 3, 2, 1, 4)).reshape(L_, n, 128, K * 128)
    w_dn_u = np.ascontiguousarray(inp["w_down"].reshape(L_, 3, 8, 128, 8, 128).transpose(0, 1, 4, 3, 2, 5)).reshape(L_, 24, 128, 1024)
    shared = dict(cst=cst, cstb=cstb, pv=pv, pvf=pvf, lw=lw, g2=inp["rw_g2"],
                  w_in_u=w_in_u.reshape(L_, 51, 128, 1024),
                  w_co_u=units(inp["w_conv_out"], 4), w_ro_u=units(inp["w_rw_out"], 8),
                  w_mx_u=units(inp["w_mix_out"], 8), w_up_u=units(inp["w_up"], 8), w_dn_u=w_dn_u)
    maps = []
    for c in (range(NCORE) if cores is None else cores):
        sb = slice(16 * c, 16 * c + 16)
        xs = np.concatenate([inp["x_prompt"][c], inp["x_sample"][sb].reshape(64, D)], axis=0)
        xT = np.ascontiguousarray(xs.T.reshape(8, 128, NTOK).transpose(1, 0, 2))
        stc = np.ascontiguousarray(inp["state_conv"][:, sb].reshape(DEPTH, 16, 30, 4, 128).transpose(0, 4, 3, 1, 2))
        ss = np.zeros((DEPTH, 16, 27 * 128), np.float32)
        ss[:, :, :3360] = inp["state_shift"][:, sb]
        sts = np.ascontiguousarray(ss.reshape(DEPTH, 16, 27, 128).transpose(0, 3, 2, 1))
        sw = inp["state_wkv"][:, sb].reshape(DEPTH, 16, 8, 2, 64, 64)
        stw = np.ascontiguousarray(sw.transpose(0, 3, 5, 2, 1, 4).reshape(DEPTH, 128, 8, 16, 64))
        stf = np.ascontiguousarray(inp["state_ffn"][:, sb].reshape(DEPTH, 16, 2, 48, 128).transpose(0, 4, 3, 1, 2))
        m = dict(shared)
        m.update(xT=xT, stc=stc, sts=sts, stw=stw, stf=stf)
        maps.append(m)
    return maps


def assemble(results):
    L_ = DEPTH
    y_prompt = np.zeros((8, SEQ, D), np.float32)
    y_sample = np.zeros((128, 4, D), np.float32)
    p_conv = np.zeros((L_, 8, 30, 512), np.float32)
    p_shift = np.zeros((L_, 8, 3360), np.float32)
    p_wkv = np.zeros((L_, 8, 16, 64, 64), np.float32)
    p_ffn = np.zeros((L_, 8, 2, 6144), np.float32)
    s_conv = np.zeros((L_, 128, 30, 512), np.float32)
    s_shift = np.zeros((L_, 128, 3360), np.float32)
    s_wkv = np.zeros((L_, 128, 16, 64, 64), np.float32)
    s_ffn = np.zeros((L_, 128, 2, 6144), np.float32)
    for c, r in enumerate(results):
        sb = slice(16 * c, 16 * c + 16)
        yT = r["o_y"].transpose(1, 0, 2).reshape(D, NTOK)
        y_prompt[c] = yT[:, :SEQ].T
        y_sample[sb] = yT[:, SEQ:].T.reshape(16, 4, D)
        p_conv[:, c] = r["o_pconv"].transpose(0, 3, 2, 1).reshape(L_, 30, 512)
        s_conv[:, sb] = r["o_sconv"].transpose(0, 3, 4, 2, 1).reshape(L_, 16, 30, 512)
        p_shift[:, c] = r["o_pshift"].transpose(0, 2, 1).reshape(L_, 27 * 128)[:, :3360]
        s_shift[:, sb] = r["o_sshift"].transpose(0, 3, 2, 1).reshape(L_, 16, 27 * 128)[:, :, :3360]
        pw = r["o_pwkv"].reshape(L_, 2, 64, 8, 64)
        p_wkv[:, c] = pw.transpose(0, 3, 1, 4, 2).reshape(L_, 16, 64, 64)
        sw = r["o_swkv"].reshape(L_, 2, 64, 8, 16, 64)
        s_wkv[:, sb] = sw.transpose(0, 4, 3, 1, 5, 2).reshape(L_, 16, 16, 64, 64)
        p_ffn[:, c] = r["o_pffn"].transpose(0, 3, 2, 1).reshape(L_, 2, 6144)
        s_ffn[:, sb] = r["o_sffn"].transpose(0, 3, 4, 2, 1).reshape(L_, 16, 2, 6144)
    return (y_prompt, y_sample, p_conv, p_shift, p_wkv, p_ffn, s_conv, s_shift, s_wkv, s_ffn)


def kernel(**inputs):
    inp = {k: np.asarray(v) for k, v in inputs.items()}
    nc, P, _ = build()
    maps = make_in_maps(inp)
    res = run_bass_kernel_spmd(nc, maps, core_ids=list(range(NCORE)))
    return assemble(res.results)
```

```python
import numpy as np
import concourse.bass as bass
import concourse.mybir as mybir

F32 = mybir.dt.float32
BF16 = mybir.dt.bfloat16
AF = mybir.ActivationFunctionType
ALU = mybir.AluOpType


ATTACH_WAIT = 1
SAME_ENG_MASK = 7


class V:
    __slots__ = ("tile", "ap", "sub")

    def __init__(self, tile, ap, sub=None):
        self.tile = tile
        self.ap = ap
        self.sub = sub

    def __getitem__(self, idx):
        return V(self.tile, self.ap[idx], self.sub)

    def k(self, sub):
        return V(self.tile, self.ap, sub)

    def r(self, pat, **kw):
        return V(self.tile, self.ap.rearrange(pat, **kw), self.sub)

    def bc(self, shape):
        return V(self.tile, self.ap.broadcast_to(shape), self.sub)

    def un(self, axis):
        return V(self.tile, self.ap.unsqueeze(axis), self.sub)

    def bitcast(self, dt):
        return V(self.tile, self.ap.bitcast(dt), self.sub)


class Tile:
    def __init__(self, name, handle):
        self.name = name
        self.h = handle

    def __getitem__(self, idx):
        return V(self.name, self.h[idx], None)

    def v(self, idx, sub):
        return V(self.name, self.h[idx], sub)


class Op:
    __slots__ = ("id", "eng", "fn", "dma", "deps", "sem", "semval", "signal", "rank", "prewait")

    def __init__(self, id, eng, fn, dma):
        self.id = id
        self.eng = eng
        self.fn = fn
        self.dma = dma
        self.deps = {}
        self.sem = None
        self.semval = 0
        self.signal = False
        self.rank = 0
        self.prewait = None


class Prog:
    ENGS = ("pe", "act", "dve", "pool", "sp")

    def __init__(self, nc, n_dma_sems=40):
        self.nc = nc
        self.ops = []
        self.state = {}
        self.n_dma_sems = n_dma_sems
        self.sb_bytes = 0

    def sb(self, name, shape, dtype):
        h = self.nc.alloc_sbuf_tensor("sb_" + name, list(shape), dtype)
        n = 1
        for s in shape[1:]:
            n *= s
        self.sb_bytes += n * (4 if dtype == F32 else 2)
        return Tile(name, h)

    def ps(self, name, shape, dtype=F32):
        h = self.nc.alloc_psum_tensor("ps_" + name, list(shape), dtype)
        return Tile(name, h)

    def _collect(self, v, is_write, deps):
        if v is None or not isinstance(v, V) or v.tile is None:
            return
        st = self.state.setdefault(v.tile, {})
        subs = list(st.keys()) if v.sub is None else [s for s in (v.sub, None) if s in st]
        for s in subs:
            w, rs = st[s]
            if w is not None:
                deps[w] = deps.get(w, 0) | (2 if is_write else 1)
            if is_write:
                for r in rs:
                    deps[r] = deps.get(r, 0) | 4

    def _update(self, v, is_write, opid):
        if v is None or not isinstance(v, V) or v.tile is None:
            return
        st = self.state.setdefault(v.tile, {})
        if is_write:
            if v.sub is None:
                st.clear()
            st[v.sub] = [opid, []]
        else:
            if v.sub not in st:
                st[v.sub] = [None, []]
            st[v.sub][1].append(opid)

    def add(self, eng, fn, reads, writes, dma=False):
        op = Op(len(self.ops), eng, fn, dma)
        deps = {}
        for v in reads:
            self._collect(v, False, deps)
        for v in writes:
            self._collect(v, True, deps)
        for v in reads:
            self._update(v, False, op.id)
        for v in writes:
            self._update(v, True, op.id)
        deps.pop(op.id, None)
        op.deps = deps
        self.ops.append(op)
        return op

    @staticmethod
    def _a(x):
        return x.ap if isinstance(x, V) else x

    def mm(self, out, lhsT, rhs, start=True, stop=True):
        a = self._a
        return self.add("pe", lambda e: e.matmul(a(out), a(lhsT), a(rhs), start=start, stop=stop),
                        [lhsT, rhs], [out])

    def tr(self, out, in_, ident):
        a = self._a
        return self.add("pe", lambda e: e.transpose(a(out), a(in_), a(ident)), [in_, ident], [out])

    def act(self, out, in_, func, scale=1.0, bias=0.0, eng="act"):
        a = self._a
        return self.add(eng, lambda e: e.activation(a(out), a(in_), func, bias=a(bias), scale=a(scale)),
                        [in_, scale, bias], [out])

    def tt(self, out, in0, in1, op, eng="dve"):
        a = self._a
        return self.add(eng, lambda e: e.tensor_tensor(a(out), a(in0), a(in1), op), [in0, in1], [out])

    def ts(self, out, in0, s1, op0, s2=None, op1=None, eng="dve"):
        a = self._a
        if op1 is None:
            return self.add(eng, lambda e: e.tensor_scalar(a(out), a(in0), a(s1), None, op0), [in0, s1], [out])
        return self.add(eng, lambda e: e.tensor_scalar(a(out), a(in0), a(s1), a(s2), op0, op1),
                        [in0, s1, s2], [out])

    def stt(self, out, in0, scalar, in1, op0, op1):
        a = self._a
        return self.add("dve", lambda e: e.scalar_tensor_tensor(a(out), a(in0), a(scalar), a(in1), op0, op1),
                        [in0, scalar, in1], [out])

    def scan(self, out, d0, d1, init, op0, op1):
        a = self._a
        return self.add("dve", lambda e: e.tensor_tensor_scan(a(out), a(d0), a(d1), a(init), op0, op1),
                        [d0, d1, init], [out])

    def copy(self, out, in_, eng="dve"):
        a = self._a
        if eng == "act":
            return self.add("act", lambda e: e.copy(a(out), a(in_)), [in_], [out])
        return self.add(eng, lambda e: e.tensor_copy(a(out), a(in_)), [in_], [out])

    def recip(self, out, in_):
        a = self._a
        return self.add("dve", lambda e: e.reciprocal(a(out), a(in_)), [in_], [out])

    def memset(self, out, val, eng="dve"):
        a = self._a
        return self.add(eng, lambda e: e.memset(a(out), val), [], [out])

    def dma(self, out, in_, eng="sp"):
        a = self._a
        return self.add(eng, lambda e: e.dma_start(out=a(out), in_=a(in_)), [in_], [out], dma=True)

    def finish(self, eng="sp"):
        op = Op(len(self.ops), eng, lambda e: None, False)
        op.deps = {o.id: 1 for o in self.ops if o.dma}
        self.ops.append(op)

    def emit(self):
        nc = self.nc
        ops = self.ops
        from contextlib import ExitStack
        with ExitStack() as es:
            eng_sem = {e: es.enter_context(nc.semaphore("s_" + e)) for e in self.ENGS}
            dma_sems = [es.enter_context(nc.semaphore("d%d" % i)) for i in range(self.n_dma_sems)]
            nsw = (self.n_dma_sems * 3) // 5
            pools = {True: list(range(0, nsw)), False: list(range(nsw, self.n_dma_sems))}
            dma_cnt = [0] * self.n_dma_sems
            dma_last = [None] * self.n_dma_sems
            kk_ = {True: 0, False: 0}
            for op in ops:
                if op.dma:
                    sw = op.eng == "pool"
                    pl = pools[sw]
                    i = pl[kk_[sw] % len(pl)]
                    kk_[sw] += 1
                    op.prewait = dma_last[i]
                    dma_cnt[i] += 16
                    op.sem = dma_sems[i]
                    op.semval = dma_cnt[i]
                    dma_last[i] = op.id
            for op in ops:
                for d, kind in op.deps.items():
                    dop = ops[d]
                    if dop.dma:
                        continue
                    if dop.eng == op.eng:
                        if op.eng == "pe":
                            continue
                        if not (kind & SAME_ENG_MASK):
                            continue
                    dop.signal = True
            rank = {e: 0 for e in self.ENGS}
            for op in ops:
                if not op.dma and op.signal:
                    rank[op.eng] += 1
                    op.rank = rank[op.eng]
            clocks = [None] * len(ops)
            eng_clock = {e: {} for e in self.ENGS}
            dma_known = {e: set() for e in self.ENGS}
            plans = {e: [] for e in self.ENGS}
            for op in ops:
                ck = eng_clock[op.eng]
                waits = []
                deplist = list(op.deps.items())
                if op.prewait is not None:
                    deplist.append((op.prewait, 3))
                for d, kind in sorted(deplist):
                    dop = ops[d]
                    if dop.dma:
                        if d in dma_known[op.eng]:
                            continue
                        waits.append((dop.sem, dop.semval))
                        dma_known[op.eng].add(d)
                    else:
                        if dop.eng == op.eng and (op.eng == "pe" or not (kind & SAME_ENG_MASK)):
                            continue
                        key = dop.eng
                        if ck.get(key, 0) >= dop.rank:
                            continue
                        waits.append((eng_sem[dop.eng], dop.rank))
                    for kk, vv in clocks[d].items():
                        if ck.get(kk, 0) < vv:
                            ck[kk] = vv
                best = {}
                for s, val in waits:
                    kid = id(s)
                    if kid not in best or best[kid][1] < val:
                        best[kid] = (s, val)
                myck = dict(ck)
                if (not op.dma) and op.signal:
                    myck[op.eng] = op.rank
                clocks[op.id] = myck
                plans[op.eng].append((op, list(best.values())))
            self.n_waits = sum(len(w) for e in plans for _, w in plans[e])
            handles = {"pe": "tensor", "act": "scalar", "dve": "vector", "pool": "gpsimd", "sp": "sync"}
            with nc.Block() as block:
                def run(eng_name):
                    def body(e):
                        for op, waits in plans[eng_name]:
                            attach = None
                            if ATTACH_WAIT and waits and not op.dma and op.eng != "sp":
                                attach = waits[-1]
                                waits = waits[:-1]
                            for s, val in waits:
                                e.wait_ge(s, val)
                            ins = op.fn(e)
                            if ins is None:
                                if attach is not None:
                                    e.wait_ge(*attach)
                                continue
                            if attach is not None:
                                ins._wait_ge(attach[0], attach[1])
                            if op.dma:
                                ins.then_inc(op.sem, 16)
                            elif op.signal:
                                ins.then_inc(eng_sem[eng_name], 1)
                    return body
                block.tensor(run("pe"))
                block.scalar(run("act"))
                block.vector(run("dve"))
                block.gpsimd(run("pool"))
                block.sync(run("sp"))


from concourse.bass_utils import run_bass_kernel_spmd

DEPTH = 4
NCORE = 8
HP_LAG = 0
D = 1024
SEQ = 2048
NTOK = 2112
RMS_EPS = 1e-6
LN_EPS = 1e-5
GN_EPS = 64e-5
H0 = 0.5 * float(np.exp(-0.5))

NMG, NFG, CDW, CDB, CLG, CLB, MU, W0, A0, KK, KA, RK, LNG, LNB, FDW, FDB, NPV = (
    0, 8, 16, 140, 144, 148, 152, 179, 187, 195, 203, 211, 219, 227, 235, 379, 427)
CLGH, CLBH, W0H, A0H, KAH, KAB, OMM, NDV = 0, 4, 8, 16, 24, 32, 40, 67
C_ID, C_IDB, NCST = 0, 128, 192
C_MP, C_MS, C_RMP, C_RMS, C_SEQM, C_SEQMT, NCSTB = 0, 320, 640, 1152, 1216, 2240, 2256


def make_consts():
    import ml_dtypes
    c0 = np.zeros((128, NCST), np.float32)
    c0[:, C_ID:C_ID + 128] = np.eye(128, dtype=np.float32)
    p = np.arange(128) % 64
    c0[:, C_IDB:C_IDB + 64] = (p[:, None] == np.arange(64)[None, :])
    c = np.zeros((128, NCSTB), np.float32)
    t = np.arange(64)[None, :]
    s = p[:, None]
    for base, same in ((C_MP, np.ones((128, 64), bool)), (C_MS, (s // 4) == (t // 4))):
        su = (s < t) & same
        iu = (s <= t) & same
        sl = (s > t) & same
        c[:, base:base + 320] = np.concatenate([su, iu, su, iu, sl], axis=1)
    c[:, C_RMP:C_RMP + 512] = (np.arange(512) % 64 != 0)[None, :]
    c[:, C_RMS:C_RMS + 64] = (np.arange(64) % 4 != 0)[None, :]
    sm = (np.arange(16)[:, None] == (np.arange(64) // 4)[None, :]).astype(np.float32)
    c[:, C_SEQM:C_SEQM + 1024] = sm.reshape(1, 1024)
    c[:, C_SEQMT:C_SEQMT + 16] = ((p // 4)[:, None] == np.arange(16)[None, :])
    return c0, c.astype(ml_dtypes.bfloat16)


class Geo:
    def __init__(self, kind, t0, ti, first=False, last=False):
        self.kind, self.t0, self.ti, self.first, self.last = kind, t0, ti, first, last
        if kind == "p":
            self.N, self.nseq, self.L, self.nb, self.Lb, self.nblk = 512, 1, 512, 8, 64, 8
        else:
            self.N, self.nseq, self.L, self.nb, self.Lb, self.nblk = 64, 16, 4, 16, 4, 1


def build(depth=DEPTH, tiles=None, dump=None, stop=None):
    sched = []
    _build(depth, tiles, None, stop, sched, True)
    return _build(depth, tiles, dump, stop, sched, False)


def _build(depth, tiles, dump, stop, sched, dry):
    nc = bass.Bass("TRN2", target_bir_lowering=False)
    P = Prog(nc)
    L_ = DEPTH

    def din(name, shape, dt=F32):
        return nc.dram_tensor(name, list(shape), dt, kind="ExternalInput").ap()

    def dout(name, shape):
        return nc.dram_tensor(name, list(shape), F32, kind="ExternalOutput").ap()

    xT_d = din("xT", [128, 8, NTOK])
    cst_d = din("cst", [128, NCST])
    cstb_d = din("cstb", [128, NCSTB], BF16)
    pv_d = din("pv", [L_, 128, NPV])
    pvf_d = din("pvf", [128, 8])
    stc_d = din("stc", [L_, 128, 4, 16, 30])
    sts_d = din("sts", [L_, 128, 27, 16])
    stw_d = din("stw", [L_, 128, 8, 16, 64])
    stf_d = din("stf", [L_, 128, 48, 16, 2])
    w_in_d = din("w_in_u", [L_, 51, 128, 1024])
    wco_d = din("w_co_u", [L_, 8, 128, 512])
    lw_d = din("lw", [L_, 128, 1024])
    g2_d = din("g2", [L_, 160, 1024])
    wro_d = din("w_ro_u", [L_, 8, 128, 1024])
    wmx_d = din("w_mx_u", [L_, 8, 128, 1024])
    wup_d = din("w_up_u", [L_, 48, 128, 1024])
    wdn_d = din("w_dn_u", [L_, 24, 128, 1024])

    o_y = dout("o_y", [128, 8, NTOK])
    o_pconv = dout("o_pconv", [L_, 128, 4, 30])
    o_sconv = dout("o_sconv", [L_, 128, 4, 16, 30])
    o_pshift = dout("o_pshift", [L_, 128, 27])
    o_sshift = dout("o_sshift", [L_, 128, 27, 16])
    o_pwkv = dout("o_pwkv", [L_, 128, 8, 64])
    o_swkv = dout("o_swkv", [L_, 128, 8, 16, 64])
    o_pffn = dout("o_pffn", [L_, 128, 48, 2])
    o_sffn = dout("o_sffn", [L_, 128, 48, 16, 2])

    if tiles is None:
        tiles = [Geo("p", 512 * i, i, first=(i == 0), last=(i == 3)) for i in range(4)] + [Geo("s", 2048, 4)]

    XT = P.sb("XT", [128, 8, 512], F32)
    H = P.sb("H", [128, 8, 512], BF16)
    NSLOT = 8
    WB = [P.sb("WB%d" % i, [128, 1024], BF16) for i in range(NSLOT)]
    UB = [P.sb("UB%d" % i, [128, 544], F32) for i in range(4)]
    CA = P.sb("CA", [128, 4, 512], BF16)
    BF = P.sb("BF", [128, 16, 512], BF16)
    LWT = P.sb("LWT", [128, 1024], BF16)
    G2A = P.sb("G2A", [128, 1024], BF16)
    G2B = P.sb("G2B", [128, 1024], BF16)
    TWL = P.sb("TWL", [128, 512], BF16)
    SGL = P.sb("SGL", [128, 512], BF16)
    SGL2 = P.sb("SGL2", [128, 512], BF16)
    T = [P.sb("T%d" % i, [128, 544], F32) for i in range(12)]
    TB = [P.sb("TBh%d" % i, [128, 512], BF16) for i in range(2)]

    class HPS:
        pass

    def mk_set(i, Tl, psb):
        S = HPS()
        S.T = Tl
        S.TB = [P.sb("hTB%d_%d" % (i, j), [128, 512], BF16) for j in range(2)] if i else TB
        S.AR = P.sb("AR%d" % i, [128, 8, 128], BF16)
        S.BK = P.sb("BK%d" % i, [128, 8, 128], BF16)
        S.TOK = P.sb("TOK%d" % i, [128, 8, 3, 64], BF16)
        S.AM = P.sb("AM%d" % i, [128, 8, 320], BF16)
        S.CH = [P.sb("CH%d_%d" % (i, j), [128, 8, 64], BF16) for j in range(4)]
        S.TTa = P.sb("TTa%d" % i, [128, 8, 64], BF16)
        S.TTb = P.sb("TTb%d" % i, [128, 8, 64], BF16)
        S.RHSb = P.sb("RHSb%d" % i, [128, 64], BF16)
        S.Ub = P.sb("Ub%d" % i, [128, 64], BF16)
        S.VBF = [P.sb("VBF%d_%d" % (i, j), [128, 512], BF16) for j in range(3)]
        S.PSB = psb
        return S

    PS = [P.ps("PS%d" % i, [128, 512], F32) for i in range(8)]
    T1 = {j: P.sb("hT1_%d" % j, [128, 544], F32) for j in (0, 2, 3, 4, 5, 6, 7, 8, 9, 10)}
    SET0 = mk_set(0, {j: T[j] for j in range(12)}, PS[0:4])
    SET1 = mk_set(1, T1, PS[4:8])
    SETS = mk_set
    STpb = P.sb("STpb", [128, 8, 64], BF16)
    S0 = P.sb("S0", [128, 16, 64], F32)
    S0b = P.sb("S0b", [128, 16, 64], BF16)
    STS = P.sb("STS", [128, 27, 16], F32)
    STF = P.sb("STF", [128, 48, 16, 2], F32)
    CHALO = P.sb("CHALO", [128, L_, 4, 30], F32)
    CSH = P.sb("CSH", [128, L_, 27, 1], F32)
    STp = P.sb("STp", [128, L_, 8, 64], F32)
    CFF = P.sb("CFF", [128, L_, 48, 2], F32)
    PVs = [P.sb("PV%d" % i, [128, NPV], F32) for i in range(1)]
    DV = P.sb("DV", [128, NDV], F32)
    PVF = P.sb("PVF", [128, 8], F32)
    CST = P.sb("CST", [128, NCST], F32)
    CSTB = P.sb("CSTB", [128, NCSTB], BF16)
    IDBb = P.sb("IDBb", [128, 64], BF16)
    IDENTB = P.sb("IDENTB", [128, 128], BF16)
    DIAG = [P.sb("DIAG%d" % i, [128, 128], BF16) for i in range(8)]
    ONES = P.sb("ONES", [128, 128], BF16)
    ONEB = P.sb("ONEB", [128, 128], BF16)
    ONEB64 = P.sb("ONEB64", [128, 128], BF16)

    mask_p = CSTB[:, C_MP:C_MP + 320]
    mask_s = CSTB[:, C_MS:C_MS + 320]
    rm_p = CSTB[:, C_RMP:C_RMP + 512]
    rm_s = CSTB[:, C_RMS:C_RMS + 64]
    seqm = CSTB[:, C_SEQM:C_SEQM + 1024].r("p (q t) -> p q t", q=16)
    seqmt = CSTB[:, C_SEQMT:C_SEQMT + 16]

    dumps = {}

    def dumpv(name, view, shape):
        if dump is None or name not in dump:
            return
        d = dout("dbg_" + name, shape)
        P.dma(d, view, eng="pool")
        dumps[name] = shape

    def wsrc(kind, l, a, b=None):
        if kind == "in":
            idx = a // 128 if a < 4352 else (34 if a == 4352 else 35 + (a - 4384) // 128)
            return w_in_d[l, idx].rearrange("p (k m) -> p k m", k=8), 8, 128
        if kind == "co":
            return wco_d[l, a // 128].rearrange("p (k m) -> p k m", k=4), 4, 128
        if kind == "ro":
            return wro_d[l, a // 128].rearrange("p (k m) -> p k m", k=8), 8, 128
        if kind == "mx":
            return wmx_d[l, a // 128].rearrange("p (k m) -> p k m", k=8), 8, 128
        if kind == "up":
            return wup_d[l, a // 128].rearrange("p (k m) -> p k m", k=8), 8, 128
        if kind == "dn":
            return wdn_d[l, a * 8 + b // 128].rearrange("p (k m) -> p k m", k=8), 8, 128
        raise ValueError(kind)

    class WS:
        issued = 0
        taken = 0

    def w_issue():
        i = WS.issued
        if i >= len(sched):
            return
        src, K, M = wsrc(*sched[i])
        dst = WB[i % NSLOT][:, 0:K * M].r("p (k m) -> p k m", k=K)
        P.dma(dst, src, eng="pool")
        WS.issued += 1

    def w_next(*unit):
        i = WS.taken
        if dry:
            sched.append(unit)
        else:
            assert sched[i] == unit, (i, sched[i], unit)
            while WS.issued < min(len(sched), i + NSLOT - 1):
                w_issue()
        _, K, M = wsrc(*unit)
        WS.taken += 1
        return WB[i % NSLOT][:, 0:K * M].r("p (k m) -> p k m", k=K)

    def v3(v, g):
        return v.r("p (s t) -> p s t", s=g.nseq)

    def hv(tile, M, g, h):
        return tile[:M, 0:g.nseq * (h + g.L)].r("p (s t) -> p s t", s=g.nseq)

    def proj(ps_view, w, M, N):
        for k in range(8):
            P.mm(ps_view, w[:, k, 0:M], H[:, k, :N], start=(k == 0), stop=(k == 7))

    def rmsnorm(xc, N):
        for c in range(8):
            sq = TB[c % 2][:, :N]
            P.act(sq, xc[c], AF.Square)
            P.mm(PS[2][:, :N], ONES[:], sq, start=(c == 0), stop=(c == 7))
        t = T[10][:, :N]
        P.act(t, PS[2][:, :N], AF.Ln, scale=1.0 / D, bias=RMS_EPS)
        P.act(t, t, AF.Exp, scale=-0.5)
        return t

    P.dma(CST[:], cst_d)
    P.dma(CSTB[:], cstb_d)
    P.dma(PVF[:], pvf_d)
    P.act(IDBb[:], CST[:, C_IDB:C_IDB + 64], AF.Identity)
    P.act(IDENTB[:], CST[:, C_ID:C_ID + 128], AF.Identity)
    P.memset(ONES[:], 1.0)
    P.memset(CSH[:], 0.0)
    P.memset(G2B[:], 0.0)
    P.memset(SGL2[:], 0.0)
    P.memset(ONEB[:], 0.0)
    P.memset(ONEB[0:64, 0:64], 1.0)
    P.memset(ONEB[64:128, 64:128], 1.0)
    P.act(ONEB64[:], ONEB[:], AF.Identity, scale=1.0 / 64)

    class Cur:
        PV = None
        npass = 0

    def layer_setup(l):
        PV = PVs[0]
        Cur.PV = PV
        Cur.npass += 1
        P.dma(PV[:], pv_d[l])
        P.dma(LWT[:], lw_d[l], eng="pool")
        P.dma(G2A[:], g2_d[l][0:128, :], eng="pool")
        P.dma(G2B[0:32, :], g2_d[l][128:160, :], eng="pool")
        P.ts(DV[:, CLGH:CLGH + 4], PV[:, CLG:CLG + 4], 0.5, ALU.mult)
        P.ts(DV[:, CLBH:CLBH + 4], PV[:, CLB:CLB + 4], 0.5, ALU.mult)
        P.ts(DV[:, W0H:W0H + 8], PV[:, W0:W0 + 8], 0.5, ALU.mult)
        P.ts(DV[:, A0H:A0H + 8], PV[:, A0:A0 + 8], 0.5, ALU.mult)
        P.ts(DV[:, KAH:KAH + 8], PV[:, KA:KA + 8], 0.5, ALU.mult)
        P.ts(DV[:, KAB:KAB + 8], PV[:, KA:KA + 8], -0.5, ALU.mult, 1.0, ALU.add)
        P.ts(DV[:, OMM:OMM + 27], PV[:, MU:MU + 27], -1.0, ALU.mult, 1.0, ALU.add)

    def shift_chunk(l, g, ps_view, q, M, out_xs, zs_tile, d_tile):
        N, L = g.N, g.L
        PV = Cur.PV
        z3 = hv(zs_tile, M, g, 1)
        if g.kind == "p":
            if g.first:
                P.memset(z3[:, :, 0:1], 0.0)
            else:
                P.copy(z3[:, :, 0:1], CSH[:M, l, q:q + 1, :], eng="act")
        else:
            P.copy(z3[:, :, 0:1], STS[:M, q, :].un(2), eng="act")
        P.act(z3[:, :, 1:1 + L], v3(ps_view, g), AF.Identity)
        d3 = v3(d_tile[:M, :N], g)
        P.act(d3, v3(ps_view, g), AF.Identity, scale=DV[:M, OMM + q:OMM + q + 1])
        P.stt(v3(out_xs, g), z3[:, :, 0:L], PV[:M, MU + q:MU + q + 1], d3, ALU.mult, ALU.add)
        if g.kind == "p":
            P.copy(CSH[:M, l, q:q + 1, :], z3[:, :, L:L + 1], eng="act")
        else:
            P.copy(STS[:M, q, :].un(2), z3[:, :, L:L + 1], eng="act")

    def wkv(l, g, hp, S, XV, BHF, KHF, WC):
        N, nblk, Q = g.N, g.nblk, g.nseq
        HS = [slice(0, 64), slice(64, 128)]
        prompt = g.kind == "p"
        nlev = 5 if prompt else 1
        MASK = mask_p if prompt else mask_s
        AR, BK, TOK, AM, CH, TTa, TTb, RHSb, Ub, PSB = S.AR, S.BK, S.TOK, S.AM, S.CH, S.TTa, S.TTb, S.RHSb, S.Ub, S.PSB
        Tt = S.T
        for b in range(nblk):
            cb = slice(b * 64, (b + 1) * 64)
            pa = PSB[b % 2]
            ptb = PSB[2 + (b % 2)][:, 0:96].bitcast(BF16)
            for qi, src in enumerate((XV, BHF, KHF)):
                for hs in HS:
                    P.tr(ptb[hs, qi * 64:(qi + 1) * 64], src[hs, cb], IDENTB[hs, hs])
            for hs in HS:
                P.mm(pa[hs, 0:128], BK[hs, b, 0:64], AR[hs, b, :])
            for hs in HS:
                P.mm(pa[hs, 128:256], BK[hs, b, 64:128], AR[hs, b, :])
            for hs in HS:
                P.mm(pa[hs, 256:320], AR[hs, b, 0:64], BK[hs, b, 0:64])
            P.copy(TOK[:, b, :, :], ptb.r("p (q i) -> p q i", q=3), eng="act")
            P.tt(AM[:, b, :], pa[:, 0:320], MASK, ALU.mult)
            yield
        yield
        if stop == "W1":
            return
        P.tt(TTa[:, 0:nblk, :], AM[:, 0:nblk, 0:64], IDBb[:].un(1).bc([128, nblk, 64]), ALU.add)
        Xp = AM[:, :, 0:64]
        Pp = AM[:, :, 256:320]
        TTp, TTn = TTa, TTb
        for k in range(1, nlev + 1):
            Pk = CH[(k % 2) * 2]
            Xk = CH[(k % 2) * 2 + 1]
            for b in range(nblk):
                cb = slice(b * 64, (b + 1) * 64)
                for hs in HS:
                    P.mm(PSB[0][hs, cb], Xp[hs, b, :], Pp[hs, b, :])
                if k < nlev:
                    for hs in HS:
                        P.mm(PSB[1][hs, cb], Pp[hs, b, :], Xp[hs, b, :])
            P.copy(Pk[:, 0:nblk, :], PSB[0][:, 0:nblk * 64].r("p (b i) -> p b i", b=nblk), eng="act")
            if k < nlev:
                P.copy(Xk[:, 0:nblk, :], PSB[1][:, 0:nblk * 64].r("p (b i) -> p b i", b=nblk))
            yield
            for b in range(nblk):
                cb = slice(b * 64, (b + 1) * 64)
                for hs in HS:
                    P.mm(PSB[2][hs, cb], Pk[hs, b, :], TTp[hs, b, :])
            P.tt(TTn[:, 0:nblk, :], TTp[:, 0:nblk, :],
                 PSB[2][:, 0:nblk * 64].r("p (b i) -> p b i", b=nblk), ALU.add)
            yield
            Xp, Pp = Xk, Pk
            TTp, TTn = TTn, TTp
        TTf = TTp
        if stop == "W2":
            return
        if not prompt:
            def bfv(t):
                return t[:, 0:512].bitcast(BF16).r("p (q i) -> p q i", q=16)
            AMSK, RMSK, BHM, KHM = bfv(Tt[0]), bfv(Tt[1]), bfv(Tt[2]), bfv(Tt[3])
            S0w = [Tt[11][:, 0:512].r("p (q i) -> p q i", q=8), Tt[5][:, 0:512].r("p (q i) -> p q i", q=8)]
            P.dma(S0[:], stw_d[l][:, hp])
            P.act(S0b[:], S0[:], AF.Identity)
            P.tt(AMSK, AR[:, 0, 0:64].un(1).bc([128, 16, 64]), seqm, ALU.mult)
            P.tt(RMSK, AR[:, 0, 64:128].un(1).bc([128, 16, 64]), seqm, ALU.mult)
            P.tt(BHM, TOK[:, 0, 1, :].un(1).bc([128, 16, 64]), seqmt.un(2).bc([128, 16, 64]), ALU.mult)
            P.tt(KHM, TOK[:, 0, 2, :].un(1).bc([128, 16, 64]), seqmt.un(2).bc([128, 16, 64]), ALU.mult)
            wc3 = WC.r("p (s t) -> p s t", s=16)
            for rnd in range(2):
                P.tt(S0w[rnd], S0[:, rnd * 8:rnd * 8 + 8, :], wc3[:, rnd * 8:rnd * 8 + 8, 3:4].bc([128, 8, 64]), ALU.mult)
        stv = STp.v((slice(None), l, hp, slice(None)), (l, hp))
        spb = STpb.v((slice(None), hp, slice(None)), hp)
        LB = PSB[0]
        for b in range(nblk):
            cb = slice(b * 64, (b + 1) * 64)

            def ops(h2):
                hs = HS[h2]
                if prompt:
                    return ([AR[hs, b, 0:64]], [AR[hs, b, 64:128]], [TOK[hs, b, 1, :]], [TOK[hs, b, 2, :]],
                            [STpb.v((hs, hp, slice(None)), hp)])
                return ([AMSK[hs, q, :] for q in range(Q)], [RMSK[hs, q, :] for q in range(Q)],
                        [BHM[hs, q, :] for q in range(Q)], [KHM[hs, q, :] for q in range(Q)],
                        [S0b[hs, q, :] for q in range(Q)])
            seqs = []
            for h2 in range(2):
                hs = HS[h2]
                a_, r_, bh_, kh_, s_ = ops(h2)
                sq = [(LB[hs, 0:64], a_[q], s_[q], q == 0, False) for q in range(Q)]
                sq.append((LB[hs, 0:64], AM[hs, b, 128:192], TOK[hs, b, 0, :], False, True))
                seqs.append(sq)
            for i in range(len(seqs[0])):
                for sq in seqs:
                    o_, l_, r2_, st_, sp_ = sq[i]
                    P.mm(o_, l_, r2_, start=st_, stop=sp_)
            P.copy(RHSb[:], LB[:, 0:64], eng="act")
            yield
            for hs in HS:
                P.mm(LB[hs, 64:128], TTf[hs, b, :], RHSb[hs, :])
            P.copy(Ub[:], LB[:, 64:128], eng="act")
            yield
            seqs = []
            for h2 in range(2):
                hs = HS[h2]
                a_, r_, bh_, kh_, s_ = ops(h2)
                YTb = PSB[3][hs, cb]
                Vt = TOK[hs, b, 0, :]
                sq = [(YTb, s_[q], r_[q], q == 0, False) for q in range(Q)]
                sq.append((YTb, Ub[hs, :], AM[hs, b, 64:128], False, False))
                sq.append((YTb, Vt, AM[hs, b, 192:256], False, True))
                seqs.append(sq)
            for i in range(len(seqs[0])):
                for sq in seqs:
                    o_, l_, r2_, st_, sp_ = sq[i]
                    P.mm(o_, l_, r2_, start=st_, stop=sp_)
            for rnd in range((Q + 7) // 8):
                nq = min(8, Q - rnd * 8)
                SNB = LB if prompt else PS[4]
                c0 = 128 if prompt else 0
                hops = [ops(h2) for h2 in range(2)]
                for qq in range(nq):
                    q = rnd * 8 + qq
                    for h2 in range(2):
                        hs = HS[h2]
                        P.mm(SNB[hs, c0 + qq * 64:c0 + (qq + 1) * 64], hops[h2][2][q], Ub[hs, :], start=True, stop=False)
                    for h2 in range(2):
                        hs = HS[h2]
                        P.mm(SNB[hs, c0 + qq * 64:c0 + (qq + 1) * 64], hops[h2][3][q], TOK[hs, b, 0, :], start=False, stop=True)
                if prompt:
                    if b < nblk - 1:
                        P.stt(spb, stv, WC[:, b * 64 + 63:b * 64 + 64], SNB[:, 128:192], ALU.mult, ALU.add)
                    P.stt(stv, stv, WC[:, b * 64 + 63:b * 64 + 64], SNB[:, 128:192], ALU.mult, ALU.add)
                else:
                    P.tt(S0[:, rnd * 8:rnd * 8 + nq, :], S0w[rnd],
                         SNB[:, 0:nq * 64].r("p (q i) -> p q i", q=nq), ALU.add)
            yield
        if not prompt:
            P.dma(o_swkv[l][:, hp], S0[:])

    def hp_gen(l, g, hp, S):
        N = g.N
        prompt = g.kind == "p"
        PV = Cur.PV
        Tt, TBs, AR, BK, VBF, PSB = S.T, S.TB, S.AR, S.BK, S.VBF, S.PSB
        hc = slice(hp * 128, (hp + 1) * 128)
        XR, XK, XV = Tt[2][:, :N], Tt[3][:, :N], Tt[4][:, :N]
        if prompt:
            stv_ = STp.v((slice(None), l, hp, slice(None)), (l, hp))
            spb_ = STpb.v((slice(None), hp, slice(None)), hp)
            if g.first:
                P.memset(stv_, 0.0)
                P.memset(spb_, 0.0)
            else:
                P.copy(spb_, stv_, eng="act")
        for i, (q, xs) in enumerate(((hp, XR), (8 + hp, XK), (16 + hp, XV))):
            w = w_next("in", l, 1024 + q * 128, 128)
            ps = PSB[i]
            proj(ps[:, :N], w, 128, N)
            shift_chunk(l, g, ps[:, :N], q, 128, xs, Tt[(0, 6, 8)[i]], Tt[(5, 7, 9)[i]])
            yield
        if stop == "Cb":
            return
        P.mm(PSB[0][:, :N], LWT[0:64, hc], TWL[0:64, :N])
        SG = Tt[5][:, :N]
        P.act(SG, PSB[0][:, :N], AF.Tanh, scale=0.5, bias=DV[:, W0H + hp:W0H + hp + 1])
        P.ts(SG, SG, 1.0, ALU.add)
        CS = Tt[7][:, :N]
        P.scan(CS, (rm_p if prompt else rm_s)[:, :N], SG, 0.0, ALU.mult, ALU.add)
        cs3 = CS.r("p (s t) -> p s t", s=g.nb)
        CSE = Tt[8][:, :N]
        P.tt(CSE.r("p (s t) -> p s t", s=g.nb), cs3[:, :, g.Lb - 1:g.Lb].bc([128, g.nb, g.Lb]), cs3, ALU.subtract)
        P.tt(SG, CS, SG, ALU.subtract)
        WI = Tt[9][:, :N]
        P.act(WI, CS, AF.Exp, scale=H0)
        P.act(CS, CS, AF.Exp, scale=-H0)
        P.act(SG, SG, AF.Exp, scale=-H0)
        P.act(CSE, CSE, AF.Exp, scale=-H0)
        WC, WM, WE = CS, SG, CSE
        yield
        P.mm(PSB[1][:, :N], LWT[64:128, hc], TWL[64:128, :N])
        THA = Tt[10][:, :N]
        P.act(THA, PSB[1][:, :N], AF.Tanh, scale=0.5, bias=DV[:, A0H + hp:A0H + hp + 1])
        ksq = TBs[0][:, :N]
        P.act(ksq, XK, AF.Square, scale=PV[:, KK + hp:KK + hp + 1])
        P.mm(PSB[1][:, :N], ONEB[:], ksq)
        NR = Tt[0][:, :N]
        P.act(NR, PSB[1][:, :N], AF.Ln, bias=1e-24)
        P.act(NR, NR, AF.Exp, scale=-0.5)
        KKN = Tt[6][:, :N]
        P.stt(KKN, XK, PV[:, KK + hp:KK + hp + 1], NR, ALU.mult, ALU.mult)
        yield
        nbk = g.nblk
        ar3a = AR[:, 0:nbk, 0:64]
        ar3r = AR[:, 0:nbk, 64:128]
        bk3b = BK[:, 0:nbk, 0:64]
        bk3k = BK[:, 0:nbk, 64:128]

        def b3(v):
            return v.r("p (b t) -> p b t", b=nbk)
        P.stt(ar3a, b3(KKN), -1.0, b3(WM), ALU.mult, ALU.mult)
        P.tt(ar3r, b3(XR), b3(WC), ALU.mult)
        B2 = Tt[5][:, :N]
        P.stt(B2, THA, 1.0, KKN, ALU.add, ALU.mult)
        P.stt(bk3b, b3(B2), 0.5, b3(WI), ALU.mult, ALU.mult)
        BHF = VBF[1][:, :N]
        P.stt(BHF, B2, 0.5, WE, ALU.mult, ALU.mult)
        yield
        KF = Tt[5][:, :N]
        P.act(KF, THA, AF.Identity, scale=DV[:, KAH + hp:KAH + hp + 1], bias=DV[:, KAB + hp:KAB + hp + 1])
        P.tt(KF, KF, XK, ALU.mult)
        P.tt(bk3k, b3(KF), b3(WI), ALU.mult)
        KHF = VBF[2][:, :N]
        P.tt(KHF, KF, WE, ALU.mult)
        VB = VBF[0][:, :N]
        P.act(VB, XV, AF.Identity)
        rkb = TBs[1][:, :N]
        P.stt(rkb, XR, PV[:, RK + hp:RK + hp + 1], KF, ALU.mult, ALU.mult)
        P.mm(PSB[0][:, :N], ONEB[:], rkb)
        BON = Tt[9][:, :N]
        P.tt(BON, PSB[0][:, :N], XV, ALU.mult)
        P.mm(PSB[1][:, :N], G2A[:, hc], SGL[:, :N], start=True, stop=False)
        P.mm(PSB[1][:, :N], G2B[:, hc], SGL2[:, :N], start=False, stop=True)
        GP = Tt[10][:, :N]
        P.act(GP, PSB[1][:, :N], AF.Identity)
        yield
        for _ in wkv(l, g, hp, S, VB, BHF, KHF, WC):
            yield
        if stop in ("W1", "W2"):
            return
        YS = Tt[5][:, :N]
        P.act(YS, PSB[3][:, :N], AF.Identity)
        if stop == "C1" and hp == 0:
            dumpv("YS", YS, [128, N])
            return
        yb_, y2_ = TBs[0][:, :N], TBs[1][:, :N]
        P.act(yb_, YS, AF.Identity)
        P.act(y2_, YS, AF.Square)
        P.mm(PSB[0][:, :N], ONEB64[:], yb_)
        P.mm(PSB[1][:, :N], ONEB64[:], y2_)
        yield
        VR = Tt[6][:, :N]
        MS = Tt[7][:, :N]
        P.act(MS, PSB[0][:, :N], AF.Identity)
        P.tt(VR, MS, MS, ALU.mult)
        P.tt(VR, PSB[1][:, :N], VR, ALU.subtract)
        P.act(VR, VR, AF.Ln, bias=GN_EPS)
        P.act(VR, VR, AF.Exp, scale=-0.5)
        P.tt(YS, YS, MS, ALU.subtract)
        P.tt(YS, YS, VR, ALU.mult)
        P.act(YS, YS, AF.Identity, scale=PV[:, LNG + hp:LNG + hp + 1], bias=PV[:, LNB + hp:LNB + hp + 1])
        P.tt(YS, YS, BON, ALU.add)
        P.stt(BF.v((slice(None), 8 + hp, slice(0, N)), 8 + hp), YS, 0.5, GP, ALU.mult, ALU.mult)
        yield

    def run_skewed(fns, sets, lag):
        pending = list(fns)
        free = list(sets)
        active = []
        while pending or active:
            if pending and free and (not active or min(a[2] for a in active) >= lag):
                S = free.pop(0)
                active.append([pending.pop(0)(S), S, 0])
            for a in list(active):
                try:
                    next(a[0])
                    a[2] += 1
                except StopIteration:
                    active.remove(a)
                    free.append(a[1])

    def run_gens(gens):
        active = list(gens)
        while active:
            for gen in list(active):
                try:
                    next(gen)
                except StopIteration:
                    active.remove(gen)

    def run_pass(l, g):
        N, L, ti = g.N, g.L, g.ti
        prompt = g.kind == "p"
        PV = Cur.PV
        xc = [XT.v((slice(None), c, slice(0, N)), c) for c in range(8)]
        if not prompt:
            P.dma(STS[:], sts_d[l])
            P.dma(STF[:], stf_d[l])
        rstd = rmsnorm(xc, N)
        for c in range(8):
            P.stt(H[:, c, :N], xc[c], PV[:, NMG + c:NMG + c + 1], rstd, ALU.mult, ALU.mult)
        if stop == "A":
            return
        def conv_proj(cc):
            pa_, pb_ = PS[2 * (cc % 2)], PS[2 * (cc % 2) + 1]
            wa = w_next("in", l, cc * 128, 128)
            proj(pa_[:, :N], wa, 128, N)
            wb = w_next("in", l, 512 + cc * 128, 128)
            proj(pb_[:, :N], wb, 128, N)

        conv_proj(0)
        for cc in range(4):
            pa_, pb_ = PS[2 * (cc % 2)], PS[2 * (cc % 2) + 1]
            th, zah = T[4 + (cc % 2)][:, :N], T[6 + (cc % 2)][:, :N]
            P.act(th, pb_[:, :N], AF.Tanh, scale=0.5)
            P.act(zah, pa_[:, :N], AF.Identity, scale=0.5)
            if cc < 3:
                conv_proj(cc + 1)
            u3 = hv(UB[cc], 128, g, 30)
            if prompt:
                if g.first:
                    P.memset(u3[:, :, 0:30], 0.0)
                else:
                    P.copy(u3[:, :, 0:30], CHALO[:, l, cc, :].un(1), eng="act")
            else:
                P.dma(u3[:, :, 0:30], stc_d[l][:, cc])
            P.stt(u3[:, :, 30:30 + L], v3(th, g), 1.0, v3(zah, g), ALU.add, ALU.mult)
            a3 = v3(T[cc][:, :N], g)
            if prompt:
                ubf = T[8 + (cc % 2)][:, 0:272].bitcast(BF16)
                P.act(ubf[:, 0:30 + L], UB[cc][:, 0:30 + L], AF.Identity)
                for j in range(31):
                    dg = DIAG[j % 8]
                    P.ts(dg[:], IDENTB[:], PV[:, CDW + cc * 31 + j:CDW + cc * 31 + j + 1], ALU.mult)
                    P.mm(PS[4 + (cc % 2)][:, :N], dg[:], ubf[:, j:j + L], start=(j == 0), stop=(j == 30))
                P.act(T[cc][:, :N], PS[4 + (cc % 2)][:, :N], AF.Identity, bias=PV[:, CDB + cc:CDB + cc + 1])
            else:
                P.ts(a3, u3[:, :, 0:L], PV[:, CDW + cc * 31:CDW + cc * 31 + 1], ALU.mult,
                     PV[:, CDB + cc:CDB + cc + 1], ALU.add)
                for j in range(1, 31):
                    P.stt(a3, u3[:, :, j:j + L], PV[:, CDW + cc * 31 + j:CDW + cc * 31 + j + 1], a3, ALU.mult, ALU.add)
            cb_, c2_ = TB[0][:, :N], TB[1][:, :N]
            P.act(cb_, T[cc][:, :N], AF.Identity)
            P.act(c2_, T[cc][:, :N], AF.Square)
            P.mm(PS[6][:, :N], ONES[:], cb_, start=(cc == 0), stop=(cc == 3))
            P.mm(PS[7][:, :N], ONES[:], c2_, start=(cc == 0), stop=(cc == 3))
            if prompt:
                P.copy(CHALO[:, l, cc, :].un(1), u3[:, :, L:L + 30], eng="act")
            else:
                P.dma(o_sconv[l][:, cc], u3[:, :, 4:34])
        if prompt and g.last:
            P.dma(o_pconv[l], CHALO[:, l])
        mean, var = T[6][:, :N], T[7][:, :N]
        P.ts(mean, PS[6][:, :N], 1.0 / 512, ALU.mult)
        P.tt(var, mean, mean, ALU.mult)
        P.stt(var, PS[7][:, :N], 1.0 / 512, var, ALU.mult, ALU.subtract)
        P.act(var, var, AF.Ln, bias=LN_EPS)
        P.act(var, var, AF.Exp, scale=-0.5)
        for cc in range(4):
            a = T[cc][:, :N]
            P.tt(a, a, mean, ALU.subtract)
            P.tt(a, a, var, ALU.mult)
            P.act(a, a, AF.Identity, scale=DV[:, CLGH + cc:CLGH + cc + 1], bias=DV[:, CLBH + cc:CLBH + cc + 1])
            th = T[4 + (cc % 2)][:, :N]
            P.act(th, a, AF.Tanh)
            P.stt(CA[:, cc, :N], th, 1.0, a, ALU.add, ALU.mult)
        if stop == "B":
            dumpv("CA", CA[:, :, :N], [128, 4, N])
            return
        for q, M in ((24, 128), (25, 128), (26, 32)):
            w = w_next("in", l, 1024 + q * 128, 128)
            ps = PS[q % 2]
            proj(ps[:M, :N], w, M, N)
            xs = T[2][:M, :N]
            shift_chunk(l, g, ps[:M, :N], q, M, xs, T[0], T[1])
            if q == 24:
                P.act(TWL[0:64, :N], T[2][0:64, :N], AF.Tanh)
                P.act(TWL[64:128, :N], T[2][64:128, :N], AF.Identity)
            elif q == 25:
                P.act(T[3][:, :N], xs, AF.Tanh, scale=0.5)
                P.ts(SGL[:, :N], T[3][:, :N], 1.0, ALU.add)
            else:
                P.act(T[3][0:32, :N], xs, AF.Tanh, scale=0.5)
                P.ts(SGL2[0:32, :N], T[3][0:32, :N], 1.0, ALU.add)
        if stop == "Ca":
            return
        if prompt:
            run_skewed([(lambda S, hp=hp: hp_gen(l, g, hp, S)) for hp in range(8)], [SET0, SET1], HP_LAG)
        else:
            SS = HPS()
            SS.__dict__.update(SET0.__dict__)
            SS.PSB = PS[0:4]
            for hp in range(8):
                run_gens([hp_gen(l, g, hp, SS)])
        if prompt and g.last:
            P.dma(o_pshift[l], CSH[:, l, :, 0])
            P.dma(o_pwkv[l], STp[:, l])
        if not prompt:
            P.dma(o_sshift[l], STS[:])
        if stop in ("C", "C1", "W1", "W2", "Cb"):
            dumpv("YF", BF[:, 8:16, :N], [128, 8, N])
            dumpv("STp", STp[:, l], [128, 8, 64])
            return
        for m in range(8):
            o = 4 * (m % 2)
            wco = w_next("co", l, m * 128)
            for k in range(4):
                P.mm(PS[o][:, :N], wco[:, k, :], CA[:, k, :N], start=(k == 0), stop=(k == 3))
            wro = w_next("ro", l, m * 128)
            for k in range(8):
                P.mm(PS[o + 1][:, :N], wro[:, k, :], BF.v((slice(None), 8 + k, slice(0, N)), 8 + k), start=(k == 0), stop=(k == 7))
            wg1 = w_next("in", l, 4384 + m * 128, 128)
            proj(PS[o + 2][:, :N], wg1, 128, N)
            wg2 = w_next("in", l, 5408 + m * 128, 128)
            proj(PS[o + 3][:, :N], wg2, 128, N)
            t1, t2 = T[(m % 2) * 2][:, :N], T[(m % 2) * 2 + 1][:, :N]
            P.act(t1, PS[o + 2][:, :N], AF.Tanh, scale=0.5)
            P.act(t2, PS[o + 3][:, :N], AF.Tanh, scale=0.5)
            P.stt(t1, t1, 1.0, PS[o][:, :N], ALU.add, ALU.mult)
            P.stt(t2, t2, 1.0, PS[o + 1][:, :N], ALU.add, ALU.mult)
            P.tt(BF.v((slice(None), m, slice(0, N)), m), t1, t2, ALU.add)
        for mo in range(8):
            w = w_next("mx", l, mo * 128)
            ps = PS[mo % 2]
            for k in range(8):
                P.mm(ps[:, :N], w[:, k, :], BF.v((slice(None), k, slice(0, N)), k), start=(k == 0), stop=(k == 7))
            P.stt(xc[mo], ps[:, :N], 0.5, xc[mo], ALU.mult, ALU.add)
        if stop == "M":
            return
        rstd = rmsnorm(xc, N)
        for c in range(8):
            P.stt(H[:, c, :N], xc[c], PV[:, NFG + c:NFG + c + 1], rstd, ALU.mult, ALU.mult)
        def bfrow(r):
            return BF.v((slice(None), r, slice(0, N)), r)

        def up_gen(part, r0):
            for jj in range(8):
                j = part * 8 + jj
                cus = []
                for half, q in ((0, j), (1, 24 + j)):
                    w = w_next("up", l, q * 128)
                    ps = PS[half + 2 * (jj % 2)]
                    proj(ps[:, :N], w, 128, N)
                    upb = T[half + 2 * (jj % 2)]
                    u3 = hv(upb, 128, g, 2)
                    if prompt:
                        if g.first:
                            P.memset(u3[:, :, 0:2], 0.0)
                        else:
                            P.copy(u3[:, :, 0:2], CFF[:, l, q, :].un(1), eng="act")
                    else:
                        P.copy(u3[:, :, 0:2], STF[:, q, :, :], eng="act")
                    P.act(u3[:, :, 2:2 + L], v3(ps[:, :N], g), AF.Identity)
                    cu = T[4 + half + 2 * (jj % 2)][:, :N]
                    c3 = v3(cu, g)
                    fw0 = FDW + q * 3
                    P.act(c3, v3(ps[:, :N], g), AF.Identity, scale=PV[:, fw0 + 2:fw0 + 3], bias=PV[:, FDB + q:FDB + q + 1])
                    P.stt(c3, u3[:, :, 1:1 + L], PV[:, fw0 + 1:fw0 + 2], c3, ALU.mult, ALU.add)
                    P.stt(c3, u3[:, :, 0:L], PV[:, fw0:fw0 + 1], c3, ALU.mult, ALU.add)
                    if prompt:
                        P.copy(CFF[:, l, q, :].un(1), u3[:, :, L:L + 2], eng="act")
                    else:
                        P.copy(STF[:, q, :, :], u3[:, :, L:L + 2], eng="act")
                    cus.append(cu)
                ga = T[8 + (jj % 2)][:, :N]
                P.act(ga, cus[0], AF.Gelu_apprx_tanh)
                P.tt(bfrow(r0 + jj), ga, cus[1], ALU.mult)
                yield

        def dn_gen(part, r0):
            for mo in range(8):
                w = w_next("dn", l, part, mo * 128)
                ps = PS[4 + (mo % 2)]
                for k in range(8):
                    P.mm(ps[:, :N], w[:, k, :], bfrow(r0 + k), start=(k == 0), stop=(k == 7))
                P.tt(xc[mo], xc[mo], ps[:, :N], ALU.add)
                yield

        run_gens([up_gen(0, 0)])
        for part in range(3):
            gens = [dn_gen(part, (part % 2) * 8)]
            if part < 2:
                gens.append(up_gen(part + 1, ((part + 1) % 2) * 8))
            run_gens(gens)
        if prompt and g.last:
            P.dma(o_pffn[l], CFF[:, l])
        if not prompt:
            P.dma(o_sffn[l], STF[:])

    dbg_x = dout("dbg_xT", [128, 8, NTOK]) if (dump is not None and stop is not None) else None
    if dbg_x is not None:
        dumps["xT"] = [128, 8, NTOK]
    for g in tiles:
        N = g.N
        P.dma(XT[:, :, :N], xT_d[:, :, g.t0:g.t0 + N])
        for l in range(depth):
            layer_setup(l)
            run_pass(l, g)
        xc = [XT.v((slice(None), c, slice(0, N)), c) for c in range(8)]
        if stop is None:
            rstd = rmsnorm(xc, N)
            for c in range(8):
                yo = T[c % 4][:, :N]
                P.stt(yo, xc[c], PVF[:, c:c + 1], rstd, ALU.mult, ALU.mult)
                P.dma(o_y[:, c, g.t0:g.t0 + N], yo)
        elif dbg_x is not None:
            P.dma(dbg_x[:, :, g.t0:g.t0 + N], XT[:, :, :N])
    P.finish()
    if not dry:
        P.emit()
    return nc, P, dumps


def _pm(v):
    return np.ascontiguousarray(v.reshape(-1, 128).T)


def pack_params(inp):
    pv = np.zeros((DEPTH, 128, NPV), np.float32)
    for l in range(DEPTH):
        pv[l, :, NMG:NMG + 8] = _pm(inp["norm_mix_g"][l])
        pv[l, :, NFG:NFG + 8] = _pm(inp["norm_ffn_g"][l])
        cw = inp["conv_dw_w"][l]
        pv[l, :, CDW:CDW + 124] = cw.reshape(31, 4, 128).transpose(2, 1, 0).reshape(128, 124)
        pv[l, :, CDB:CDB + 4] = _pm(inp["conv_dw_b"][l])
        pv[l, :, CLG:CLG + 4] = _pm(inp["conv_ln_g"][l])
        pv[l, :, CLB:CLB + 4] = _pm(inp["conv_ln_b"][l])
        mu = np.zeros(27 * 128, np.float32)
        mu[:3360] = inp["rw_mu"][l]
        pv[l, :, MU:MU + 27] = _pm(mu)
        pv[l, :, W0:W0 + 8] = _pm(inp["rw_w0"][l])
        pv[l, :, A0:A0 + 8] = _pm(inp["rw_a0"][l])
        pv[l, :, KK:KK + 8] = _pm(inp["rw_k_k"][l])
        pv[l, :, KA:KA + 8] = _pm(inp["rw_k_a"][l])
        pv[l, :, RK:RK + 8] = _pm(inp["rw_r_k"][l].reshape(-1))
        pv[l, :, LNG:LNG + 8] = _pm(inp["rw_ln_g"][l])
        pv[l, :, LNB:LNB + 8] = _pm(inp["rw_ln_b"][l])
        fw_ = inp["ffn_dw_w"][l]
        pv[l, :, FDW:FDW + 144] = fw_.reshape(3, 48, 128).transpose(2, 1, 0).reshape(128, 144)
        pv[l, :, FDB:FDB + 48] = _pm(inp["ffn_dw_b"][l])
    return pv


def make_in_maps(inp, cores=None):
    cst, cstb = make_consts()
    pv = pack_params(inp)
    pvf = _pm(inp["norm_final_g"])
    lw = np.ascontiguousarray(np.concatenate([inp["rw_w2"], inp["rw_a2"]], axis=1))
    L_ = DEPTH
    wi = inp["w_in"]
    w_in_u = np.zeros((L_, 51, 128, 8, 128), np.float32)
    w_in_u[:, 0:34] = wi[:, :, 0:4352].reshape(L_, 8, 128, 34, 128).transpose(0, 3, 2, 1, 4)
    w_in_u[:, 34, :, :, 0:32] = wi[:, :, 4352:4384].reshape(L_, 8, 128, 32).transpose(0, 2, 1, 3)
    w_in_u[:, 35:51] = wi[:, :, 4384:6432].reshape(L_, 8, 128, 16, 128).transpose(0, 3, 2, 1, 4)

    def units(w, K):
        n = w.shape[2] // 128
        return np.ascontiguousarray(w.reshape(L_, K, 128, n, 128).transpose(0, 3, 2, 1, 4)).reshape(L_, n, 128, K * 128)
    w_dn_u = np.ascontiguousarray(inp["w_down"].reshape(L_, 3, 8, 128, 8, 128).transpose(0, 1, 4, 3, 2, 5)).reshape(L_, 24, 128, 1024)
    shared = dict(cst=cst, cstb=cstb, pv=pv, pvf=pvf, lw=lw, g2=inp["rw_g2"],
                  w_in_u=w_in_u.reshape(L_, 51, 128, 1024),
                  w_co_u=units(inp["w_conv_out"], 4), w_ro_u=units(inp["w_rw_out"], 8),
                  w_mx_u=units(inp["w_mix_out"], 8), w_up_u=units(inp["w_up"], 8), w_dn_u=w_dn_u)
    maps = []
    for c in (range(NCORE) if cores is None else cores):
        sb = slice(16 * c, 16 * c + 16)
        xs = np.concatenate([inp["x_prompt"][c], inp["x_sample"][sb].reshape(64, D)], axis=0)
        xT = np.ascontiguousarray(xs.T.reshape(8, 128, NTOK).transpose(1, 0, 2))
        stc = np.ascontiguousarray(inp["state_conv"][:, sb].reshape(DEPTH, 16, 30, 4, 128).transpose(0, 4, 3, 1, 2))
        ss = np.zeros((DEPTH, 16, 27 * 128), np.float32)
        ss[:, :, :3360] = inp["state_shift"][:, sb]
        sts = np.ascontiguousarray(ss.reshape(DEPTH, 16, 27, 128).transpose(0, 3, 2, 1))
        sw = inp["state_wkv"][:, sb].reshape(DEPTH, 16, 8, 2, 64, 64)
        stw = np.ascontiguousarray(sw.transpose(0, 3, 5, 2, 1, 4).reshape(DEPTH, 128, 8, 16, 64))
        stf = np.ascontiguousarray(inp["state_ffn"][:, sb].reshape(DEPTH, 16, 2, 48, 128).transpose(0, 4, 3, 1, 2))
        m = dict(shared)
        m.update(xT=xT, stc=stc, sts=sts, stw=stw, stf=stf)
        maps.append(m)
    return maps


def assemble(results):
    L_ = DEPTH
    y_prompt = np.zeros((8, SEQ, D), np.float32)
    y_sample = np.zeros((128, 4, D), np.float32)
    p_conv = np.zeros((L_, 8, 30, 512), np.float32)
    p_shift = np.zeros((L_, 8, 3360), np.float32)
    p_wkv = np.zeros((L_, 8, 16, 64, 64), np.float32)
    p_ffn = np.zeros((L_, 8, 2, 6144), np.float32)
    s_conv = np.zeros((L_, 128, 30, 512), np.float32)
    s_shift = np.zeros((L_, 128, 3360), np.float32)
    s_wkv = np.zeros((L_, 128, 16, 64, 64), np.float32)
    s_ffn = np.zeros((L_, 128, 2, 6144), np.float32)
    for c, r in enumerate(results):
        sb = slice(16 * c, 16 * c + 16)
        yT = r["o_y"].transpose(1, 0, 2).reshape(D, NTOK)
        y_prompt[c] = yT[:, :SEQ].T
        y_sample[sb] = yT[:, SEQ:].T.reshape(16, 4, D)
        p_conv[:, c] = r["o_pconv"].transpose(0, 3, 2, 1).reshape(L_, 30, 512)
        s_conv[:, sb] = r["o_sconv"].transpose(0, 3, 4, 2, 1).reshape(L_, 16, 30, 512)
        p_shift[:, c] = r["o_pshift"].transpose(0, 2, 1).reshape(L_, 27 * 128)[:, :3360]
        s_shift[:, sb] = r["o_sshift"].transpose(0, 3, 2, 1).reshape(L_, 16, 27 * 128)[:, :, :3360]
        pw = r["o_pwkv"].reshape(L_, 2, 64, 8, 64)
        p_wkv[:, c] = pw.transpose(0, 3, 1, 4, 2).reshape(L_, 16, 64, 64)
        sw = r["o_swkv"].reshape(L_, 2, 64, 8, 16, 64)
        s_wkv[:, sb] = sw.transpose(0, 4, 3, 1, 5, 2).reshape(L_, 16, 16, 64, 64)
        p_ffn[:, c] = r["o_pffn"].transpose(0, 3, 2, 1).reshape(L_, 2, 6144)
        s_ffn[:, sb] = r["o_sffn"].transpose(0, 3, 4, 2, 1).reshape(L_, 16, 2, 6144)
    return (y_prompt, y_sample, p_conv, p_shift, p_wkv, p_ffn, s_conv, s_shift, s_wkv, s_ffn)


def kernel(**inputs):
    inp = {k: np.asarray(v) for k, v in inputs.items()}
    nc, P, _ = build()
    maps = make_in_maps(inp)
    res = run_bass_kernel_spmd(nc, maps, core_ids=list(range(NCORE)))
    return assemble(res.results)
```

```python
import numpy as np
import concourse.bass as bass
import concourse.mybir as mybir

F32 = mybir.dt.float32
BF16 = mybir.dt.bfloat16
AF = mybir.ActivationFunctionType
ALU = mybir.AluOpType


ATTACH_WAIT = 1
SAME_ENG_MASK = 7


class V:
    __slots__ = ("tile", "ap", "sub")

    def __init__(self, tile, ap, sub=None):
        self.tile = tile
        self.ap = ap
        self.sub = sub

    def __getitem__(self, idx):
        return V(self.tile, self.ap[idx], self.sub)

    def k(self, sub):
        return V(self.tile, self.ap, sub)

    def r(self, pat, **kw):
        return V(self.tile, self.ap.rearrange(pat, **kw), self.sub)

    def bc(self, shape):
        return V(self.tile, self.ap.broadcast_to(shape), self.sub)

    def un(self, axis):
        return V(self.tile, self.ap.unsqueeze(axis), self.sub)

    def bitcast(self, dt):
        return V(self.tile, self.ap.bitcast(dt), self.sub)


class Tile:
    def __init__(self, name, handle):
        self.name = name
        self.h = handle

    def __getitem__(self, idx):
        return V(self.name, self.h[idx], None)

    def v(self, idx, sub):
        return V(self.name, self.h[idx], sub)


class Op:
    __slots__ = ("id", "eng", "fn", "dma", "deps", "sem", "semval", "signal", "rank", "prewait")

    def __init__(self, id, eng, fn, dma):
        self.id = id
        self.eng = eng
        self.fn = fn
        self.dma = dma
        self.deps = {}
        self.sem = None
        self.semval = 0
        self.signal = False
        self.rank = 0
        self.prewait = None


class Prog:
    ENGS = ("pe", "act", "dve", "pool", "sp")

    def __init__(self, nc, n_dma_sems=40):
        self.nc = nc
        self.ops = []
        self.state = {}
        self.n_dma_sems = n_dma_sems
        self.sb_bytes = 0

    def sb(self, name, shape, dtype):
        h = self.nc.alloc_sbuf_tensor("sb_" + name, list(shape), dtype)
        n = 1
        for s in shape[1:]:
            n *= s
        self.sb_bytes += n * (4 if dtype == F32 else 2)
        return Tile(name, h)

    def ps(self, name, shape, dtype=F32):
        h = self.nc.alloc_psum_tensor("ps_" + name, list(shape), dtype)
        return Tile(name, h)

    def _collect(self, v, is_write, deps):
        if v is None or not isinstance(v, V) or v.tile is None:
            return
        st = self.state.setdefault(v.tile, {})
        subs = list(st.keys()) if v.sub is None else [s for s in (v.sub, None) if s in st]
        for s in subs:
            w, rs = st[s]
            if w is not None:
                deps[w] = deps.get(w, 0) | (2 if is_write else 1)
            if is_write:
                for r in rs:
                    deps[r] = deps.get(r, 0) | 4

    def _update(self, v, is_write, opid):
        if v is None or not isinstance(v, V) or v.tile is None:
            return
        st = self.state.setdefault(v.tile, {})
        if is_write:
            if v.sub is None:
                st.clear()
            st[v.sub] = [opid, []]
        else:
            if v.sub not in st:
                st[v.sub] = [None, []]
            st[v.sub][1].append(opid)

    def add(self, eng, fn, reads, writes, dma=False):
        op = Op(len(self.ops), eng, fn, dma)
        deps = {}
        for v in reads:
            self._collect(v, False, deps)
        for v in writes:
            self._collect(v, True, deps)
        for v in reads:
            self._update(v, False, op.id)
        for v in writes:
            self._update(v, True, op.id)
        deps.pop(op.id, None)
        op.deps = deps
        self.ops.append(op)
        return op

    @staticmethod
    def _a(x):
        return x.ap if isinstance(x, V) else x

    def mm(self, out, lhsT, rhs, start=True, stop=True):
        a = self._a
        return self.add("pe", lambda e: e.matmul(a(out), a(lhsT), a(rhs), start=start, stop=stop),
                        [lhsT, rhs], [out])

    def tr(self, out, in_, ident):
        a = self._a
        return self.add("pe", lambda e: e.transpose(a(out), a(in_), a(ident)), [in_, ident], [out])

    def act(self, out, in_, func, scale=1.0, bias=0.0, eng="act"):
        a = self._a
        return self.add(eng, lambda e: e.activation(a(out), a(in_), func, bias=a(bias), scale=a(scale)),
                        [in_, scale, bias], [out])

    def tt(self, out, in0, in1, op, eng="dve"):
        a = self._a
        return self.add(eng, lambda e: e.tensor_tensor(a(out), a(in0), a(in1), op), [in0, in1], [out])

    def ts(self, out, in0, s1, op0, s2=None, op1=None, eng="dve"):
        a = self._a
        if op1 is None:
            return self.add(eng, lambda e: e.tensor_scalar(a(out), a(in0), a(s1), None, op0), [in0, s1], [out])
        return self.add(eng, lambda e: e.tensor_scalar(a(out), a(in0), a(s1), a(s2), op0, op1),
                        [in0, s1, s2], [out])

    def stt(self, out, in0, scalar, in1, op0, op1):
        a = self._a
        return self.add("dve", lambda e: e.scalar_tensor_tensor(a(out), a(in0), a(scalar), a(in1), op0, op1),
                        [in0, scalar, in1], [out])

    def scan(self, out, d0, d1, init, op0, op1):
        a = self._a
        return self.add("dve", lambda e: e.tensor_tensor_scan(a(out), a(d0), a(d1), a(init), op0, op1),
                        [d0, d1, init], [out])

    def copy(self, out, in_, eng="dve"):
        a = self._a
        if eng == "act":
            return self.add("act", lambda e: e.copy(a(out), a(in_)), [in_], [out])
        return self.add(eng, lambda e: e.tensor_copy(a(out), a(in_)), [in_], [out])

    def recip(self, out, in_):
        a = self._a
        return self.add("dve", lambda e: e.reciprocal(a(out), a(in_)), [in_], [out])

    def memset(self, out, val, eng="dve"):
        a = self._a
        return self.add(eng, lambda e: e.memset(a(out), val), [], [out])

    def dma(self, out, in_, eng="sp"):
        a = self._a
        return self.add(eng, lambda e: e.dma_start(out=a(out), in_=a(in_)), [in_], [out], dma=True)

    def finish(self, eng="sp"):
        op = Op(len(self.ops), eng, lambda e: None, False)
        op.deps = {o.id: 1 for o in self.ops if o.dma}
        self.ops.append(op)

    def emit(self):
        nc = self.nc
        ops = self.ops
        from contextlib import ExitStack
        with ExitStack() as es:
            eng_sem = {e: es.enter_context(nc.semaphore("s_" + e)) for e in self.ENGS}
            dma_sems = [es.enter_context(nc.semaphore("d%d" % i)) for i in range(self.n_dma_sems)]
            nsw = (self.n_dma_sems * 3) // 5
            pools = {True: list(range(0, nsw)), False: list(range(nsw, self.n_dma_sems))}
            dma_cnt = [0] * self.n_dma_sems
            dma_last = [None] * self.n_dma_sems
            kk_ = {True: 0, False: 0}
            for op in ops:
                if op.dma:
                    sw = op.eng == "pool"
                    pl = pools[sw]
                    i = pl[kk_[sw] % len(pl)]
                    kk_[sw] += 1
                    op.prewait = dma_last[i]
                    dma_cnt[i] += 16
                    op.sem = dma_sems[i]
                    op.semval = dma_cnt[i]
                    dma_last[i] = op.id
            for op in ops:
                for d, kind in op.deps.items():
                    dop = ops[d]
                    if dop.dma:
                        continue
                    if dop.eng == op.eng:
                        if op.eng == "pe":
                            continue
                        if not (kind & SAME_ENG_MASK):
                            continue
                    dop.signal = True
            rank = {e: 0 for e in self.ENGS}
            for op in ops:
                if not op.dma and op.signal:
                    rank[op.eng] += 1
                    op.rank = rank[op.eng]
            clocks = [None] * len(ops)
            eng_clock = {e: {} for e in self.ENGS}
            dma_known = {e: set() for e in self.ENGS}
            plans = {e: [] for e in self.ENGS}
            for op in ops:
                ck = eng_clock[op.eng]
                waits = []
                deplist = list(op.deps.items())
                if op.prewait is not None:
                    deplist.append((op.prewait, 3))
                for d, kind in sorted(deplist):
                    dop = ops[d]
                    if dop.dma:
                        if d in dma_known[op.eng]:
                            continue
                        waits.append((dop.sem, dop.semval))
                        dma_known[op.eng].add(d)
                    else:
                        if dop.eng == op.eng and (op.eng == "pe" or not (kind & SAME_ENG_MASK)):
                            continue
                        key = dop.eng
                        if ck.get(key, 0) >= dop.rank:
                            continue
                        waits.append((eng_sem[dop.eng], dop.rank))
                    for kk, vv in clocks[d].items():
                        if ck.get(kk, 0) < vv:
                            ck[kk] = vv
                best = {}
                for s, val in waits:
                    kid = id(s)
                    if kid not in best or best[kid][1] < val:
                        best[kid] = (s, val)
                myck = dict(ck)
                if (not op.dma) and op.signal:
                    myck[op.eng] = op.rank
                clocks[op.id] = myck
                plans[op.eng].append((op, list(best.values())))
            self.n_waits = sum(len(w) for e in plans for _, w in plans[e])
            handles = {"pe": "tensor", "act": "scalar", "dve": "vector", "pool": "gpsimd", "sp": "sync"}
            with nc.Block() as block:
                def run(eng_name):
                    def body(e):
                        for op, waits in plans[eng_name]:
                            attach = None
                            if ATTACH_WAIT and waits and not op.dma and op.eng != "sp":
                                attach = waits[-1]
                                waits = waits[:-1]
                            for s, val in waits:
                                e.wait_ge(s, val)
                            ins = op.fn(e)
                            if ins is None:
                                if attach is not None:
                                    e.wait_ge(*attach)
                                continue
                            if attach is not None:
                                ins._wait_ge(attach[0], attach[1])
                            if op.dma:
                                ins.then_inc(op.sem, 16)
                            elif op.signal:
                                ins.then_inc(eng_sem[eng_name], 1)
                    return body
                block.tensor(run("pe"))
                block.scalar(run("act"))
                block.vector(run("dve"))
                block.gpsimd(run("pool"))
                block.sync(run("sp"))


from concourse.bass_utils import run_bass_kernel_spmd

DEPTH = 4
NCORE = 8
HP_LAG = 0
D = 1024
SEQ = 2048
NTOK = 2112
RMS_EPS = 1e-6
LN_EPS = 1e-5
GN_EPS = 64e-5
H0 = 0.5 * float(np.exp(-0.5))

NMG, NFG, CDW, CDB, CLG, CLB, MU, W0, A0, KK, KA, RK, LNG, LNB, FDW, FDB, NPV = (
    0, 8, 16, 140, 144, 148, 152, 179, 187, 195, 203, 211, 219, 227, 235, 379, 427)
CLGH, CLBH, W0H, A0H, KAH, KAB, OMM, NDV = 0, 4, 8, 16, 24, 32, 40, 67
C_ID, C_IDB, NCST = 0, 128, 192
C_MP, C_MS, C_RMP, C_RMS, C_SEQM, C_SEQMT, NCSTB = 0, 320, 640, 1152, 1216, 2240, 2256


def make_consts():
    import ml_dtypes
    c0 = np.zeros((128, NCST), np.float32)
    c0[:, C_ID:C_ID + 128] = np.eye(128, dtype=np.float32)
    p = np.arange(128) % 64
    c0[:, C_IDB:C_IDB + 64] = (p[:, None] == np.arange(64)[None, :])
    c = np.zeros((128, NCSTB), np.float32)
    t = np.arange(64)[None, :]
    s = p[:, None]
    for base, same in ((C_MP, np.ones((128, 64), bool)), (C_MS, (s // 4) == (t // 4))):
        su = (s < t) & same
        iu = (s <= t) & same
        sl = (s > t) & same
        c[:, base:base + 320] = np.concatenate([su, iu, su, iu, sl], axis=1)
    c[:, C_RMP:C_RMP + 512] = (np.arange(512) % 64 != 0)[None, :]
    c[:, C_RMS:C_RMS + 64] = (np.arange(64) % 4 != 0)[None, :]
    sm = (np.arange(16)[:, None] == (np.arange(64) // 4)[None, :]).astype(np.float32)
    c[:, C_SEQM:C_SEQM + 1024] = sm.reshape(1, 1024)
    c[:, C_SEQMT:C_SEQMT + 16] = ((p // 4)[:, None] == np.arange(16)[None, :])
    return c0, c.astype(ml_dtypes.bfloat16)


class Geo:
    def __init__(self, kind, t0, ti, first=False, last=False):
        self.kind, self.t0, self.ti, self.first, self.last = kind, t0, ti, first, last
        if kind == "p":
            self.N, self.nseq, self.L, self.nb, self.Lb, self.nblk = 512, 1, 512, 8, 64, 8
        else:
            self.N, self.nseq, self.L, self.nb, self.Lb, self.nblk = 64, 16, 4, 16, 4, 1


def build(depth=DEPTH, tiles=None, dump=None, stop=None):
    sched = []
    _build(depth, tiles, None, stop, sched, True)
    return _build(depth, tiles, dump, stop, sched, False)


def _build(depth, tiles, dump, stop, sched, dry):
    nc = bass.Bass("TRN2", target_bir_lowering=False)
    P = Prog(nc)
    L_ = DEPTH

    def din(name, shape, dt=F32):
        return nc.dram_tensor(name, list(shape), dt, kind="ExternalInput").ap()

    def dout(name, shape):
        return nc.dram_tensor(name, list(shape), F32, kind="ExternalOutput").ap()

    xT_d = din("xT", [128, 8, NTOK])
    cst_d = din("cst", [128, NCST])
    cstb_d = din("cstb", [128, NCSTB], BF16)
    pv_d = din("pv", [L_, 128, NPV])
    pvf_d = din("pvf", [128, 8])
    stc_d = din("stc", [L_, 128, 4, 16, 30])
    sts_d = din("sts", [L_, 128, 27, 16])
    stw_d = din("stw", [L_, 128, 8, 16, 64])
    stf_d = din("stf", [L_, 128, 48, 16, 2])
    w_in_d = din("w_in_u", [L_, 51, 128, 1024])
    wco_d = din("w_co_u", [L_, 8, 128, 512])
    lw_d = din("lw", [L_, 128, 1024])
    g2_d = din("g2", [L_, 160, 1024])
    wro_d = din("w_ro_u", [L_, 8, 128, 1024])
    wmx_d = din("w_mx_u", [L_, 8, 128, 1024])
    wup_d = din("w_up_u", [L_, 48, 128, 1024])
    wdn_d = din("w_dn_u", [L_, 24, 128, 1024])

    o_y = dout("o_y", [128, 8, NTOK])
    o_pconv = dout("o_pconv", [L_, 128, 4, 30])
    o_sconv = dout("o_sconv", [L_, 128, 4, 16, 30])
    o_pshift = dout("o_pshift", [L_, 128, 27])
    o_sshift = dout("o_sshift", [L_, 128, 27, 16])
    o_pwkv = dout("o_pwkv", [L_, 128, 8, 64])
    o_swkv = dout("o_swkv", [L_, 128, 8, 16, 64])
    o_pffn = dout("o_pffn", [L_, 128, 48, 2])
    o_sffn = dout("o_sffn", [L_, 128, 48, 16, 2])

    if tiles is None:
        tiles = [Geo("p", 512 * i, i, first=(i == 0), last=(i == 3)) for i in range(4)] + [Geo("s", 2048, 4)]

    XT = P.sb("XT", [128, 8, 512], F32)
    H = P.sb("H", [128, 8, 512], BF16)
    NSLOT = 8
    WB = [P.sb("WB%d" % i, [128, 1024], BF16) for i in range(NSLOT)]
    UB = [P.sb("UB%d" % i, [128, 544], F32) for i in range(4)]
    CA = P.sb("CA", [128, 4, 512], BF16)
    BF = P.sb("BF", [128, 16, 512], BF16)
    LWT = P.sb("LWT", [128, 1024], BF16)
    G2A = P.sb("G2A", [128, 1024], BF16)
    G2B = P.sb("G2B", [128, 1024], BF16)
    TWL = P.sb("TWL", [128, 512], BF16)
    SGL = P.sb("SGL", [128, 512], BF16)
    SGL2 = P.sb("SGL2", [128, 512], BF16)
    T = [P.sb("T%d" % i, [128, 544], F32) for i in range(12)]
    TB = [P.sb("TBh%d" % i, [128, 512], BF16) for i in range(2)]

    class HPS:
        pass

    def mk_set(i, Tl, psb):
        S = HPS()
        S.T = Tl
        S.TB = [P.sb("hTB%d_%d" % (i, j), [128, 512], BF16) for j in range(2)] if i else TB
        S.AR = P.sb("AR%d" % i, [128, 8, 128], BF16)
        S.BK = P.sb("BK%d" % i, [128, 8, 128], BF16)
        S.TOK = P.sb("TOK%d" % i, [128, 8, 3, 64], BF16)
        S.AM = P.sb("AM%d" % i, [128, 8, 320], BF16)
        S.CH = [P.sb("CH%d_%d" % (i, j), [128, 8, 64], BF16) for j in range(4)]
        S.TTa = P.sb("TTa%d" % i, [128, 8, 64], BF16)
        S.TTb = P.sb("TTb%d" % i, [128, 8, 64], BF16)
        S.RHSb = P.sb("RHSb%d" % i, [128, 64], BF16)
        S.Ub = P.sb("Ub%d" % i, [128, 64], BF16)
        S.VBF = [P.sb("VBF%d_%d" % (i, j), [128, 512], BF16) for j in range(3)]
        S.PSB = psb
        return S

    PS = [P.ps("PS%d" % i, [128, 512], F32) for i in range(8)]
    T1 = {j: P.sb("hT1_%d" % j, [128, 544], F32) for j in (0, 2, 3, 4, 5, 6, 7, 8, 9, 10)}
    SET0 = mk_set(0, {j: T[j] for j in range(12)}, PS[0:4])
    SET1 = mk_set(1, T1, PS[4:8])
    SETS = mk_set
    STpb = P.sb("STpb", [128, 8, 64], BF16)
    S0 = P.sb("S0", [128, 16, 64], F32)
    S0b = P.sb("S0b", [128, 16, 64], BF16)
    STS = P.sb("STS", [128, 27, 16], F32)
    STF = P.sb("STF", [128, 48, 16, 2], F32)
    CHALO = P.sb("CHALO", [128, L_, 4, 30], F32)
    CSH = P.sb("CSH", [128, L_, 27, 1], F32)
    STp = P.sb("STp", [128, L_, 8, 64], F32)
    CFF = P.sb("CFF", [128, L_, 48, 2], F32)
    PVs = [P.sb("PV%d" % i, [128, NPV], F32) for i in range(1)]
    DV = P.sb("DV", [128, NDV], F32)
    PVF = P.sb("PVF", [128, 8], F32)
    CST = P.sb("CST", [128, NCST], F32)
    CSTB = P.sb("CSTB", [128, NCSTB], BF16)
    IDBb = P.sb("IDBb", [128, 64], BF16)
    IDENTB = P.sb("IDENTB", [128, 128], BF16)
    DIAG = [P.sb("DIAG%d" % i, [128, 128], BF16) for i in range(8)]
    ONES = P.sb("ONES", [128, 128], BF16)
    ONEB = P.sb("ONEB", [128, 128], BF16)
    ONEB64 = P.sb("ONEB64", [128, 128], BF16)

    mask_p = CSTB[:, C_MP:C_MP + 320]
    mask_s = CSTB[:, C_MS:C_MS + 320]
    rm_p = CSTB[:, C_RMP:C_RMP + 512]
    rm_s = CSTB[:, C_RMS:C_RMS + 64]
    seqm = CSTB[:, C_SEQM:C_SEQM + 1024].r("p (q t) -> p q t", q=16)
    seqmt = CSTB[:, C_SEQMT:C_SEQMT + 16]

    dumps = {}

    def dumpv(name, view, shape):
        if dump is None or name not in dump:
            return
        d = dout("dbg_" + name, shape)
        P.dma(d, view, eng="pool")
        dumps[name] = shape

    def wsrc(kind, l, a, b=None):
        if kind == "in":
            idx = a // 128 if a < 4352 else (34 if a == 4352 else 35 + (a - 4384) // 128)
            return w_in_d[l, idx].rearrange("p (k m) -> p k m", k=8), 8, 128
        if kind == "co":
            return wco_d[l, a // 128].rearrange("p (k m) -> p k m", k=4), 4, 128
        if kind == "ro":
            return wro_d[l, a // 128].rearrange("p (k m) -> p k m", k=8), 8, 128
        if kind == "mx":
            return wmx_d[l, a // 128].rearrange("p (k m) -> p k m", k=8), 8, 128
        if kind == "up":
            return wup_d[l, a // 128].rearrange("p (k m) -> p k m", k=8), 8, 128
        if kind == "dn":
            return wdn_d[l, a * 8 + b // 128].rearrange("p (k m) -> p k m", k=8), 8, 128
        raise ValueError(kind)

    class WS:
        issued = 0
        taken = 0

    def w_issue():
        i = WS.issued
        if i >= len(sched):
            return
        src, K, M = wsrc(*sched[i])
        dst = WB[i % NSLOT][:, 0:K * M].r("p (k m) -> p k m", k=K)
        P.dma(dst, src, eng="pool")
        WS.issued += 1

    def w_next(*unit):
        i = WS.taken
        if dry:
            sched.append(unit)
        else:
            assert sched[i] == unit, (i, sched[i], unit)
            while WS.issued < min(len(sched), i + NSLOT - 1):
                w_issue()
        _, K, M = wsrc(*unit)
        WS.taken += 1
        return WB[i % NSLOT][:, 0:K * M].r("p (k m) -> p k m", k=K)

    def v3(v, g):
        return v.r("p (s t) -> p s t", s=g.nseq)

    def hv(tile, M, g, h):
        return tile[:M, 0:g.nseq * (h + g.L)].r("p (s t) -> p s t", s=g.nseq)

    def proj(ps_view, w, M, N):
        for k in range(8):
            P.mm(ps_view, w[:, k, 0:M], H[:, k, :N], start=(k == 0), stop=(k == 7))

    def rmsnorm(xc, N):
        for c in range(8):
            sq = TB[c % 2][:, :N]
            P.act(sq, xc[c], AF.Square)
            P.mm(PS[2][:, :N], ONES[:], sq, start=(c == 0), stop=(c == 7))
        t = T[10][:, :N]
        P.act(t, PS[2][:, :N], AF.Ln, scale=1.0 / D, bias=RMS_EPS)
        P.act(t, t, AF.Exp, scale=-0.5)
        return t

    P.dma(CST[:], cst_d)
    P.dma(CSTB[:], cstb_d)
    P.dma(PVF[:], pvf_d)
    P.act(IDBb[:], CST[:, C_IDB:C_IDB + 64], AF.Identity)
    P.act(IDENTB[:], CST[:, C_ID:C_ID + 128], AF.Identity)
    P.memset(ONES[:], 1.0)
    P.memset(CSH[:], 0.0)
    P.memset(G2B[:], 0.0)
    P.memset(SGL2[:], 0.0)
    P.memset(ONEB[:], 0.0)
    P.memset(ONEB[0:64, 0:64], 1.0)
    P.memset(ONEB[64:128, 64:128], 1.0)
    P.act(ONEB64[:], ONEB[:], AF.Identity, scale=1.0 / 64)

    class Cur:
        PV = None
        npass = 0

    def layer_setup(l):
        PV = PVs[0]
        Cur.PV = PV
        Cur.npass += 1
        P.dma(PV[:], pv_d[l])
        P.dma(LWT[:], lw_d[l], eng="pool")
        P.dma(G2A[:], g2_d[l][0:128, :], eng="pool")
        P.dma(G2B[0:32, :], g2_d[l][128:160, :], eng="pool")
        P.ts(DV[:, CLGH:CLGH + 4], PV[:, CLG:CLG + 4], 0.5, ALU.mult)
        P.ts(DV[:, CLBH:CLBH + 4], PV[:, CLB:CLB + 4], 0.5, ALU.mult)
        P.ts(DV[:, W0H:W0H + 8], PV[:, W0:W0 + 8], 0.5, ALU.mult)
        P.ts(DV[:, A0H:A0H + 8], PV[:, A0:A0 + 8], 0.5, ALU.mult)
        P.ts(DV[:, KAH:KAH + 8], PV[:, KA:KA + 8], 0.5, ALU.mult)
        P.ts(DV[:, KAB:KAB + 8], PV[:, KA:KA + 8], -0.5, ALU.mult, 1.0, ALU.add)
        P.ts(DV[:, OMM:OMM + 27], PV[:, MU:MU + 27], -1.0, ALU.mult, 1.0, ALU.add)

    def shift_chunk(l, g, ps_view, q, M, out_xs, zs_tile, d_tile):
        N, L = g.N, g.L
        PV = Cur.PV
        z3 = hv(zs_tile, M, g, 1)
        if g.kind == "p":
            if g.first:
                P.memset(z3[:, :, 0:1], 0.0)
            else:
                P.copy(z3[:, :, 0:1], CSH[:M, l, q:q + 1, :], eng="act")
        else:
            P.copy(z3[:, :, 0:1], STS[:M, q, :].un(2), eng="act")
        P.act(z3[:, :, 1:1 + L], v3(ps_view, g), AF.Identity)
        d3 = v3(d_tile[:M, :N], g)
        P.act(d3, v3(ps_view, g), AF.Identity, scale=DV[:M, OMM + q:OMM + q + 1])
        P.stt(v3(out_xs, g), z3[:, :, 0:L], PV[:M, MU + q:MU + q + 1], d3, ALU.mult, ALU.add)
        if g.kind == "p":
            P.copy(CSH[:M, l, q:q + 1, :], z3[:, :, L:L + 1], eng="act")
        else:
            P.copy(STS[:M, q, :].un(2), z3[:, :, L:L + 1], eng="act")

    def wkv(l, g, hp, S, XV, BHF, KHF, WC):
        N, nblk, Q = g.N, g.nblk, g.nseq
        HS = [slice(0, 64), slice(64, 128)]
        prompt = g.kind == "p"
        nlev = 5 if prompt else 1
        MASK = mask_p if prompt else mask_s
        AR, BK, TOK, AM, CH, TTa, TTb, RHSb, Ub, PSB = S.AR, S.BK, S.TOK, S.AM, S.CH, S.TTa, S.TTb, S.RHSb, S.Ub, S.PSB
        Tt = S.T
        for b in range(nblk):
            cb = slice(b * 64, (b + 1) * 64)
            pa = PSB[b % 2]
            ptb = PSB[2 + (b % 2)][:, 0:96].bitcast(BF16)
            for qi, src in enumerate((XV, BHF, KHF)):
                for hs in HS:
                    P.tr(ptb[hs, qi * 64:(qi + 1) * 64], src[hs, cb], IDENTB[hs, hs])
            for hs in HS:
                P.mm(pa[hs, 0:128], BK[hs, b, 0:64], AR[hs, b, :])
            for hs in HS:
                P.mm(pa[hs, 128:256], BK[hs, b, 64:128], AR[hs, b, :])
            for hs in HS:
                P.mm(pa[hs, 256:320], AR[hs, b, 0:64], BK[hs, b, 0:64])
            P.copy(TOK[:, b, :, :], ptb.r("p (q i) -> p q i", q=3), eng="act")
            P.tt(AM[:, b, :], pa[:, 0:320], MASK, ALU.mult)
            yield
        yield
        if stop == "W1":
            return
        P.tt(TTa[:, 0:nblk, :], AM[:, 0:nblk, 0:64], IDBb[:].un(1).bc([128, nblk, 64]), ALU.add)
        Xp = AM[:, :, 0:64]
        Pp = AM[:, :, 256:320]
        TTp, TTn = TTa, TTb
        for k in range(1, nlev + 1):
            Pk = CH[(k % 2) * 2]
            Xk = CH[(k % 2) * 2 + 1]
            for b in range(nblk):
                cb = slice(b * 64, (b + 1) * 64)
                for hs in HS:
                    P.mm(PSB[0][hs, cb], Xp[hs, b, :], Pp[hs, b, :])
                if k < nlev:
                    for hs in HS:
                        P.mm(PSB[1][hs, cb], Pp[hs, b, :], Xp[hs, b, :])
            P.copy(Pk[:, 0:nblk, :], PSB[0][:, 0:nblk * 64].r("p (b i) -> p b i", b=nblk), eng="act")
            if k < nlev:
                P.copy(Xk[:, 0:nblk, :], PSB[1][:, 0:nblk * 64].r("p (b i) -> p b i", b=nblk))
            yield
            for b in range(nblk):
                cb = slice(b * 64, (b + 1) * 64)
                for hs in HS:
                    P.mm(PSB[2][hs, cb], Pk[hs, b, :], TTp[hs, b, :])
            P.tt(TTn[:, 0:nblk, :], TTp[:, 0:nblk, :],
                 PSB[2][:, 0:nblk * 64].r("p (b i) -> p b i", b=nblk), ALU.add)
            yield
            Xp, Pp = Xk, Pk
            TTp, TTn = TTn, TTp
        TTf = TTp
        if stop == "W2":
            return
        if not prompt:
            def bfv(t):
                return t[:, 0:512].bitcast(BF16).r("p (q i) -> p q i", q=16)
            AMSK, RMSK, BHM, KHM = bfv(Tt[0]), bfv(Tt[1]), bfv(Tt[2]), bfv(Tt[3])
            S0w = [Tt[11][:, 0:512].r("p (q i) -> p q i", q=8), Tt[5][:, 0:512].r("p (q i) -> p q i", q=8)]
            P.dma(S0[:], stw_d[l][:, hp])
            P.act(S0b[:], S0[:], AF.Identity)
            P.tt(AMSK, AR[:, 0, 0:64].un(1).bc([128, 16, 64]), seqm, ALU.mult)
            P.tt(RMSK, AR[:, 0, 64:128].un(1).bc([128, 16, 64]), seqm, ALU.mult)
            P.tt(BHM, TOK[:, 0, 1, :].un(1).bc([128, 16, 64]), seqmt.un(2).bc([128, 16, 64]), ALU.mult)
            P.tt(KHM, TOK[:, 0, 2, :].un(1).bc([128, 16, 64]), seqmt.un(2).bc([128, 16, 64]), ALU.mult)
            wc3 = WC.r("p (s t) -> p s t", s=16)
            for rnd in range(2):
                P.tt(S0w[rnd], S0[:, rnd * 8:rnd * 8 + 8, :], wc3[:, rnd * 8:rnd * 8 + 8, 3:4].bc([128, 8, 64]), ALU.mult)
        stv = STp.v((slice(None), l, hp, slice(None)), (l, hp))
        spb = STpb.v((slice(None), hp, slice(None)), hp)
        LB = PSB[0]
        for b in range(nblk):
            cb = slice(b * 64, (b + 1) * 64)

            def ops(h2):
                hs = HS[h2]
                if prompt:
                    return ([AR[hs, b, 0:64]], [AR[hs, b, 64:128]], [TOK[hs, b, 1, :]], [TOK[hs, b, 2, :]],
                            [STpb.v((hs, hp, slice(None)), hp)])
                return ([AMSK[hs, q, :] for q in range(Q)], [RMSK[hs, q, :] for q in range(Q)],
                        [BHM[hs, q, :] for q in range(Q)], [KHM[hs, q, :] for q in range(Q)],
                        [S0b[hs, q, :] for q in range(Q)])
            seqs = []
            for h2 in range(2):
                hs = HS[h2]
                a_, r_, bh_, kh_, s_ = ops(h2)
                sq = [(LB[hs, 0:64], a_[q], s_[q], q == 0, False) for q in range(Q)]
                sq.append((LB[hs, 0:64], AM[hs, b, 128:192], TOK[hs, b, 0, :], False, True))
                seqs.append(sq)
            for i in range(len(seqs[0])):
                for sq in seqs:
                    o_, l_, r2_, st_, sp_ = sq[i]
                    P.mm(o_, l_, r2_, start=st_, stop=sp_)
            P.copy(RHSb[:], LB[:, 0:64], eng="act")
            yield
            for hs in HS:
                P.mm(LB[hs, 64:128], TTf[hs, b, :], RHSb[hs, :])
            P.copy(Ub[:], LB[:, 64:128], eng="act")
            yield
            seqs = []
            for h2 in range(2):
                hs = HS[h2]
                a_, r_, bh_, kh_, s_ = ops(h2)
                YTb = PSB[3][hs, cb]
                Vt = TOK[hs, b, 0, :]
                sq = [(YTb, s_[q], r_[q], q == 0, False) for q in range(Q)]
                sq.append((YTb, Ub[hs, :], AM[hs, b, 64:128], False, False))
                sq.append((YTb, Vt, AM[hs, b, 192:256], False, True))
                seqs.append(sq)
            for i in range(len(seqs[0])):
                for sq in seqs:
                    o_, l_, r2_, st_, sp_ = sq[i]
                    P.mm(o_, l_, r2_, start=st_, stop=sp_)
            for rnd in range((Q + 7) // 8):
                nq = min(8, Q - rnd * 8)
                SNB = LB if prompt else PS[4]
                c0 = 128 if prompt else 0
                hops = [ops(h2) for h2 in range(2)]
                for qq in range(nq):
                    q = rnd * 8 + qq
                    for h2 in range(2):
                        hs = HS[h2]
                        P.mm(SNB[hs, c0 + qq * 64:c0 + (qq + 1) * 64], hops[h2][2][q], Ub[hs, :], start=True, stop=False)
                    for h2 in range(2):
                        hs = HS[h2]
                        P.mm(SNB[hs, c0 + qq * 64:c0 + (qq + 1) * 64], hops[h2][3][q], TOK[hs, b, 0, :], start=False, stop=True)
                if prompt:
                    if b < nblk - 1:
                        P.stt(spb, stv, WC[:, b * 64 + 63:b * 64 + 64], SNB[:, 128:192], ALU.mult, ALU.add)
                    P.stt(stv, stv, WC[:, b * 64 + 63:b * 64 + 64], SNB[:, 128:192], ALU.mult, ALU.add)
                else:
                    P.tt(S0[:, rnd * 8:rnd * 8 + nq, :], S0w[rnd],
                         SNB[:, 0:nq * 64].r("p (q i) -> p q i", q=nq), ALU.add)
            yield
        if not prompt:
            P.dma(o_swkv[l][:, hp], S0[:])

    def hp_gen(l, g, hp, S):
        N = g.N
        prompt = g.kind == "p"
        PV = Cur.PV
        Tt, TBs, AR, BK, VBF, PSB = S.T, S.TB, S.AR, S.BK, S.VBF, S.PSB
        hc = slice(hp * 128, (hp + 1) * 128)
        XR, XK, XV = Tt[2][:, :N], Tt[3][:, :N], Tt[4][:, :N]
        if prompt:
            stv_ = STp.v((slice(None), l, hp, slice(None)), (l, hp))
            spb_ = STpb.v((slice(None), hp, slice(None)), hp)
            if g.first:
                P.memset(stv_, 0.0)
                P.memset(spb_, 0.0)
            else:
                P.copy(spb_, stv_, eng="act")
        for i, (q, xs) in enumerate(((hp, XR), (8 + hp, XK), (16 + hp, XV))):
            w = w_next("in", l, 1024 + q * 128, 128)
            ps = PSB[i]
            proj(ps[:, :N], w, 128, N)
            shift_chunk(l, g, ps[:, :N], q, 128, xs, Tt[(0, 6, 8)[i]], Tt[(5, 7, 9)[i]])
            yield
        if stop == "Cb":
            return
        P.mm(PSB[0][:, :N], LWT[0:64, hc], TWL[0:64, :N])
        SG = Tt[5][:, :N]
        P.act(SG, PSB[0][:, :N], AF.Tanh, scale=0.5, bias=DV[:, W0H + hp:W0H + hp + 1])
        P.ts(SG, SG, 1.0, ALU.add)
        CS = Tt[7][:, :N]
        P.scan(CS, (rm_p if prompt else rm_s)[:, :N], SG, 0.0, ALU.mult, ALU.add)
        cs3 = CS.r("p (s t) -> p s t", s=g.nb)
        CSE = Tt[8][:, :N]
        P.tt(CSE.r("p (s t) -> p s t", s=g.nb), cs3[:, :, g.Lb - 1:g.Lb].bc([128, g.nb, g.Lb]), cs3, ALU.subtract)
        P.tt(SG, CS, SG, ALU.subtract)
        WI = Tt[9][:, :N]
        P.act(WI, CS, AF.Exp, scale=H0)
        P.act(CS, CS, AF.Exp, scale=-H0)
        P.act(SG, SG, AF.Exp, scale=-H0)
        P.act(CSE, CSE, AF.Exp, scale=-H0)
        WC, WM, WE = CS, SG, CSE
        yield
        P.mm(PSB[1][:, :N], LWT[64:128, hc], TWL[64:128, :N])
        THA = Tt[10][:, :N]
        P.act(THA, PSB[1][:, :N], AF.Tanh, scale=0.5, bias=DV[:, A0H + hp:A0H + hp + 1])
        ksq = TBs[0][:, :N]
        P.act(ksq, XK, AF.Square, scale=PV[:, KK + hp:KK + hp + 1])
        P.mm(PSB[1][:, :N], ONEB[:], ksq)
        NR = Tt[0][:, :N]
        P.act(NR, PSB[1][:, :N], AF.Ln, bias=1e-24)
        P.act(NR, NR, AF.Exp, scale=-0.5)
        KKN = Tt[6][:, :N]
        P.stt(KKN, XK, PV[:, KK + hp:KK + hp + 1], NR, ALU.mult, ALU.mult)
        yield
        nbk = g.nblk
        ar3a = AR[:, 0:nbk, 0:64]
        ar3r = AR[:, 0:nbk, 64:128]
        bk3b = BK[:, 0:nbk, 0:64]
        bk3k = BK[:, 0:nbk, 64:128]

        def b3(v):
            return v.r("p (b t) -> p b t", b=nbk)
        P.stt(ar3a, b3(KKN), -1.0, b3(WM), ALU.mult, ALU.mult)
        P.tt(ar3r, b3(XR), b3(WC), ALU.mult)
        B2 = Tt[5][:, :N]
        P.stt(B2, THA, 1.0, KKN, ALU.add, ALU.mult)
        P.stt(bk3b, b3(B2), 0.5, b3(WI), ALU.mult, ALU.mult)
        BHF = VBF[1][:, :N]
        P.stt(BHF, B2, 0.5, WE, ALU.mult, ALU.mult)
        yield
        KF = Tt[5][:, :N]
        P.act(KF, THA, AF.Identity, scale=DV[:, KAH + hp:KAH + hp + 1], bias=DV[:, KAB + hp:KAB + hp + 1])
        P.tt(KF, KF, XK, ALU.mult)
        P.tt(bk3k, b3(KF), b3(WI), ALU.mult)
        KHF = VBF[2][:, :N]
        P.tt(KHF, KF, WE, ALU.mult)
        VB = VBF[0][:, :N]
        P.act(VB, XV, AF.Identity)
        rkb = TBs[1][:, :N]
        P.stt(rkb, XR, PV[:, RK + hp:RK + hp + 1], KF, ALU.mult, ALU.mult)
        P.mm(PSB[0][:, :N], ONEB[:], rkb)
        BON = Tt[9][:, :N]
        P.tt(BON, PSB[0][:, :N], XV, ALU.mult)
        P.mm(PSB[1][:, :N], G2A[:, hc], SGL[:, :N], start=True, stop=False)
        P.mm(PSB[1][:, :N], G2B[:, hc], SGL2[:, :N], start=False, stop=True)
        GP = Tt[10][:, :N]
        P.act(GP, PSB[1][:, :N], AF.Identity)
        yield
        for _ in wkv(l, g, hp, S, VB, BHF, KHF, WC):
            yield
        if stop in ("W1", "W2"):
            return
        YS = Tt[5][:, :N]
        P.act(YS, PSB[3][:, :N], AF.Identity)
        if stop == "C1" and hp == 0:
            dumpv("YS", YS, [128, N])
            return
        yb_, y2_ = TBs[0][:, :N], TBs[1][:, :N]
        P.act(yb_, YS, AF.Identity)
        P.act(y2_, YS, AF.Square)
        P.mm(PSB[0][:, :N], ONEB64[:], yb_)
        P.mm(PSB[1][:, :N], ONEB64[:], y2_)
        yield
        VR = Tt[6][:, :N]
        MS = Tt[7][:, :N]
        P.act(MS, PSB[0][:, :N], AF.Identity)
        P.tt(VR, MS, MS, ALU.mult)
        P.tt(VR, PSB[1][:, :N], VR, ALU.subtract)
        P.act(VR, VR, AF.Ln, bias=GN_EPS)
        P.act(VR, VR, AF.Exp, scale=-0.5)
        P.tt(YS, YS, MS, ALU.subtract)
        P.tt(YS, YS, VR, ALU.mult)
        P.act(YS, YS, AF.Identity, scale=PV[:, LNG + hp:LNG + hp + 1], bias=PV[:, LNB + hp:LNB + hp + 1])
        P.tt(YS, YS, BON, ALU.add)
        P.stt(BF.v((slice(None), 8 + hp, slice(0, N)), 8 + hp), YS, 0.5, GP, ALU.mult, ALU.mult)
        yield

    def run_skewed(fns, sets, lag):
        pending = list(fns)
        free = list(sets)
        active = []
        while pending or active:
            if pending and free and (not active or min(a[2] for a in active) >= lag):
                S = free.pop(0)
                active.append([pending.pop(0)(S), S, 0])
            for a in list(active):
                try:
                    next(a[0])
                    a[2] += 1
                except StopIteration:
                    active.remove(a)
                    free.append(a[1])

    def run_gens(gens):
        active = list(gens)
        while active:
            for gen in list(active):
                try:
                    next(gen)
                except StopIteration:
                    active.remove(gen)

    def run_pass(l, g):
        N, L, ti = g.N, g.L, g.ti
        prompt = g.kind == "p"
        PV = Cur.PV
        xc = [XT.v((slice(None), c, slice(0, N)), c) for c in range(8)]
        if not prompt:
            P.dma(STS[:], sts_d[l])
            P.dma(STF[:], stf_d[l])
        rstd = rmsnorm(xc, N)
        for c in range(8):
            P.stt(H[:, c, :N], xc[c], PV[:, NMG + c:NMG + c + 1], rstd, ALU.mult, ALU.mult)
        if stop == "A":
            return
        def conv_proj(cc):
            pa_, pb_ = PS[2 * (cc % 2)], PS[2 * (cc % 2) + 1]
            wa = w_next("in", l, cc * 128, 128)
            proj(pa_[:, :N], wa, 128, N)
            wb = w_next("in", l, 512 + cc * 128, 128)
            proj(pb_[:, :N], wb, 128, N)

        conv_proj(0)
        for cc in range(4):
            pa_, pb_ = PS[2 * (cc % 2)], PS[2 * (cc % 2) + 1]
            th, zah = T[4 + (cc % 2)][:, :N], T[6 + (cc % 2)][:, :N]
            P.act(th, pb_[:, :N], AF.Tanh, scale=0.5)
            P.act(zah, pa_[:, :N], AF.Identity, scale=0.5)
            if cc < 3:
                conv_proj(cc + 1)
            u3 = hv(UB[cc], 128, g, 30)
            if prompt:
                if g.first:
                    P.memset(u3[:, :, 0:30], 0.0)
                else:
                    P.copy(u3[:, :, 0:30], CHALO[:, l, cc, :].un(1), eng="act")
            else:
                P.dma(u3[:, :, 0:30], stc_d[l][:, cc])
            P.stt(u3[:, :, 30:30 + L], v3(th, g), 1.0, v3(zah, g), ALU.add, ALU.mult)
            a3 = v3(T[cc][:, :N], g)
            if True:
                ubf = T[8 + (cc % 2)][:, 0:272].bitcast(BF16)
                nfl = g.nseq * (30 + L)
                P.act(ubf[:, 0:nfl], UB[cc][:, 0:nfl], AF.Identity)
                ubf3 = ubf[:, 0:nfl].r("p (s t) -> p s t", s=g.nseq)
                for j in range(31):
                    dg = DIAG[j % 8]
                    P.ts(dg[:], IDENTB[:], PV[:, CDW + cc * 31 + j:CDW + cc * 31 + j + 1], ALU.mult)
                    rhs_ = ubf[:, j:j + L] if prompt else ubf3[:, :, j:j + L]
                    out_ = PS[4 + (cc % 2)][:, :N] if prompt else v3(PS[4 + (cc % 2)][:, :N], g)
                    P.mm(out_, dg[:], rhs_, start=(j == 0), stop=(j == 30))
                P.act(T[cc][:, :N], PS[4 + (cc % 2)][:, :N], AF.Identity, bias=PV[:, CDB + cc:CDB + cc + 1])
            else:
                P.ts(a3, u3[:, :, 0:L], PV[:, CDW + cc * 31:CDW + cc * 31 + 1], ALU.mult,
                     PV[:, CDB + cc:CDB + cc + 1], ALU.add)
                for j in range(1, 31):
                    P.stt(a3, u3[:, :, j:j + L], PV[:, CDW + cc * 31 + j:CDW + cc * 31 + j + 1], a3, ALU.mult, ALU.add)
            cb_, c2_ = TB[0][:, :N], TB[1][:, :N]
            P.act(cb_, T[cc][:, :N], AF.Identity)
            P.act(c2_, T[cc][:, :N], AF.Square)
            P.mm(PS[6][:, :N], ONES[:], cb_, start=(cc == 0), stop=(cc == 3))
            P.mm(PS[7][:, :N], ONES[:], c2_, start=(cc == 0), stop=(cc == 3))
            if prompt:
                P.copy(CHALO[:, l, cc, :].un(1), u3[:, :, L:L + 30], eng="act")
            else:
                P.dma(o_sconv[l][:, cc], u3[:, :, 4:34])
        if prompt and g.last:
            P.dma(o_pconv[l], CHALO[:, l])
        mean, var = T[6][:, :N], T[7][:, :N]
        P.ts(mean, PS[6][:, :N], 1.0 / 512, ALU.mult)
        P.tt(var, mean, mean, ALU.mult)
        P.stt(var, PS[7][:, :N], 1.0 / 512, var, ALU.mult, ALU.subtract)
        P.act(var, var, AF.Ln, bias=LN_EPS)
        P.act(var, var, AF.Exp, scale=-0.5)
        for cc in range(4):
            a = T[cc][:, :N]
            P.tt(a, a, mean, ALU.subtract)
            P.tt(a, a, var, ALU.mult)
            P.act(a, a, AF.Identity, scale=DV[:, CLGH + cc:CLGH + cc + 1], bias=DV[:, CLBH + cc:CLBH + cc + 1])
            th = T[4 + (cc % 2)][:, :N]
            P.act(th, a, AF.Tanh)
            P.stt(CA[:, cc, :N], th, 1.0, a, ALU.add, ALU.mult)
        if stop == "B":
            dumpv("CA", CA[:, :, :N], [128, 4, N])
            return
        for q, M in ((24, 128), (25, 128), (26, 32)):
            w = w_next("in", l, 1024 + q * 128, 128)
            ps = PS[q % 2]
            proj(ps[:M, :N], w, M, N)
            xs = T[2][:M, :N]
            shift_chunk(l, g, ps[:M, :N], q, M, xs, T[0], T[1])
            if q == 24:
                P.act(TWL[0:64, :N], T[2][0:64, :N], AF.Tanh)
                P.act(TWL[64:128, :N], T[2][64:128, :N], AF.Identity)
            elif q == 25:
                P.act(T[3][:, :N], xs, AF.Tanh, scale=0.5)
                P.ts(SGL[:, :N], T[3][:, :N], 1.0, ALU.add)
            else:
                P.act(T[3][0:32, :N], xs, AF.Tanh, scale=0.5)
                P.ts(SGL2[0:32, :N], T[3][0:32, :N], 1.0, ALU.add)
        if stop == "Ca":
            return
        if prompt:
            run_skewed([(lambda S, hp=hp: hp_gen(l, g, hp, S)) for hp in range(8)], [SET0, SET1], HP_LAG)
        else:
            SS = HPS()
            SS.__dict__.update(SET0.__dict__)
            SS.PSB = PS[0:4]
            for hp in range(8):
                run_gens([hp_gen(l, g, hp, SS)])
        if prompt and g.last:
            P.dma(o_pshift[l], CSH[:, l, :, 0])
            P.dma(o_pwkv[l], STp[:, l])
        if not prompt:
            P.dma(o_sshift[l], STS[:])
        if stop in ("C", "C1", "W1", "W2", "Cb"):
            dumpv("YF", BF[:, 8:16, :N], [128, 8, N])
            dumpv("STp", STp[:, l], [128, 8, 64])
            return
        for m in range(8):
            o = 4 * (m % 2)
            wco = w_next("co", l, m * 128)
            for k in range(4):
                P.mm(PS[o][:, :N], wco[:, k, :], CA[:, k, :N], start=(k == 0), stop=(k == 3))
            wro = w_next("ro", l, m * 128)
            for k in range(8):
                P.mm(PS[o + 1][:, :N], wro[:, k, :], BF.v((slice(None), 8 + k, slice(0, N)), 8 + k), start=(k == 0), stop=(k == 7))
            wg1 = w_next("in", l, 4384 + m * 128, 128)
            proj(PS[o + 2][:, :N], wg1, 128, N)
            wg2 = w_next("in", l, 5408 + m * 128, 128)
            proj(PS[o + 3][:, :N], wg2, 128, N)
            t1, t2 = T[(m % 2) * 2][:, :N], T[(m % 2) * 2 + 1][:, :N]
            P.act(t1, PS[o + 2][:, :N], AF.Tanh, scale=0.5)
            P.act(t2, PS[o + 3][:, :N], AF.Tanh, scale=0.5)
            P.stt(t1, t1, 1.0, PS[o][:, :N], ALU.add, ALU.mult)
            P.stt(t2, t2, 1.0, PS[o + 1][:, :N], ALU.add, ALU.mult)
            P.tt(BF.v((slice(None), m, slice(0, N)), m), t1, t2, ALU.add)
        for mo in range(8):
            w = w_next("mx", l, mo * 128)
            ps = PS[mo % 2]
            for k in range(8):
                P.mm(ps[:, :N], w[:, k, :], BF.v((slice(None), k, slice(0, N)), k), start=(k == 0), stop=(k == 7))
            P.stt(xc[mo], ps[:, :N], 0.5, xc[mo], ALU.mult, ALU.add)
        if stop == "M":
            return
        rstd = rmsnorm(xc, N)
        for c in range(8):
            P.stt(H[:, c, :N], xc[c], PV[:, NFG + c:NFG + c + 1], rstd, ALU.mult, ALU.mult)
        def bfrow(r):
            return BF.v((slice(None), r, slice(0, N)), r)

        def up_gen(part, r0):
            for jj in range(8):
                j = part * 8 + jj
                cus = []
                for half, q in ((0, j), (1, 24 + j)):
                    w = w_next("up", l, q * 128)
                    ps = PS[half + 2 * (jj % 2)]
                    proj(ps[:, :N], w, 128, N)
                    upb = T[half + 2 * (jj % 2)]
                    u3 = hv(upb, 128, g, 2)
                    if prompt:
                        if g.first:
                            P.memset(u3[:, :, 0:2], 0.0)
                        else:
                            P.copy(u3[:, :, 0:2], CFF[:, l, q, :].un(1), eng="act")
                    else:
                        P.copy(u3[:, :, 0:2], STF[:, q, :, :], eng="act")
                    P.act(u3[:, :, 2:2 + L], v3(ps[:, :N], g), AF.Identity)
                    cu = T[4 + half + 2 * (jj % 2)][:, :N]
                    c3 = v3(cu, g)
                    fw0 = FDW + q * 3
                    P.act(c3, v3(ps[:, :N], g), AF.Identity, scale=PV[:, fw0 + 2:fw0 + 3], bias=PV[:, FDB + q:FDB + q + 1])
                    P.stt(c3, u3[:, :, 1:1 + L], PV[:, fw0 + 1:fw0 + 2], c3, ALU.mult, ALU.add)
                    P.stt(c3, u3[:, :, 0:L], PV[:, fw0:fw0 + 1], c3, ALU.mult, ALU.add)
                    if prompt:
                        P.copy(CFF[:, l, q, :].un(1), u3[:, :, L:L + 2], eng="act")
                    else:
                        P.copy(STF[:, q, :, :], u3[:, :, L:L + 2], eng="act")
                    cus.append(cu)
                ga = T[8 + (jj % 2)][:, :N]
                P.act(ga, cus[0], AF.Gelu_apprx_tanh)
                P.tt(bfrow(r0 + jj), ga, cus[1], ALU.mult)
                yield

        def dn_gen(part, r0):
            for mo in range(8):
                w = w_next("dn", l, part, mo * 128)
                ps = PS[4 + (mo % 2)]
                for k in range(8):
                    P.mm(ps[:, :N], w[:, k, :], bfrow(r0 + k), start=(k == 0), stop=(k == 7))
                P.tt(xc[mo], xc[mo], ps[:, :N], ALU.add)
                yield

        run_gens([up_gen(0, 0)])
        for part in range(3):
            gens = [dn_gen(part, (part % 2) * 8)]
            if part < 2:
                gens.append(up_gen(part + 1, ((part + 1) % 2) * 8))
            run_gens(gens)
        if prompt and g.last:
            P.dma(o_pffn[l], CFF[:, l])
        if not prompt:
            P.dma(o_sffn[l], STF[:])

    dbg_x = dout("dbg_xT", [128, 8, NTOK]) if (dump is not None and stop is not None) else None
    if dbg_x is not None:
        dumps["xT"] = [128, 8, NTOK]
    for g in tiles:
        N = g.N
        P.dma(XT[:, :, :N], xT_d[:, :, g.t0:g.t0 + N])
        for l in range(depth):
            layer_setup(l)
            run_pass(l, g)
        xc = [XT.v((slice(None), c, slice(0, N)), c) for c in range(8)]
        if stop is None:
            rstd = rmsnorm(xc, N)
            for c in range(8):
                yo = T[c % 4][:, :N]
                P.stt(yo, xc[c], PVF[:, c:c + 1], rstd, ALU.mult, ALU.mult)
                P.dma(o_y[:, c, g.t0:g.t0 + N], yo)
        elif dbg_x is not None:
            P.dma(dbg_x[:, :, g.t0:g.t0 + N], XT[:, :, :N])
    P.finish()
    if not dry:
        P.emit()
    return nc, P, dumps


def _pm(v):
    return np.ascontiguousarray(v.reshape(-1, 128).T)


def pack_params(inp):
    pv = np.zeros((DEPTH, 128, NPV), np.float32)
    for l in range(DEPTH):
        pv[l, :, NMG:NMG + 8] = _pm(inp["norm_mix_g"][l])
        pv[l, :, NFG:NFG + 8] = _pm(inp["norm_ffn_g"][l])
        cw = inp["conv_dw_w"][l]
        pv[l, :, CDW:CDW + 124] = cw.reshape(31, 4, 128).transpose(2, 1, 0).reshape(128, 124)
        pv[l, :, CDB:CDB + 4] = _pm(inp["conv_dw_b"][l])
        pv[l, :, CLG:CLG + 4] = _pm(inp["conv_ln_g"][l])
        pv[l, :, CLB:CLB + 4] = _pm(inp["conv_ln_b"][l])
        mu = np.zeros(27 * 128, np.float32)
        mu[:3360] = inp["rw_mu"][l]
        pv[l, :, MU:MU + 27] = _pm(mu)
        pv[l, :, W0:W0 + 8] = _pm(inp["rw_w0"][l])
        pv[l, :, A0:A0 + 8] = _pm(inp["rw_a0"][l])
        pv[l, :, KK:KK + 8] = _pm(inp["rw_k_k"][l])
        pv[l, :, KA:KA + 8] = _pm(inp["rw_k_a"][l])
        pv[l, :, RK:RK + 8] = _pm(inp["rw_r_k"][l].reshape(-1))
        pv[l, :, LNG:LNG + 8] = _pm(inp["rw_ln_g"][l])
        pv[l, :, LNB:LNB + 8] = _pm(inp["rw_ln_b"][l])
        fw_ = inp["ffn_dw_w"][l]
        pv[l, :, FDW:FDW + 144] = fw_.reshape(3, 48, 128).transpose(2, 1, 0).reshape(128, 144)
        pv[l, :, FDB:FDB + 48] = _pm(inp["ffn_dw_b"][l])
    return pv


def make_in_maps(inp, cores=None):
    cst, cstb = make_consts()
    pv = pack_params(inp)
    pvf = _pm(inp["norm_final_g"])
    lw = np.ascontiguousarray(np.concatenate([inp["rw_w2"], inp["rw_a2"]], axis=1))
    L_ = DEPTH
    wi = inp["w_in"]
    w_in_u = np.zeros((L_, 51, 128, 8, 128), np.float32)
    w_in_u[:, 0:34] = wi[:, :, 0:4352].reshape(L_, 8, 128, 34, 128).transpose(0, 3, 2, 1, 4)
    w_in_u[:, 34, :, :, 0:32] = wi[:, :, 4352:4384].reshape(L_, 8, 128, 32).transpose(0, 2, 1, 3)
    w_in_u[:, 35:51] = wi[:, :, 4384:6432].reshape(L_, 8, 128, 16, 128).transpose(0, 3, 2, 1, 4)

    def units(w, K):
        n = w.shape[2] // 128
        return np.ascontiguousarray(w.reshape(L_, K, 128, n, 128).transpose(0, 3, 2, 1, 4)).reshape(L_, n, 128, K * 128)
    w_dn_u = np.ascontiguousarray(inp["w_down"].reshape(L_, 3, 8, 128, 8, 128).transpose(0, 1, 4, 3, 2, 5)).reshape(L_, 24, 128, 1024)
    shared = dict(cst=cst, cstb=cstb, pv=pv, pvf=pvf, lw=lw, g2=inp["rw_g2"],
                  w_in_u=w_in_u.reshape(L_, 51, 128, 1024),
                  w_co_u=units(inp["w_conv_out"], 4), w_ro_u=units(inp["w_rw_out"], 8),
                  w_mx_u=units(inp["w_mix_out"], 8), w_up_u=units(inp["w_up"], 8), w_dn_u=w_dn_u)
    maps = []
    for c in (range(NCORE) if cores is None else cores):
        sb = slice(16 * c, 16 * c + 16)
        xs = np.concatenate([inp["x_prompt"][c], inp["x_sample"][sb].reshape(64, D)], axis=0)
        xT = np.ascontiguousarray(xs.T.reshape(8, 128, NTOK).transpose(1, 0, 2))
        stc = np.ascontiguousarray(inp["state_conv"][:, sb].reshape(DEPTH, 16, 30, 4, 128).transpose(0, 4, 3, 1, 2))
        ss = np.zeros((DEPTH, 16, 27 * 128), np.float32)
        ss[:, :, :3360] = inp["state_shift"][:, sb]
        sts = np.ascontiguousarray(ss.reshape(DEPTH, 16, 27, 128).transpose(0, 3, 2, 1))
        sw = inp["state_wkv"][:, sb].reshape(DEPTH, 16, 8, 2, 64, 64)
        stw = np.ascontiguousarray(sw.transpose(0, 3, 5, 2, 1, 4).reshape(DEPTH, 128, 8, 16, 64))
        stf = np.ascontiguousarray(inp["state_ffn"][:, sb].reshape(DEPTH, 16, 2, 48, 128).transpose(0, 4, 3, 1, 2))
        m = dict(shared)
        m.update(xT=xT, stc=stc, sts=sts, stw=stw, stf=stf)
        maps.append(m)
    return maps


def assemble(results):
    L_ = DEPTH
    y_prompt = np.zeros((8, SEQ, D), np.float32)
    y_sample = np.zeros((128, 4, D), np.float32)
    p_conv = np.zeros((L_, 8, 30, 512), np.float32)
    p_shift = np.zeros((L_, 8, 3360), np.float32)
    p_wkv = np.zeros((L_, 8, 16, 64, 64), np.float32)
    p_ffn = np.zeros((L_, 8, 2, 6144), np.float32)
    s_conv = np.zeros((L_, 128, 30, 512), np.float32)
    s_shift = np.zeros((L_, 128, 3360), np.float32)
    s_wkv = np.zeros((L_, 128, 16, 64, 64), np.float32)
    s_ffn = np.zeros((L_, 128, 2, 6144), np.float32)
    for c, r in enumerate(results):
        sb = slice(16 * c, 16 * c + 16)
        yT = r["o_y"].transpose(1, 0, 2).reshape(D, NTOK)
        y_prompt[c] = yT[:, :SEQ].T
        y_sample[sb] = yT[:, SEQ:].T.reshape(16, 4, D)
        p_conv[:, c] = r["o_pconv"].transpose(0, 3, 2, 1).reshape(L_, 30, 512)
        s_conv[:, sb] = r["o_sconv"].transpose(0, 3, 4, 2, 1).reshape(L_, 16, 30, 512)
        p_shift[:, c] = r["o_pshift"].transpose(0, 2, 1).reshape(L_, 27 * 128)[:, :3360]
        s_shift[:, sb] = r["o_sshift"].transpose(0, 3, 2, 1).reshape(L_, 16, 27 * 128)[:, :, :3360]
        pw = r["o_pwkv"].reshape(L_, 2, 64, 8, 64)
        p_wkv[:, c] = pw.transpose(0, 3, 1, 4, 2).reshape(L_, 16, 64, 64)
        sw = r["o_swkv"].reshape(L_, 2, 64, 8, 16, 64)
        s_wkv[:, sb] = sw.transpose(0, 4, 3, 1, 5, 2).reshape(L_, 16, 16, 64, 64)
        p_ffn[:, c] = r["o_pffn"].transpose(0, 3, 2, 1).reshape(L_, 2, 6144)
        s_ffn[:, sb] = r["o_sffn"].transpose(0, 3, 4, 2, 1).reshape(L_, 16, 2, 6144)
    return (y_prompt, y_sample, p_conv, p_shift, p_wkv, p_ffn, s_conv, s_shift, s_wkv, s_ffn)


def kernel(**inputs):
    inp = {k: np.asarray(v) for k, v in inputs.items()}
    nc, P, _ = build()
    maps = make_in_maps(inp)
    res = run_bass_kernel_spmd(nc, maps, core_ids=list(range(NCORE)))
    return assemble(res.results)
```
